# Optimizing a Trainium2 kernel written in Bass

```python
import math
import jax, jax.numpy as jnp
from jax import lax
import numpy as np

D_MODEL = 1024
BATCH = 8
SEQ = 4096
DEPTH = 1

CHUNK = 64
MEM_TOKENS = 256

SSM_WIDTH = D_MODEL // 2
SSM_GROUP_CH = 16
SSM_GROUPS = SSM_WIDTH // SSM_GROUP_CH
SSM_STATE = 64
FOX_WIDTH = D_MODEL - SSM_WIDTH
FOX_HEAD_DIM = 64
FOX_HEADS = FOX_WIDTH // FOX_HEAD_DIM
Q_BLOCK = 128
IN_PROJ_COLS = SSM_WIDTH + 3 * FOX_WIDTH + FOX_HEADS

XA_HEADS = 4
XA_HEAD_DIM = D_MODEL // XA_HEADS

FFN_HIDDEN = int(math.ceil(8 * D_MODEL / 3 / 256) * 256)

RMS_EPS = 1e-6
NEG_INF = -1e30
DT_MIN = 1e-3
DT_MAX = 1e-1

kernel_name = "hybrid_s5_fox_memxattn_block"


def rms_norm(x, g):
    xf = x.astype(jnp.float32)
    y = xf * lax.rsqrt(jnp.mean(xf * xf, axis=-1, keepdims=True) + RMS_EPS)
    return (y * g.astype(jnp.float32)).astype(x.dtype)


def s5_mixer(u, a_re, a_im, log_dt, b_re, b_im, c_re, c_im, d_skip, glu_w, glu_b):
    bsz, seq, _ = u.shape
    uf = u.astype(jnp.float32).reshape(bsz, seq, SSM_GROUPS, SSM_GROUP_CH)
    a = lax.complex(a_re.astype(jnp.float32), a_im.astype(jnp.float32))
    dt = jnp.exp(log_dt.astype(jnp.float32))[:, None]
    a_bar = jnp.exp(a * dt)
    b = lax.complex(b_re.astype(jnp.float32), b_im.astype(jnp.float32))
    b_bar = ((a_bar - 1.0) / a)[..., None] * b
    bu = jnp.einsum('bsgi,gni->bsgn', uf.astype(jnp.complex64), b_bar)
    a_seq = jnp.broadcast_to(a_bar, bu.shape)

    def combine(left, right):
        a_l, x_l = left
        a_r, x_r = right
        return a_l * a_r, a_r * x_l + x_r

    _, states = lax.associative_scan(combine, (a_seq, bu), axis=1)
    c = lax.complex(c_re.astype(jnp.float32), c_im.astype(jnp.float32))
    y = jnp.real(jnp.einsum('bsgn,gin->bsgi', states, c))
    y = y + d_skip.astype(jnp.float32).reshape(SSM_GROUPS, SSM_GROUP_CH) * uf
    y = y.reshape(bsz, seq, SSM_WIDTH)
    g = jax.nn.gelu(y)
    out = g * jax.nn.sigmoid(g @ glu_w.astype(jnp.float32) + glu_b.astype(jnp.float32))
    return out.astype(u.dtype)


def forgetting_attention(q, k, v, f_logit):
    bsz, seq, heads, hd = q.shape
    scale = 1.0 / math.sqrt(hd)
    cum_log_f = jnp.cumsum(jax.nn.log_sigmoid(f_logit.astype(jnp.float32)), axis=1)
    cum_log_f = jnp.transpose(cum_log_f, (0, 2, 1))
    outs = []
    for blk in range(seq // Q_BLOCK):
        q0 = blk * Q_BLOCK
        k_end = q0 + Q_BLOCK
        qb = q[:, q0:k_end]
        kb = k[:, :k_end]
        vb = v[:, :k_end]
        s = jnp.einsum('bqhd,bkhd->bhqk', qb, kb).astype(jnp.float32) * scale
        bias = cum_log_f[:, :, q0:k_end, None] - cum_log_f[:, :, None, :k_end]
        q_pos = q0 + jnp.arange(Q_BLOCK)
        k_pos = jnp.arange(k_end)
        mask = q_pos[:, None] >= k_pos[None, :]
        s = jnp.where(mask, s + bias, NEG_INF)
        p = jax.nn.softmax(s, axis=-1)
        outs.append(jnp.einsum('bhqk,bkhd->bqhd', p.astype(v.dtype), vb))
    return jnp.concatenate(outs, axis=1)


def memory_cross_attention(h, mem_n, wq, wkv, wo):
    bsz, seq, _ = h.shape
    m = mem_n.shape[1]
    q = (h @ wq).reshape(bsz, seq, XA_HEADS, XA_HEAD_DIM)
    kv = mem_n @ wkv
    k = kv[..., :D_MODEL].reshape(bsz, m, XA_HEADS, XA_HEAD_DIM)
    v = kv[..., D_MODEL:].reshape(bsz, m, XA_HEADS, XA_HEAD_DIM)
    s = jnp.einsum('bqhd,bmhd->bhqm', q, k).astype(jnp.float32) / math.sqrt(XA_HEAD_DIM)
    p = jax.nn.softmax(s, axis=-1)
    o = jnp.einsum('bhqm,bmhd->bqhd', p.astype(v.dtype), v).reshape(bsz, seq, D_MODEL)
    return o @ wo


def swiglu(h, w_gate, w_up, w_down):
    return (jax.nn.silu(h @ w_gate) * (h @ w_up)) @ w_down


def setup_inputs(seed: int = 0) -> dict:
    key = jax.random.key(seed)
    ks = jax.random.split(key, 40)
    f32 = jnp.float32

    def dense(k, fan_in, shape):
        return jax.random.normal(k, shape, f32) * fan_in ** -0.5

    def gain(k, n):
        return jnp.ones((n,), f32) + 0.02 * jax.random.normal(k, (n,), f32)

    n_idx = jnp.arange(SSM_STATE, dtype=f32)
    a_re = -0.5 + 0.01 * jax.random.normal(ks[4], (SSM_GROUPS, SSM_STATE), f32)
    a_im = math.pi * n_idx[None, :] + 0.01 * jax.random.normal(ks[5], (SSM_GROUPS, SSM_STATE), f32)
    log_dt = jax.random.uniform(ks[6], (SSM_GROUPS,), f32, math.log(DT_MIN), math.log(DT_MAX))
    b_scale = (2.0 * SSM_GROUP_CH) ** -0.5
    c_scale = (2.0 * SSM_STATE) ** -0.5
    return {
        "x": jax.random.normal(ks[0], (BATCH, SEQ, D_MODEL), f32),
        "mem": jax.random.normal(ks[1], (BATCH, MEM_TOKENS, D_MODEL), f32),
        "mix_pre_g": gain(ks[2], D_MODEL),
        "w_in": dense(ks[3], D_MODEL, (D_MODEL, IN_PROJ_COLS)),
        "ssm_a_re": a_re,
        "ssm_a_im": a_im,
        "ssm_log_dt": log_dt,
        "ssm_b_re": jax.random.normal(ks[7], (SSM_GROUPS, SSM_STATE, SSM_GROUP_CH), f32) * b_scale,
        "ssm_b_im": jax.random.normal(ks[8], (SSM_GROUPS, SSM_STATE, SSM_GROUP_CH), f32) * b_scale,
        "ssm_c_re": jax.random.normal(ks[9], (SSM_GROUPS, SSM_GROUP_CH, SSM_STATE), f32) * c_scale,
        "ssm_c_im": jax.random.normal(ks[10], (SSM_GROUPS, SSM_GROUP_CH, SSM_STATE), f32) * c_scale,
        "ssm_d": jax.random.normal(ks[11], (SSM_WIDTH,), f32),
        "ssm_glu_w": dense(ks[12], SSM_WIDTH, (SSM_WIDTH, SSM_WIDTH)),
        "ssm_glu_b": 0.01 * jax.random.normal(ks[13], (SSM_WIDTH,), f32),
        "fox_f_bias": jax.random.uniform(ks[14], (FOX_HEADS,), f32, 1.0, 4.0),
        "ssm_out_g": gain(ks[15], SSM_WIDTH),
        "fox_out_g": gain(ks[16], FOX_WIDTH),
        "w_out": dense(ks[17], D_MODEL, (D_MODEL, D_MODEL)),
        "mix_post_g": gain(ks[18], D_MODEL),
        "xa_pre_g": gain(ks[19], D_MODEL),
        "mem_g": gain(ks[20], D_MODEL),
        "xa_wq": dense(ks[21], D_MODEL, (D_MODEL, D_MODEL)),
        "xa_wkv": dense(ks[22], D_MODEL, (D_MODEL, 2 * D_MODEL)),
        "xa_wo": dense(ks[23], D_MODEL, (D_MODEL, D_MODEL)),
        "xa_post_g": gain(ks[24], D_MODEL),
        "ffn_pre_g": gain(ks[25], D_MODEL),
        "w_gate": dense(ks[26], D_MODEL, (D_MODEL, FFN_HIDDEN)),
        "w_up": dense(ks[27], D_MODEL, (D_MODEL, FFN_HIDDEN)),
        "w_down": dense(ks[28], FFN_HIDDEN, (FFN_HIDDEN, D_MODEL)),
        "ffn_post_g": gain(ks[29], D_MODEL),
    }


def reference(x, mem, mix_pre_g, w_in, ssm_a_re, ssm_a_im, ssm_log_dt, ssm_b_re, ssm_b_im,
              ssm_c_re, ssm_c_im, ssm_d, ssm_glu_w, ssm_glu_b, fox_f_bias, ssm_out_g, fox_out_g,
              w_out, mix_post_g, xa_pre_g, mem_g, xa_wq, xa_wkv, xa_wo, xa_post_g,
              ffn_pre_g, w_gate, w_up, w_down, ffn_post_g):
    bsz, seq, _ = x.shape
    mem_n = rms_norm(mem, mem_g)
    for _layer in range(DEPTH):
        h = rms_norm(x, mix_pre_g)
        proj = h @ w_in
        o0 = SSM_WIDTH
        u = proj[..., :o0]
        q = proj[..., o0:o0 + FOX_WIDTH].reshape(bsz, seq, FOX_HEADS, FOX_HEAD_DIM)
        k = proj[..., o0 + FOX_WIDTH:o0 + 2 * FOX_WIDTH].reshape(bsz, seq, FOX_HEADS, FOX_HEAD_DIM)
        v = proj[..., o0 + 2 * FOX_WIDTH:o0 + 3 * FOX_WIDTH].reshape(bsz, seq, FOX_HEADS, FOX_HEAD_DIM)
        f_logit = proj[..., o0 + 3 * FOX_WIDTH:] + fox_f_bias

        y_ssm = s5_mixer(u, ssm_a_re, ssm_a_im, ssm_log_dt, ssm_b_re, ssm_b_im,
                         ssm_c_re, ssm_c_im, ssm_d, ssm_glu_w, ssm_glu_b)
        y_fox = forgetting_attention(q, k, v, f_logit).reshape(bsz, seq, FOX_WIDTH)
        y_mix = jnp.concatenate([rms_norm(y_ssm, ssm_out_g), rms_norm(y_fox, fox_out_g)], axis=-1)
        x = x + rms_norm(y_mix @ w_out, mix_post_g)

        h = rms_norm(x, xa_pre_g)
        x = x + rms_norm(memory_cross_attention(h, mem_n, xa_wq, xa_wkv, xa_wo), xa_post_g)

        h = rms_norm(x, ffn_pre_g)
        x = x + rms_norm(swiglu(h, w_gate, w_up, w_down), ffn_post_g)
    return x
```

```python
import numpy as np
from contextlib import ExitStack
import concourse.bass as bass
import concourse.mybir as mybir
from concourse.bass_utils import run_bass_kernel_spmd

F32 = mybir.dt.float32
BF16 = mybir.dt.bfloat16
I32 = mybir.dt.int32
ALU = mybir.AluOpType
AF = mybir.ActivationFunctionType
ENGS = ['pe', 'act', 'dve', 'pool', 'sp']

D = 1024
KC = 8
TT = 512
NT = 8
HC = 22
NM = 256
NBT = 864
NBP = 128
CH = 16
NSLOT = 5
NCH_T = NBT // CH
NCH_P = NBP // CH
EPS = 1e-6
TWO_PI = float(2 * np.pi)
SMP_N = 329 + 64
SMT_N = 2672


class Prog:
    def __init__(self, nc, es):
        self.nc = nc
        self.es = es
        self.ops = {e: [] for e in ENGS}
        self.cnt = {}
        self.semh = {}
        self.lastw = {}
        self.readers = {}
        self.seen = {e: {} for e in ENGS}
        for e in ENGS:
            self.newsem('S_' + e)

    def newsem(self, name):
        if name not in self.semh:
            self.semh[name] = self.es.enter_context(self.nc.semaphore(name))
            self.cnt[name] = 0

    def op(self, eng, fn, reads=(), writes=(), chan=None):
        need = {}

        def add(tok, raw):
            if tok is None:
                return
            sem, val, teng, isdma = tok
            if (not isdma) and teng == eng:
                if eng == 'pe':
                    return
            if need.get(sem, 0) < val:
                need[sem] = val

        for k in reads:
            add(self.lastw.get(k), True)
        for k in writes:
            add(self.lastw.get(k), False)
            for t in self.readers.get(k, {}).values():
                add(t, False)
        waits = []
        for sem, val in need.items():
            if self.seen[eng].get(sem, 0) < val:
                self.seen[eng][sem] = val
                waits.append((sem, val))
        if fn is None:
            self.ops[eng].append((waits, None, None, 0))
            return None
        if chan is not None:
            self.newsem(chan)
            sem, inc, isdma = chan, 16, True
        else:
            sem, inc, isdma = 'S_' + eng, 1, False
        self.cnt[sem] += inc
        tok = (sem, self.cnt[sem], eng, isdma)
        for k in writes:
            self.lastw[k] = tok
            self.readers[k] = {}
        for k in reads:
            self.readers.setdefault(k, {})[sem] = tok
        self.ops[eng].append((waits, fn, sem, inc))
        return tok

    def barrier(self):
        for e in ENGS:
            waits = []
            for sem, val in self.cnt.items():
                if val > 0 and self.seen[e].get(sem, 0) < val and sem != 'S_' + e:
                    self.seen[e][sem] = val
                    waits.append((sem, val))
            self.ops[e].append((waits, None, None, 0))

    def emit(self, block):
        decos = {'pe': block.tensor, 'act': block.scalar, 'dve': block.vector,
                 'pool': block.gpsimd, 'sp': block.sync}
        for e in ENGS:
            ops = self.ops[e]

            def body(engh, ops=ops):
                for waits, fn, sem, inc in ops:
                    for ws, wv in waits:
                        engh.wait_ge(self.semh[ws], wv)
                    if fn is not None:
                        ins = fn(engh)
                        ins.then_inc(self.semh[sem], inc)

            decos[e](body)


def build(nt=NT):
    S_ = nt * TT
    NBLKS = 4 * nt
    nc = bass.Bass("TRN2", target_bir_lowering=False)
    xT = nc.dram_tensor("xT", [D, S_], F32, kind="ExternalInput").ap()
    memT = nc.dram_tensor("memT", [D, NM], F32, kind="ExternalInput").ap()
    wst = nc.dram_tensor("wst", [128, (NBT + NBP) * 128], F32, kind="ExternalInput").ap()
    smp_d = nc.dram_tensor("smp", [128, SMP_N], F32, kind="ExternalInput").ap()
    smt_d = nc.dram_tensor("smt", [128, SMT_N], F32, kind="ExternalInput").ap()
    yT = nc.dram_tensor("yT", [D, S_], F32, kind="ExternalOutput").ap()
    wscr = nc.dram_tensor("wscr", [128, (NBT + NBP) * 128], BF16).ap()
    escr = nc.dram_tensor("escr", [128, 16 * 2 * 512], BF16).ap().rearrange("p (a c b) -> p a c b", c=2, b=512)
    xT_v = xT.rearrange("(kc p) t -> p kc t", p=128)
    yT_v = yT.rearrange("(kc p) t -> p kc t", p=128)
    memT_v = memT.rearrange("(kc p) t -> p kc t", p=128)

    with ExitStack() as es:
        P = Prog(nc, es)
        T = lambda name, shape, dt: es.enter_context(nc.sbuf_tensor(name, shape, dt))
        op = P.op

        smp = T("smp_s", [128, SMP_N], F32)
        negm = smp[:, 0:128]
        identf = smp[:, 128:256]
        pv = smp[:, 256:328]
        fb = smp[:, 328:329]
        g_mixpre, g_ssm, g_fox, g_mixpost = pv[:, 0:8], pv[:, 8:12], pv[:, 12:16], pv[:, 16:24]
        g_xapre, g_mem, g_xapost, g_ffnpre, g_ffnpost = pv[:, 24:32], pv[:, 32:40], pv[:, 40:48], pv[:, 48:56], pv[:, 56:64]
        d_skip, glu_b = pv[:, 64:68], pv[:, 68:72]
        cvec = T("cvec", [128, 8], F32)
        onesb = T("onesb", [128, 128], BF16)
        onesf = T("onesf", [128, 128], F32)
        identb = T("identb", [128, 128], BF16)
        selb = T("selb", [128, 8, 128], BF16)
        wfb = T("wfb", [128, 8, 8], BF16)
        ring = T("ring", [128, NSLOT, CH * 128], BF16)
        KT = T("KT", [128, 4, S_], BF16)
        VC = T("VC", [128, NBLKS, 8, 64], BF16)
        Ebuf = T("Ebuf", [128, 3, 2, 512], BF16)
        s5x = T("s5x", [128, 4, 512], F32)
        Btab = T("Btab", [128, 16, 2, 128], BF16)
        Ctab = T("Ctab", [128, 16, 2, 128], BF16)
        Rr = T("Rr", [128, 16], F32)
        E512 = T("E512", [128, 2, 16], F32)
        sinit = T("sinit", [128, 2, 16], F32)
        zl = T("zl", [128, 2, 16], F32)
        ztmp = T("ztmp", [128, 4, 16], F32)
        KM = T("KM", [128, 8, NM], BF16)
        VM = T("VM", [128, 2, D], BF16)
        sq = T("sq", [128, 2, 512], BF16)
        rstd = T("rstd", [128, 512], F32)
        pT = T("pT", [128, 4, 512], BF16)
        xr = T("xr", [128, 2, 2, 512], BF16)
        negmb = T("negmb", [128, 128], BF16)
        rl = T("rl", [128, 512], F32)
        rlb = T("rlb", [128, 512], F32)
        negF = T("negF", [128, NBLKS, 8], F32)
        biasT = T("biasT", [128, NBLKS, 8], F32)
        frbc = T("frbc", [128, 8], F32)
        fref = T("fref", [8, 1], F32)
        dg8 = T("dg8", [8, 8], F32)
        ft1 = T("ft1", [8, 512], F32)
        fabs_ = T("fabs", [8, 512], F32)
        ga = T("ga", [128, 512], BF16)
        gml = T("gml", [8, 2, 512], BF16)
        arena = T("arena", [128, 13824], F32)
        ps = [es.enter_context(nc.psum_tensor("ps%d" % i, [128, 512], F32)) for i in range(8)]
        PK = lambda b: ('ps', b)

        xs = arena[:, 0:4096].rearrange("p (a b) -> p a b", b=512)
        hid = arena[:, 4096:9728].bitcast(BF16).rearrange("p (a b) -> p a b", b=512)
        SC = arena[:, 4096:9728].rearrange("p (a b) -> p a b", b=512)
        ACT_ = arena[:, 9728:13824].bitcast(BF16).rearrange("p (a b) -> p a b", b=512)
        O3 = arena[:, 9728:13824].rearrange("p (a b) -> p a b", b=512)
        XK = lambda kc: ('xs', kc)
        HK = lambda c: ('H', c)
        SK = lambda i: [('H', 2 * i), ('H', 2 * i + 1)]
        AK = lambda c: ('A', c)
        O3K = lambda i: [('A', 2 * i), ('A', 2 * i + 1)]
        stage = arena[:, 0:4096].rearrange("p (a b) -> p a b", b=2048)
        cbuf = arena[:, 4096:6144].bitcast(BF16).rearrange("p (a b) -> p a b", b=2048)
        smt = arena[:, 6144:6144 + SMT_N]
        pw = arena[:, 8832:13824]
        iota = smt[:, 0:512]
        self_f = smt[:, 512:1536]
        wf_f = smt[:, 1536:1600]
        s5o = 1600
        are, aim, ldt = smt[:, s5o:s5o + 16], smt[:, s5o + 16:s5o + 32], smt[:, s5o + 32:s5o + 48]
        s5v = lambda k: smt[:, s5o + 48 + 256 * k: s5o + 48 + 256 * (k + 1)].rearrange("p (a b) -> p a b", b=16)
        bre_, bim_, cre_, cim_ = s5v(0), s5v(1), s5v(2), s5v(3)

        if NBLKS >= 20:
            lstage = VC[:, 4:20, :, :].rearrange("p a h d -> p (a h d)").bitcast(F32).rearrange("p (a b) -> p a b", b=2048)
        else:
            lstage = T("lstage", [128, 2, 2048], F32)
        class Ring:
            def __init__(self):
                self.seq = [NCH_T + i for i in range(NCH_P)] + [c for _ in range(nt) for c in range(NCH_T)]
                self.next_dma = 0
                self.cur = 0
                self.casted = set()
                self.ncast = 0

            def ensure(self, q_lo):
                lim = min(len(self.seq), q_lo + NSLOT)
                while self.next_dma < lim:
                    q = self.next_dma
                    slot = q % NSLOT
                    dc = self.seq[q]
                    if dc not in self.casted:
                        self.casted.add(dc)
                        b = self.ncast % 2
                        eng = ['act', 'dve'][self.ncast % 2]
                        self.ncast += 1
                        op('sp', lambda e, dc=dc, b=b: e.dma_start(out=lstage[:, b, :], in_=wst[:, dc * 2048:(dc + 1) * 2048]),
                           writes=[('stage', b)], chan='ST%d' % b)
                        if eng == 'act':
                            op('act', lambda e, b=b, slot=slot: e.activation(ring[:, slot, :], lstage[:, b, :], AF.Copy), reads=[('stage', b)], writes=[('ring', slot)])
                        else:
                            op(eng, lambda e, b=b, slot=slot: e.tensor_copy(ring[:, slot, :], lstage[:, b, :]), reads=[('stage', b)], writes=[('ring', slot)])
                        if dc < NCH_T and nt > 1:
                            op('pool', lambda e, dc=dc, slot=slot: e.dma_start(out=wscr[:, dc * 2048:(dc + 1) * 2048], in_=ring[:, slot, :]),
                               reads=[('ring', slot)], writes=[('wscr', dc)], chan='SO%d' % slot)
                    else:
                        op('sp', lambda e, slot=slot, dc=dc: e.dma_start(out=ring[:, slot, :], in_=wscr[:, dc * 2048:(dc + 1) * 2048]),
                           reads=[('wscr', dc)], writes=[('ring', slot)], chan='W%d' % slot)
                    self.next_dma += 1

            def take(self, n, span=0):
                g0 = self.cur
                self.cur += n
                self.ensure(g0 // CH)
                aps, keys = [], set()
                step = span if span else 1
                for g in range(g0, g0 + n, step):
                    q = g // CH
                    slot = q % NSLOT
                    j = g % CH
                    aps.append(ring[:, slot, j * 128:(j + step) * 128])
                    keys.add(('ring', slot))
                    if span:
                        assert (g + step - 1) // CH == q
                return aps, list(keys)

        R = Ring()

        def mm(out, pairs, reads, writes, start=True, stop=True):
            def fn(e):
                n = len(pairs)
                ins = None
                for i, (l, r) in enumerate(pairs):
                    ins = e.matmul(out, l, r, start=(start and i == 0), stop=(stop and i == n - 1))
                return ins
            op('pe', fn, reads, writes)

        op('sp', lambda e: e.dma_start(out=smp[:], in_=smp_d), writes=['smp'], chan='C0')
        op('sp', lambda e: e.dma_start(out=smt, in_=smt_d), writes=['smt'], chan='C1')
        op('dve', lambda e: e.memset(cvec[:], EPS), writes=['cvec'])
        op('dve', lambda e: e.tensor_scalar(cvec[0:8, 1:2], fb[0:8, :], -1.0, None, ALU.mult), reads=['smp', 'cvec'], writes=['cvec'])
        op('dve', lambda e: e.memset(cvec[:, 2:3], 0.25), reads=['cvec'], writes=['cvec'])
        op('dve', lambda e: e.memset(onesb[:], 1.0), writes=['onesb'])
        op('pool', lambda e: e.memset(onesf[:], 1.0), writes=['onesf'])
        op('dve', lambda e: e.tensor_copy(identb[:], identf), reads=['smp'], writes=['identb'])
        op('dve', lambda e: e.tensor_copy(negmb[:], negm), reads=['smp'], writes=['negmb'])
        op('dve', lambda e: e.tensor_copy(selb[:].rearrange("p a b -> p (a b)"), self_f), reads=['smt'], writes=['selb'])
        op('dve', lambda e: e.tensor_copy(wfb[:].rearrange("p a b -> p (a b)"), wf_f), reads=['smt'], writes=['wfb'])
        op('pool', lambda e: e.memset(fref[:], 0.0), writes=['fref'])
        op('pool', lambda e: e.memset(sinit[:], 0.0), writes=['sinit'])
        op('pool', lambda e: e.memset(ga[:], 0.0), writes=['ga'])

        ss = lambda i: pw[:, 16 * i:16 * (i + 1)]
        big = lambda i: pw[:, 1024 + 512 * i: 1024 + 512 * (i + 1)]
        bigi = pw[:, 1024 + 512 * 6: 1024 + 512 * 7].bitcast(I32)
        DT_, LRE, TH, TQ, C1, S1, P1R, P1I, DEN, QRE, QIM, TA, TB, TI_, Y5 = range(15)
        sI = pw[:, 16 * 20:16 * 21].bitcast(I32)

        def dv(fn, r=('pw',), w=('pw',)):
            op('dve', fn, reads=list(r), writes=list(w))

        def frac_reduce(y, tf, ti):
            dv(lambda e: e.tensor_copy(ti, y))
            dv(lambda e: e.tensor_copy(tf, ti))
            dv(lambda e: e.tensor_tensor(y, y, tf, ALU.subtract))
            dv(lambda e: e.tensor_single_scalar(tf, y, 0.5, ALU.is_gt))
            dv(lambda e: e.tensor_tensor(y, y, tf, ALU.subtract))
            dv(lambda e: e.tensor_single_scalar(tf, y, -0.5, ALU.is_lt))
            dv(lambda e: e.tensor_tensor(y, y, tf, ALU.add))

        def act_pw(fn, r=('pw',), w=('pw',)):
            op('act', fn, reads=list(r), writes=list(w))

        act_pw(lambda e: e.activation(ss(DT_), ldt, AF.Exp), r=('smt', 'pw'))
        dv(lambda e: e.tensor_tensor(ss(LRE), are, ss(DT_), ALU.mult), r=('smt', 'pw'))
        dv(lambda e: e.tensor_tensor(ss(TH), aim, ss(DT_), ALU.mult), r=('smt', 'pw'))
        act_pw(lambda e: e.activation(Rr[:], ss(LRE), AF.Exp), w=('Rr', 'pw'))
        dv(lambda e: e.tensor_scalar(ss(TQ), ss(TH), 1.0 / TWO_PI, None, ALU.mult))
        frac_reduce(ss(TQ), ss(TA), sI)
        act_pw(lambda e: e.activation(ss(S1), ss(TQ), AF.Sin, scale=TWO_PI))
        dv(lambda e: e.tensor_scalar(ss(Y5), ss(TQ), 0.25, None, ALU.add))
        frac_reduce(ss(Y5), ss(TA), sI)
        act_pw(lambda e: e.activation(ss(C1), ss(Y5), AF.Sin, scale=TWO_PI))
        dv(lambda e: e.tensor_scalar(ss(Y5), ss(TQ), 512.0, None, ALU.mult))
        frac_reduce(ss(Y5), ss(TA), sI)
        act_pw(lambda e: e.activation(E512[:, 1, :], ss(Y5), AF.Sin, scale=TWO_PI), w=('E512', 'pw'))
        dv(lambda e: e.tensor_scalar(ss(Y5), ss(Y5), 0.25, None, ALU.add))
        frac_reduce(ss(Y5), ss(TA), sI)
        act_pw(lambda e: e.activation(E512[:, 0, :], ss(Y5), AF.Sin, scale=TWO_PI), w=('E512', 'pw'))
        dv(lambda e: e.tensor_tensor(ss(P1R), Rr[:], ss(C1), ALU.mult), r=('Rr', 'pw'))
        dv(lambda e: e.tensor_tensor(ss(P1I), Rr[:], ss(S1), ALU.mult), r=('Rr', 'pw'))
        dv(lambda e: e.tensor_scalar(ss(P1R), ss(P1R), -1.0, None, ALU.add))
        dv(lambda e: e.tensor_tensor(ss(DEN), are, are, ALU.mult), r=('smt', 'pw'))
        dv(lambda e: e.tensor_tensor(ss(TA), aim, aim, ALU.mult), r=('smt', 'pw'))
        dv(lambda e: e.tensor_tensor(ss(DEN), ss(DEN), ss(TA), ALU.add))
        dv(lambda e: e.reciprocal(ss(DEN), ss(DEN)))
        dv(lambda e: e.tensor_tensor(ss(QRE), ss(P1R), are, ALU.mult), r=('smt', 'pw'))
        dv(lambda e: e.tensor_tensor(ss(TA), ss(P1I), aim, ALU.mult), r=('smt', 'pw'))
        dv(lambda e: e.tensor_tensor(ss(QRE), ss(QRE), ss(TA), ALU.add))
        dv(lambda e: e.tensor_tensor(ss(QRE), ss(QRE), ss(DEN), ALU.mult))
        dv(lambda e: e.tensor_tensor(ss(QIM), ss(P1I), are, ALU.mult), r=('smt', 'pw'))
        dv(lambda e: e.tensor_tensor(ss(TA), ss(P1R), aim, ALU.mult), r=('smt', 'pw'))
        dv(lambda e: e.tensor_tensor(ss(QIM), ss(QIM), ss(TA), ALU.subtract))
        dv(lambda e: e.tensor_tensor(ss(QIM), ss(QIM), ss(DEN), ALU.mult))
        bbre = pw[:, 512:768].rearrange("p (a b) -> p a b", b=16)
        bbim = pw[:, 768:1024].rearrange("p (a b) -> p a b", b=16)
        tb16 = pw[:, 400:416]
        BLK = arena[:, 0:2048].bitcast(BF16).rearrange("p (a c b) -> p a c b", c=2, b=128)
        dv(lambda e: e.memset(BLK, 0.0))
        dv(lambda e: e.memset(Ctab[:], 0.0), w=('Ctab',))
        for pt in range(16):
            q = pt % 4
            dv(lambda e, pt=pt: e.tensor_scalar(tb16, bim_[:, pt, :], ss(QIM)[:, pt:pt + 1], None, ALU.mult), r=('smt', 'pw'))
            dv(lambda e, pt=pt: e.scalar_tensor_tensor(bbre[:, pt, :], bre_[:, pt, :], ss(QRE)[:, pt:pt + 1], tb16, ALU.mult, ALU.subtract), r=('smt', 'pw'))
            dv(lambda e, pt=pt: e.tensor_scalar(tb16, bre_[:, pt, :], ss(QIM)[:, pt:pt + 1], None, ALU.mult), r=('smt', 'pw'))
            dv(lambda e, pt=pt: e.scalar_tensor_tensor(bbim[:, pt, :], bim_[:, pt, :], ss(QRE)[:, pt:pt + 1], tb16, ALU.mult, ALU.add), r=('smt', 'pw'))
            for c, src in ((0, bbre), (1, bbim)):
                dv(lambda e, pt=pt, c=c, src=src, q=q: e.tensor_copy(BLK[0:64, pt, c, 32 * q:32 * q + 16], src[0:64, pt, :]))
                dv(lambda e, pt=pt, c=c, src=src, q=q: e.tensor_copy(BLK[64:128, pt, c, 32 * q + 16:32 * q + 32], src[64:128, pt, :]))
            dv(lambda e, pt=pt, q=q: e.tensor_copy(Ctab[0:64, pt, 0, 32 * q:32 * q + 16], cre_[0:64, pt, :]), r=('smt',), w=('Ctab',))
            dv(lambda e, pt=pt, q=q: e.tensor_copy(Ctab[64:128, pt, 0, 32 * q + 16:32 * q + 32], cre_[64:128, pt, :]), r=('smt',), w=('Ctab',))
            dv(lambda e, pt=pt, q=q: e.tensor_scalar(Ctab[0:64, pt, 1, 32 * q:32 * q + 16], cim_[0:64, pt, :], -1.0, None, ALU.mult), r=('smt',), w=('Ctab',))
            dv(lambda e, pt=pt, q=q: e.tensor_scalar(Ctab[64:128, pt, 1, 32 * q + 16:32 * q + 32], cim_[64:128, pt, :], -1.0, None, ALU.mult), r=('smt',), w=('Ctab',))
        for grp in range(8):
            bank = grp % 2
            for k in range(4):
                idx = grp * 4 + k
                pt, c = idx // 2, idx % 2
                mm(ps[bank][:, k * 128:(k + 1) * 128], [(BLK[:, pt, c, :], identb[:])], reads=['pw', 'identb'], writes=[PK(bank)])
            op('act', lambda e, grp=grp, bank=bank: e.activation(
                Btab[:].rearrange("p a c b -> p (a c b)")[:, grp * 512:(grp + 1) * 512], ps[bank][:], AF.Copy),
                reads=[PK(bank)], writes=['Btab'])
        for pt in range(16):
            for which, dst in ((0, 1), (1, 0)):
                if pt % 2 == 0:
                    y = big(which * 3)
                    tf = big(which * 3 + 1)
                    ti = bigi if which == 0 else pw[:, 1024 + 512 * 2: 1024 + 512 * 3].bitcast(I32)
                    key = ('big', which)
                else:
                    pb_ = lambda i: arena[:, 2048 + 512 * i: 2048 + 512 * (i + 1)]
                    y = pb_(which * 3)
                    tf = pb_(which * 3 + 1)
                    ti = pb_(which * 3 + 2).bitcast(I32)
                    key = ('bigp', which)
                if which == 0:
                    op('act', lambda e, y=y, pt=pt: e.activation(y, iota, AF.Copy, scale=ss(TQ)[:, pt:pt + 1]), reads=['smt', 'pw'], writes=[key])
                else:
                    op('act', lambda e, y=y, pt=pt: e.activation(y, iota, AF.Identity, scale=ss(TQ)[:, pt:pt + 1], bias=cvec[:, 2:3]), reads=['smt', 'pw', 'cvec'], writes=[key])
                for fn_ in (
                    lambda e, y=y, tf=tf, ti=ti: e.tensor_copy(ti, y),
                    lambda e, y=y, tf=tf, ti=ti: e.tensor_copy(tf, ti),
                    lambda e, y=y, tf=tf, ti=ti: e.tensor_tensor(y, y, tf, ALU.subtract),
                ):
                    op('dve', fn_, reads=[key], writes=[key])
                op('act', lambda e, y=y, dst=dst, pt=pt: e.activation(Ebuf[:, pt % 2, dst, :], y, AF.Sin, scale=TWO_PI), reads=[key], writes=[('E', pt % 2)])
            op('sp', lambda e, pt=pt: e.dma_start(out=escr[:, pt, :, :], in_=Ebuf[:, pt % 2, :, :]), reads=[('E', pt % 2)], writes=[('escr', pt)], chan='EO%d' % (pt % 2))

        mT = arena[:, 2048:4096].rearrange("p (a b) -> p a b", b=256)
        mnT = arena[:, 4096:5120].bitcast(BF16).rearrange("p (a b) -> p a b", b=256)
        op('sp', lambda e: e.dma_start(out=mT, in_=memT_v), writes=['mT', 'mnT', ('bigp', 0), ('bigp', 1)], chan='C2')
        for kc in range(KC):
            b = kc % 2
            op('act', lambda e, kc=kc, b=b: e.activation(sq[:, b, 0:256], mT[:, kc, :], AF.Square), reads=['mT'], writes=[('sq', b)])
            mm(ps[7][:, 0:256], [(onesb[:], sq[:, b, 0:256])], reads=[('sq', b), 'onesb'], writes=[PK(7)], start=(kc == 0), stop=(kc == KC - 1))
        op('act', lambda e: e.activation(rstd[:, 0:256], ps[7][:, 0:256], AF.Sqrt, bias=cvec[:, 0:1], scale=1.0 / D), reads=[PK(7), 'cvec'], writes=['rstd'])
        op('dve', lambda e: e.reciprocal(rstd[:, 0:256], rstd[:, 0:256]), reads=['rstd'], writes=['rstd'])
        for kc in range(KC):
            op('dve', lambda e, kc=kc: e.scalar_tensor_tensor(mnT[:, kc, :], mT[:, kc, :], g_mem[:, kc:kc + 1], rstd[:, 0:256], ALU.mult, ALU.mult),
               reads=['mT', 'rstd', 'smp'], writes=['mnT'])
        for m in range(8):
            blks, rk = R.take(8)
            bank = m % 2
            mm(ps[bank][:, 0:256], [(blks[kc], mnT[:, kc, :]) for kc in range(KC)], reads=rk + ['mnT'], writes=[PK(bank)])
            op('act', lambda e, m=m, bank=bank: e.activation(KM[:, m, :], ps[bank][:, 0:256], AF.Copy), reads=[PK(bank)], writes=['KM'])
        spans, rk = R.take(64, span=4)
        for mb in range(2):
            for hf in range(2):
                bank = (mb * 2 + hf) % 2
                mm(ps[bank][:], [(mnT[:, kc, mb * 128:(mb + 1) * 128], spans[kc * 2 + hf]) for kc in range(KC)], reads=rk + ['mnT'], writes=[PK(bank)])
                op('dve', lambda e, mb=mb, hf=hf, bank=bank: e.tensor_copy(VM[:, mb, hf * 512:(hf + 1) * 512], ps[bank][:]), reads=[PK(bank)], writes=['VM'])
        P.barrier()

        def rms_rstd(srcs, nfeat):
            n = len(srcs)
            for i, (ap, keys) in enumerate(srcs):
                b = i % 2
                op('act', lambda e, ap=ap, b=b: e.activation(sq[:, b, :], ap, AF.Square), reads=keys, writes=[('sq', b)])
                mm(ps[7][:], [(onesb[:], sq[:, b, :])], reads=[('sq', b)], writes=[PK(7)], start=(i == 0), stop=(i == n - 1))
            op('act', lambda e: e.activation(rstd[:], ps[7][:], AF.Ln, bias=cvec[:, 0:1], scale=1.0 / nfeat), reads=[PK(7)], writes=['rstd'])
            op('act', lambda e: e.activation(rstd[:], rstd[:], AF.Exp, scale=-0.5), reads=['rstd'], writes=['rstd'])

        def pre_norm(gain, dst_base):
            rms_rstd([(xs[:, kc, :], [XK(kc)]) for kc in range(KC)], D)
            for kc in range(KC):
                eng = 'dve'
                op(eng, lambda e, kc=kc: e.scalar_tensor_tensor(ACT_[:, dst_base + kc, :], xs[:, kc, :], gain[:, kc:kc + 1], rstd[:], ALU.mult, ALU.mult),
                   reads=[XK(kc), 'rstd'], writes=[AK(dst_base + kc)])

        def post_norm_residual(osrc, okeys, gain, final=False):
            rms_rstd([(osrc(kc), okeys(kc)) for kc in range(KC)], D)
            for kc in range(KC):
                op('dve', lambda e, kc=kc: e.scalar_tensor_tensor(osrc(kc), osrc(kc), gain[:, kc:kc + 1], rstd[:], ALU.mult, ALU.mult),
                   reads=okeys(kc) + ['rstd'], writes=okeys(kc))
                if final:
                    op('pool', lambda e, kc=kc: e.tensor_tensor(osrc(kc), xs[:, kc, :], osrc(kc), ALU.add),
                       reads=okeys(kc) + [XK(kc)], writes=okeys(kc))
                else:
                    op('pool', lambda e, kc=kc: e.tensor_tensor(xs[:, kc, :], xs[:, kc, :], osrc(kc), ALU.add),
                       reads=okeys(kc) + [XK(kc)], writes=[XK(kc)])

        def proj8(src_base, evac):
            for m in range(8):
                blks, rk = R.take(8)
                bank = m % 2
                mm(ps[bank][:], [(blks[kc], ACT_[:, src_base + kc, :]) for kc in range(KC)],
                   reads=rk + [AK(src_base + kc) for kc in range(KC)], writes=[PK(bank)])
                evac(m, bank)

        def evac_copy(dst, dkeys, bank, i, scale=None):
            if i % 2 == 0:
                if scale is None:
                    op('act', lambda e: e.activation(dst, ps[bank][:], AF.Copy), reads=[PK(bank)], writes=dkeys)
                else:
                    op('act', lambda e: e.activation(dst, ps[bank][:], AF.Copy, scale=scale), reads=[PK(bank)], writes=dkeys)
            else:
                if scale is None:
                    op('dve', lambda e: e.tensor_copy(dst, ps[bank][:]), reads=[PK(bank)], writes=dkeys)
                else:
                    op('dve', lambda e: e.tensor_scalar(dst, ps[bank][:], scale, None, ALU.mult), reads=[PK(bank)], writes=dkeys)

        for ti in range(nt):
            T0 = ti * TT
            op('pool', lambda e, T0=T0: e.dma_start(out=xs, in_=xT_v[:, :, T0:T0 + TT]), writes=[XK(kc) for kc in range(KC)], chan='XL')
            pre_norm(g_mixpre, 0)
            hkeys = [AK(kc) for kc in range(KC)]
            for m in range(12):
                blks, rk = R.take(8)
                bank = m % 2
                mm(ps[bank][:], [(blks[kc], ACT_[:, kc, :]) for kc in range(KC)], reads=rk + hkeys, writes=[PK(bank)])
                if m < 4:
                    evac_copy(ACT_[:, 8 + m, :], [AK(8 + m)], bank, m)
                elif m < 8:
                    evac_copy(ACT_[:, 8 + m, :], [AK(8 + m)], bank, m, scale=0.125)
                else:
                    evac_copy(KT[:, m - 8, T0:T0 + TT], [('KT', m - 8, ti)], bank, m)

            def load_E(pt):
                op('sp', lambda e, pt=pt: e.dma_start(out=Ebuf[:, pt % 3, :, :], in_=escr[:, pt, :, :]), reads=[('escr', pt)], writes=[('E', pt % 3)], chan='EL%d' % (pt % 3))

            def s5A(pt):
                ut = pt // 4
                uk = [AK(8 + ut)]
                mm(ps[0][:], [(Btab[:, pt, 0, :], ACT_[:, 8 + ut, :])], reads=uk, writes=[PK(0)])
                mm(ps[1][:], [(Btab[:, pt, 1, :], ACT_[:, 8 + ut, :])], reads=uk, writes=[PK(1)])

            def s5set(pt):
                par = pt % 2
                st = pt % 3
                if st < 2:
                    S = [SC[:, 4 * st + i, :] for i in range(4)]
                    K_ = [SK(4 * st + i) for i in range(4)]
                else:
                    S = [s5x[:, i, :] for i in range(4)]
                    K_ = [[('s5x', i)] for i in range(4)]
                return (par, S, K_, Ebuf[:, st, 0, :], Ebuf[:, st, 1, :], [('E', st)])

            def s5B1(pt):
                par, S, K, c_, s_, EK = s5set(pt)
                op('dve', lambda e: e.tensor_tensor(S[0], ps[0][:], c_, ALU.mult), reads=[PK(0)] + EK, writes=K[0])
                op('dve', lambda e: e.tensor_tensor(S[1], ps[1][:], s_, ALU.mult), reads=[PK(1)] + EK, writes=K[1])
                op('dve', lambda e: e.tensor_tensor(S[2], ps[1][:], c_, ALU.mult), reads=[PK(1)] + EK, writes=K[2])
                op('dve', lambda e: e.tensor_tensor(S[3], ps[0][:], s_, ALU.mult), reads=[PK(0)] + EK, writes=K[3])

            def s5B2(pt):
                par, S, K, c_, s_, EK = s5set(pt)
                op('pool', lambda e: e.tensor_tensor(S[0], S[0], S[1], ALU.add), reads=K[0] + K[1], writes=K[0])
                op('pool', lambda e: e.tensor_tensor(S[2], S[2], S[3], ALU.subtract), reads=K[2] + K[3], writes=K[2])

            def s5B3(pt):
                par, S, K, c_, s_, EK = s5set(pt)
                op('dve', lambda e: e.tensor_tensor_scan(S[1], Rr[:, pt:pt + 1].to_broadcast([128, 512]), S[0], sinit[:, 0, pt:pt + 1], ALU.mult, ALU.add),
                   reads=K[0] + ['sinit'], writes=K[1])
                op('dve', lambda e: e.tensor_tensor_scan(S[3], Rr[:, pt:pt + 1].to_broadcast([128, 512]), S[2], sinit[:, 1, pt:pt + 1], ALU.mult, ALU.add),
                   reads=K[2] + ['sinit'], writes=K[3])
                op('act', lambda e: e.activation(zl[:, 0, pt:pt + 1], S[1][:, 511:512], AF.Copy), reads=K[1], writes=['zl'])
                op('act', lambda e: e.activation(zl[:, 1, pt:pt + 1], S[3][:, 511:512], AF.Copy), reads=K[3], writes=['zl'])

            def s5C1(pt):
                par, S, K, c_, s_, EK = s5set(pt)
                op('pool', lambda e: e.tensor_tensor(S[0], S[1], c_, ALU.mult), reads=K[1] + EK, writes=K[0])
                op('pool', lambda e: e.tensor_tensor(S[2], S[3], s_, ALU.mult), reads=K[3] + EK, writes=K[2])

            def s5C1b(pt):
                par, S, K, c_, s_, EK = s5set(pt)
                op('pool', lambda e: e.tensor_tensor(S[3], S[3], c_, ALU.mult), reads=K[3] + EK, writes=K[3])
                op('pool', lambda e: e.tensor_tensor(S[1], S[1], s_, ALU.mult), reads=K[1] + EK, writes=K[1])
                if pt + 3 < 16:
                    load_E(pt + 3)

            def s5C2a(pt):
                par, S, K, c_, s_, EK = s5set(pt)
                op('dve', lambda e: e.tensor_tensor(xr[:, par, 0, :], S[0], S[2], ALU.subtract), reads=K[0] + K[2], writes=[('xr', par, 0)])

            def s5C2b(pt):
                par, S, K, c_, s_, EK = s5set(pt)
                op('dve', lambda e: e.tensor_tensor(xr[:, par, 1, :], S[3], S[1], ALU.add), reads=K[3] + K[1], writes=[('xr', par, 1)])

            def s5D(pt):
                par = pt % 2
                ut = pt // 4
                mm(ps[2][:], [(Ctab[:, pt, 0, :], xr[:, par, 0, :]), (Ctab[:, pt, 1, :], xr[:, par, 1, :])],
                   reads=[('xr', par, 0), ('xr', par, 1)], writes=[PK(2)], start=(pt % 4 == 0), stop=(pt % 4 == 3))
                if pt % 4 == 3:
                    Y = rstd[:]
                    W = sq[:].rearrange("p a b -> p (a b)").bitcast(F32)
                    YK, WK = ['rstd'], [('sq', 0), ('sq', 1)]
                    op('dve', lambda e: e.scalar_tensor_tensor(Y, ACT_[:, 8 + ut, :], d_skip[:, ut:ut + 1], ps[2][:], ALU.mult, ALU.add),
                       reads=[PK(2), AK(8 + ut)], writes=YK)
                    op('act', lambda e: e.activation(W, Y, AF.Square), reads=YK, writes=WK)
                    op('act', lambda e: e.activation(W, W, AF.Identity, scale=0.044715, bias=1.0), reads=WK, writes=WK)
                    op('dve', lambda e: e.tensor_tensor(W, W, Y, ALU.mult), reads=WK + YK, writes=WK)
                    op('act', lambda e: e.activation(W, W, AF.Tanh, scale=float(np.sqrt(2.0 / np.pi))), reads=WK, writes=WK)
                    op('dve', lambda e: e.scalar_tensor_tensor(W, W, 1.0, Y, ALU.add, ALU.mult), reads=WK + YK, writes=WK)
                    op('act', lambda e: e.activation(ACT_[:, ut, :], W, AF.Copy, scale=0.5), reads=WK, writes=[AK(ut)])

            nkb = 4 * ti + 4
            YFt = [SC[:, 8, :], SC[:, 9, :], SC[:, 10, :], arena[:, 9728 + 1024:9728 + 1536]]
            YFk = [SK(8), SK(9), SK(10), [AK(4), AK(5)]]
            fox_items = []
            rot = [0, 0]
            for m_ in range(4):
                hA, hB = 2 * m_, 2 * m_ + 1
                qk = AK(12 + m_)

                def stageA(kb, m_=m_, hA=hA, hB=hB, qk=qk):
                    sa = 3 + (rot[0] % 3)
                    sb = 3 + ((rot[0] + 1) % 3)
                    rot[0] += 2
                    ra = rot[1] % 4
                    rb = (rot[1] + 1) % 4
                    rot[1] += 2
                    intile = kb >= 4 * ti
                    c0 = 128 * (kb - 4 * ti) if intile else 0

                    def fn(e):
                        e.matmul(ps[sa][:, c0:512], KT[0:64, m_, kb * 128:(kb + 1) * 128], ACT_[0:64, 12 + m_, c0:512], start=True, stop=False)
                        e.matmul(ps[sb][:, c0:512], KT[64:128, m_, kb * 128:(kb + 1) * 128], ACT_[64:128, 12 + m_, c0:512], start=True, stop=False)
                        e.matmul(ps[sa][:, c0:512], selb[0:24, hA, :], ga[0:24, c0:512], start=False, stop=not intile)
                        i2 = e.matmul(ps[sb][:, c0:512], selb[64:88, hB, :], ga[64:88, c0:512], start=False, stop=not intile)
                        if intile:
                            e.matmul(ps[sa][:, c0:c0 + 128], identb[:], negmb[:], start=False, stop=True)
                            i2 = e.matmul(ps[sb][:, c0:c0 + 128], identb[:], negmb[:], start=False, stop=True)
                        return i2
                    op('pe', fn, reads=[('KT', m_, kb // 4), qk, 'ga'], writes=[PK(sa), PK(sb)])
                    for (sx, rx, hx) in ((sa, ra, hA), (sb, rb, hB)):
                        op('act', lambda e, sx=sx, rx=rx, hx=hx: e.activation(pT[:, rx, c0:512], ps[sx][:, c0:512], AF.Exp, bias=biasT[:, kb, hx:hx + 1], scale=1.0),
                           reads=[PK(sx), 'biasT'], writes=[('pT', rx)])
                    return ra, rb, c0

                def stageB(kb, st, hA=hA, hB=hB):
                    ra, rb, c0 = st
                    first, last = (kb == 0), (kb == nkb - 1)

                    def fn(e):
                        e.matmul(ps[6][0:64, c0:512], VC[:, kb, hA, :], pT[:, ra, c0:512], start=first, stop=last, tile_position=(0, 0))
                        e.matmul(ps[6][64:128, c0:512], VC[:, kb, hB, :], pT[:, rb, c0:512], start=first, stop=last, tile_position=(0, 64))
                        e.matmul(ps[7][0:64, c0:512], onesb[:, 0:64], pT[:, ra, c0:512], start=first, stop=last, tile_position=(0, 0))
                        return e.matmul(ps[7][64:128, c0:512], onesb[:, 64:128], pT[:, rb, c0:512], start=first, stop=last, tile_position=(0, 64))
                    op('pe', fn, reads=[('pT', ra), ('pT', rb), ('VC', kb)], writes=[PK(6), PK(7)])

                def fin(m_=m_):
                    op('act', lambda e: e.activation(rl[:], ps[7][:], AF.Copy), reads=[PK(7)], writes=['rl'])
                    op('act', lambda e: e.activation(rlb[:], ps[6][:], AF.Copy), reads=[PK(6)], writes=['rlb'])
                    op('dve', lambda e: e.reciprocal(rl[:], rl[:]), reads=['rl'], writes=['rl'])
                    op('dve', lambda e: e.tensor_tensor(YFt[m_], rlb[:], rl[:], ALU.mult), reads=['rlb', 'rl'], writes=YFk[m_])

                state = {}

                def item_first(stageA=stageA, state=state):
                    state[0] = stageA(0)

                def item_mid(kb, stageA=stageA, stageB=stageB, state=state):
                    if kb + 1 < nkb:
                        state[kb + 1] = stageA(kb + 1)
                    stageB(kb, state[kb])

                fox_items.append(item_first)
                if m_ > 0:
                    fox_items.append(prev_fin[0])
                for kb in range(nkb):
                    fox_items.append(lambda kb=kb, item_mid=item_mid: item_mid(kb))
                prev_fin = [fin]
            fox_items.append(prev_fin[0])

            def s5_slot(k):
                if ok(k):
                    s5A(k)
                    s5B1(k)
                    s5B2(k)
                if ok(k - 2):
                    s5C2a(k - 2)
                    s5C2b(k - 2)
                if ok(k - 1):
                    s5B3(k - 1)
                    s5C1(k - 1)
                    s5C1b(k - 1)
                if ok(k - 3):
                    s5D(k - 3)

            ok = lambda p: 0 <= p < 16
            load_E(0)
            load_E(1)
            load_E(2)
            spans, rk = R.take(32, span=4)

            def vproj(tb):
                bank = 3 + tb % 2
                mm(ps[bank][:], [(ACT_[:, kc, tb * 128:(tb + 1) * 128], spans[kc]) for kc in range(KC)], reads=rk + hkeys, writes=[PK(bank)])
                blk = 4 * ti + tb
                op('dve' if tb % 2 else 'act',
                   (lambda e, blk=blk, bank=bank: e.tensor_copy(VC[:, blk, :, :], ps[bank][:].rearrange("p (h d) -> p h d", d=64))) if tb % 2 else
                   (lambda e, blk=blk, bank=bank: e.activation(VC[:, blk, :, :], ps[bank][:].rearrange("p (h d) -> p h d", d=64), AF.Copy)),
                   reads=[PK(bank)], writes=[('VC', blk)] + ([('stage', 0), ('stage', 1)] if 4 <= blk < 20 else []))
            s5_slot(0)
            vproj(0)
            vproj(1)
            s5_slot(1)
            vproj(2)
            vproj(3)
            s5_slot(2)
            mm(ps[5][0:8, :], [(wfb[:, kc, :], ACT_[:, kc, :]) for kc in range(KC)], reads=hkeys + ['wfb'], writes=[PK(5)])
            op('act', lambda e: e.activation(ft1[:], ps[5][0:8, :], AF.Exp, bias=cvec[0:8, 1:2], scale=-1.0), reads=[PK(5)], writes=['ft1'])
            op('act', lambda e: e.activation(ft1[:], ft1[:], AF.Ln, bias=1.0, scale=1.0), reads=['ft1'], writes=['ft1'])
            op('dve', lambda e: e.tensor_tensor_scan(fabs_[:], onesf[0:8, 0:1].to_broadcast([8, 512]), ft1[:], fref[:, 0:1], ALU.mult, ALU.subtract),
               reads=['ft1', 'fref'], writes=['fabs'])
            op('dve', lambda e: e.tensor_scalar(ft1[:], fabs_[:], fref[:, 0:1], None, ALU.subtract), reads=['fabs', 'fref'], writes=['ft1'])
            op('dve', lambda e: e.tensor_copy(ga[0:8, :], ft1[:]), reads=['ft1'], writes=['ga'])
            op('dve', lambda e: e.tensor_copy(ga[64:72, :], ft1[:]), reads=['ft1'], writes=['ga'])
            op('dve', lambda e: e.tensor_tensor(ft1[:], ft1[:], ga[0:8, :], ALU.subtract), reads=['ft1', 'ga'], writes=['ft1'])
            op('dve', lambda e: e.tensor_copy(gml[:, 0, :], ft1[:]), reads=['ft1'], writes=['gml'])
            op('dve', lambda e: e.tensor_tensor(ft1[:], ft1[:], gml[:, 0, :], ALU.subtract), reads=['ft1', 'gml'], writes=['ft1'])
            op('dve', lambda e: e.tensor_copy(gml[:, 1, :], ft1[:]), reads=['ft1'], writes=['gml'])
            op('pool', lambda e: e.dma_start(out=ga[8:16, :], in_=gml[:, 0, :]), reads=['gml'], writes=['ga'], chan='GA0')
            op('pool', lambda e: e.dma_start(out=ga[16:24, :], in_=gml[:, 1, :]), reads=['gml'], writes=['ga'], chan='GA1')
            op('pool', lambda e: e.dma_start(out=ga[72:80, :], in_=gml[:, 0, :]), reads=['gml'], writes=['ga'], chan='GA2')
            op('pool', lambda e: e.dma_start(out=ga[80:88, :], in_=gml[:, 1, :]), reads=['gml'], writes=['ga'], chan='GA3')
            for jb in range(4):
                mm(ps[4][:, jb * 8:(jb + 1) * 8], [(fabs_[:, jb * 128:(jb + 1) * 128], identf[0:8, 0:8])], reads=['fabs'], writes=[PK(4)])
            op('dve', lambda e, ti=ti: e.tensor_scalar(negF[:, 4 * ti:4 * ti + 4, :], ps[4][:, 0:32].rearrange("p (a b) -> p a b", b=8), -1.0, None, ALU.mult),
               reads=[PK(4)], writes=['negF'])
            op('dve', lambda e: e.tensor_scalar(dg8[:], identf[0:8, 0:8], fref[:, 0:1], None, ALU.mult), reads=['fref'], writes=['dg8'])
            mm(ps[4][:, 64:72], [(onesf[0:8, 0:128], dg8[:])], reads=['dg8'], writes=[PK(4)])
            op('dve', lambda e: e.tensor_copy(frbc[:], ps[4][:, 64:72]), reads=[PK(4)], writes=['frbc'])
            for kb in range(4 * ti + 4):
                op('dve', lambda e, kb=kb: e.tensor_tensor(biasT[:, kb, :], negF[:, kb, :], frbc[:], ALU.add), reads=['negF', 'frbc'], writes=['biasT'])
            op('dve', lambda e: e.tensor_copy(fref[:], fabs_[:, 511:512]), reads=['fabs'], writes=['fref'])
            NPRE = 3
            NSL = 20
            per = -(-len(fox_items) // (NSL - NPRE))
            fi = 0
            for k in range(NPRE, NSL):
                s5_slot(k)
                for _ in range(per):
                    if fi < len(fox_items):
                        fox_items[fi]()
                        fi += 1
            while fi < len(fox_items):
                fox_items[fi]()
                fi += 1
            c5, s5 = E512[:, 0, :], E512[:, 1, :]
            op('dve', lambda e: e.tensor_tensor(ztmp[:, 0, :], zl[:, 0, :], c5, ALU.mult), reads=['zl'], writes=['ztmp'])
            op('dve', lambda e: e.tensor_tensor(ztmp[:, 1, :], zl[:, 1, :], s5, ALU.mult), reads=['zl'], writes=['ztmp'])
            op('dve', lambda e: e.tensor_tensor(ztmp[:, 2, :], zl[:, 1, :], c5, ALU.mult), reads=['zl'], writes=['ztmp'])
            op('dve', lambda e: e.tensor_tensor(ztmp[:, 3, :], zl[:, 0, :], s5, ALU.mult), reads=['zl'], writes=['ztmp'])
            op('dve', lambda e: e.tensor_tensor(sinit[:, 0, :], ztmp[:, 0, :], ztmp[:, 1, :], ALU.subtract), reads=['ztmp'], writes=['sinit'])
            op('dve', lambda e: e.tensor_tensor(sinit[:, 1, :], ztmp[:, 2, :], ztmp[:, 3, :], ALU.add), reads=['ztmp'], writes=['sinit'])
            for m in range(4):
                blks, rk = R.take(4)
                bank = m % 2
                mm(ps[bank][:], [(blks[kc], ACT_[:, kc, :]) for kc in range(4)], reads=rk + [AK(kc) for kc in range(4)], writes=[PK(bank)])
                op('act', lambda e, m=m, bank=bank: e.activation(SC[:, 4, :], ps[bank][:], AF.Sigmoid, bias=glu_b[:, m:m + 1], scale=1.0), reads=[PK(bank)], writes=SK(4))
                op('dve', lambda e, m=m: e.tensor_tensor(SC[:, m, :], ACT_[:, m, :], SC[:, 4, :], ALU.mult), reads=SK(4) + [AK(m)], writes=SK(m))
            rms_rstd([(SC[:, m, :], SK(m)) for m in range(4)], 512)
            for m in range(4):
                op('dve', lambda e, m=m: e.scalar_tensor_tensor(ACT_[:, m, :], SC[:, m, :], g_ssm[:, m:m + 1], rstd[:], ALU.mult, ALU.mult),
                   reads=SK(m) + ['rstd'], writes=[AK(m)])
            rms_rstd([(YFt[m], YFk[m]) for m in range(4)], 512)
            for m in (3, 0, 1, 2):
                op('dve', lambda e, m=m: e.scalar_tensor_tensor(ACT_[:, 4 + m, :], YFt[m], g_fox[:, m:m + 1], rstd[:], ALU.mult, ALU.mult),
                   reads=YFk[m] + ['rstd'], writes=[AK(4 + m)])

            proj8(0, lambda m, bank: evac_copy(SC[:, m, :], SK(m), bank, m))
            post_norm_residual(lambda kc: SC[:, kc, :], SK, g_mixpost)

            pre_norm(g_xapre, 8)
            proj8(8, lambda m, bank: evac_copy(ACT_[:, m, :], [AK(m)], bank, m, scale=1.0 / 16.0))
            for hx in range(4):
                c0_, c1_ = 2 * hx, 2 * hx + 1
                par = hx % 2
                ob = (4, 5, 6) if par == 0 else (0, 1, 7)
                rlt, rlk = (rl, 'rl') if par == 0 else (rlb, 'rlb')
                for mb in range(2):
                    sb = 2 + mb
                    pi = 2 * par + mb
                    mm(ps[sb][:], [(KM[:, c0_, mb * 128:(mb + 1) * 128], ACT_[:, c0_, :]), (KM[:, c1_, mb * 128:(mb + 1) * 128], ACT_[:, c1_, :])],
                       reads=[AK(c0_), AK(c1_)], writes=[PK(sb)])
                    op('act', lambda e, sb=sb, pi=pi: e.activation(pT[:, pi, :], ps[sb][:], AF.Exp), reads=[PK(sb)], writes=[('pT', pi)])
                for mb in range(2):
                    pi = 2 * par + mb
                    for dc in range(2):
                        mm(ps[ob[dc]][:], [(VM[:, mb, (2 * hx + dc) * 128:(2 * hx + dc + 1) * 128], pT[:, pi, :])], reads=[('pT', pi)], writes=[PK(ob[dc])],
                           start=(mb == 0), stop=(mb == 1))
                    mm(ps[ob[2]][:], [(onesb[:], pT[:, pi, :])], reads=[('pT', pi)], writes=[PK(ob[2])], start=(mb == 0), stop=(mb == 1))
                op('dve', lambda e, rlt=rlt, ob=ob: e.reciprocal(rlt[:], ps[ob[2]][:]), reads=[PK(ob[2])], writes=[rlk])
                for dc in range(2):
                    op('dve', lambda e, dc=dc, hx=hx, rlt=rlt, ob=ob: e.tensor_tensor(ACT_[:, 8 + 2 * hx + dc, :], ps[ob[dc]][:], rlt[:], ALU.mult),
                       reads=[PK(ob[dc]), rlk], writes=[AK(8 + 2 * hx + dc)])
            proj8(8, lambda m, bank: evac_copy(SC[:, m, :], SK(m), bank, m))
            post_norm_residual(lambda kc: SC[:, kc, :], SK, g_xapost)

            pre_norm(g_ffnpre, 0)
            for m in range(HC):
                blks, rk = R.take(16)
                par = m % 2
                bg, bu = 2 + 2 * par, 3 + 2 * par
                mm(ps[bg][:], [(blks[kc], ACT_[:, kc, :]) for kc in range(KC)], reads=rk + hkeys, writes=[PK(bg)])
                mm(ps[bu][:], [(blks[8 + kc], ACT_[:, kc, :]) for kc in range(KC)], reads=rk + hkeys, writes=[PK(bu)])
                tq_, tk_ = (rl, 'rl') if par == 0 else (rlb, 'rlb')
                op('act', lambda e, tq_=tq_, bg=bg: e.activation(tq_[:], ps[bg][:], AF.Silu), reads=[PK(bg)], writes=[tk_])
                op('dve', lambda e, tq_=tq_, bu=bu, m=m: e.tensor_tensor(hid[:, m, :], tq_[:], ps[bu][:], ALU.mult),
                   reads=[PK(bu), tk_], writes=[HK(m)])
            for m in range(8):
                blks, rk = R.take(HC)
                bank = m % 2
                mm(ps[bank][:], [(blks[kc], hid[:, kc, :]) for kc in range(HC)], reads=rk + [HK(kc) for kc in range(HC)], writes=[PK(bank)])
                evac_copy(O3[:, m, :], O3K(m), bank, m)
            post_norm_residual(lambda kc: O3[:, kc, :], O3K, g_ffnpost, final=True)
            op('sp', lambda e, T0=T0: e.dma_start(out=yT_v[:, :, T0:T0 + TT], in_=O3), reads=[AK(c) for c in range(16)], writes=[('yT', ti)], chan='XO')

        op('sp', None, reads=[('yT', ti) for ti in range(nt)])
        with nc.Block() as block:
            P.emit(block)
    return nc


def _blk(W, kc, m):
    return W[kc * 128:(kc + 1) * 128, m * 128:(m + 1) * 128]


def _weight_stream(w_in, glu_w, w_out, xa_wq, xa_wo, w_gate, w_up, w_down, xa_wkv):
    blocks = []
    wi = w_in[:, :1536]
    for m in range(12):
        for kc in range(8):
            blocks.append(_blk(wi, kc, m))
    wv = w_in[:, 1536:2048]
    for kc in range(8):
        for j in range(4):
            blocks.append(_blk(wv, kc, j))
    for m in range(4):
        for kc in range(4):
            blocks.append(_blk(glu_w, kc, m))
    for W in (w_out, xa_wq, xa_wo):
        for m in range(8):
            for kc in range(8):
                blocks.append(_blk(W, kc, m))
    for m in range(HC):
        for kc in range(8):
            blocks.append(_blk(w_gate, kc, m))
        for kc in range(8):
            blocks.append(_blk(w_up, kc, m))
    for m in range(8):
        for kc in range(HC):
            blocks.append(_blk(w_down, kc, m))
    assert len(blocks) == NBT
    wk, wvv = xa_wkv[:, :1024], xa_wkv[:, 1024:]
    for m in range(8):
        for kc in range(8):
            blocks.append(_blk(wk, kc, m))
    for kc in range(8):
        for j in range(8):
            blocks.append(_blk(wvv, kc, j))
    assert len(blocks) == NBT + NBP
    return np.ascontiguousarray(np.concatenate(blocks, axis=1), dtype=np.float32)


def _fm(v, n):
    return np.asarray(v, np.float32).reshape(n, 128).T


def _prep_shared(inp):
    f = lambda k: np.asarray(inp[k], np.float32)
    wst = _weight_stream(f("w_in"), f("ssm_glu_w"), f("w_out"), f("xa_wq"), f("xa_wo"), f("w_gate"), f("w_up"), f("w_down"), f("xa_wkv"))
    smp = np.zeros((128, SMP_N), np.float32)
    s_idx = np.arange(128)[:, None]
    t_idx = np.arange(128)[None, :]
    smp[:, 0:128] = np.where(s_idx <= t_idx, 0.0, -30000.0)
    smp[:, 128:256] = np.eye(128, dtype=np.float32)
    pvs = [(_fm(f("mix_pre_g"), 8)), _fm(f("ssm_out_g"), 4), _fm(f("fox_out_g"), 4), _fm(f("mix_post_g"), 8), _fm(f("xa_pre_g"), 8),
           _fm(f("mem_g"), 8), _fm(f("xa_post_g"), 8), _fm(f("ffn_pre_g"), 8), _fm(f("ffn_post_g"), 8), _fm(f("ssm_d"), 4), _fm(f("ssm_glu_b"), 4)]
    smp[:, 256:328] = np.concatenate(pvs, axis=1)
    smp[0:8, 328] = f("fox_f_bias")
    smt = np.zeros((128, SMT_N), np.float32)
    smt[:, 0:512] = np.arange(512, dtype=np.float32)[None, :]
    sel = np.zeros((24, 8, 128), np.float32)
    for r in range(24):
        sel[r, r % 8, :] = 1.0
    smt[0:24, 512:1536] = sel.reshape(24, 1024)
    smt[64:88, 512:1536] = sel.reshape(24, 1024)
    wf = f("w_in")[:, 2048:2056]
    smt[:, 1536:1600] = wf.reshape(8, 128, 8).transpose(1, 0, 2).reshape(128, 64)
    o = 1600

    def gl(a):
        a = np.asarray(a, np.float32)
        tail = a.shape[2:]
        a = a.reshape(16, 2, 64, *tail)
        a = np.moveaxis(a, 0, 2)
        return a.reshape(128, 16, *tail)
    smt[:, o:o + 16] = gl(f("ssm_a_re"))
    smt[:, o + 16:o + 32] = gl(f("ssm_a_im"))
    smt[:, o + 32:o + 48] = gl(np.repeat(f("ssm_log_dt")[:, None], 64, axis=1))
    smt[:, o + 48:o + 304] = gl(f("ssm_b_re")).reshape(128, 256)
    smt[:, o + 304:o + 560] = gl(f("ssm_b_im")).reshape(128, 256)
    smt[:, o + 560:o + 816] = gl(np.transpose(f("ssm_c_re"), (0, 2, 1))).reshape(128, 256)
    smt[:, o + 816:o + 1072] = gl(np.transpose(f("ssm_c_im"), (0, 2, 1))).reshape(128, 256)
    return wst, smp, smt


_NC_CACHE = {}


def kernel(**inputs):
    x = np.asarray(inputs["x"], np.float32)
    mem = np.asarray(inputs["mem"], np.float32)
    B = x.shape[0]
    nt = x.shape[1] // TT
    wst, smp, smt = _prep_shared(inputs)
    if nt not in _NC_CACHE:
        _NC_CACHE[nt] = build(nt)
    nc = _NC_CACHE[nt]
    in_maps = []
    for b in range(B):
        in_maps.append({"xT": np.ascontiguousarray(x[b].T), "memT": np.ascontiguousarray(mem[b].T),
                        "wst": wst, "smp": smp, "smt": smt})
    res = run_bass_kernel_spmd(nc, in_maps, core_ids=list(range(B)))
    out = np.stack([np.ascontiguousarray(r["yT"].T) for r in res.results], axis=0)
    return out.astype(np.float32)
```

```python
import numpy as np
from contextlib import ExitStack
import concourse.bass as bass
import concourse.mybir as mybir
from concourse.bass_utils import run_bass_kernel_spmd

F32 = mybir.dt.float32
BF16 = mybir.dt.bfloat16
I32 = mybir.dt.int32
ALU = mybir.AluOpType
AF = mybir.ActivationFunctionType
ENGS = ['pe', 'act', 'dve', 'pool', 'sp']

D = 1024
KC = 8
TT = 512
NT = 8
HC = 22
NM = 256
NBT = 864
NBP = 128
CH = 16
NSLOT = 5
NCH_T = NBT // CH
NCH_P = NBP // CH
EPS = 1e-6
TWO_PI = float(2 * np.pi)
SMP_N = 329 + 64
SMT_N = 2672


class Prog:
    def __init__(self, nc, es):
        self.nc = nc
        self.es = es
        self.ops = {e: [] for e in ENGS}
        self.cnt = {}
        self.semh = {}
        self.lastw = {}
        self.readers = {}
        self.seen = {e: {} for e in ENGS}
        for e in ENGS:
            self.newsem('S_' + e)

    def newsem(self, name):
        if name not in self.semh:
            self.semh[name] = self.es.enter_context(self.nc.semaphore(name))
            self.cnt[name] = 0

    def op(self, eng, fn, reads=(), writes=(), chan=None):
        need = {}

        def add(tok, raw):
            if tok is None:
                return
            sem, val, teng, isdma = tok
            if (not isdma) and teng == eng:
                if eng == 'pe':
                    return
            if need.get(sem, 0) < val:
                need[sem] = val

        for k in reads:
            add(self.lastw.get(k), True)
        for k in writes:
            add(self.lastw.get(k), False)
            for t in self.readers.get(k, {}).values():
                add(t, False)
        waits = []
        for sem, val in need.items():
            if self.seen[eng].get(sem, 0) < val:
                self.seen[eng][sem] = val
                waits.append((sem, val))
        if fn is None:
            self.ops[eng].append((waits, None, None, 0))
            return None
        if chan is not None:
            self.newsem(chan)
            sem, inc, isdma = chan, 16, True
        else:
            sem, inc, isdma = 'S_' + eng, 1, False
        self.cnt[sem] += inc
        tok = (sem, self.cnt[sem], eng, isdma)
        for k in writes:
            self.lastw[k] = tok
            self.readers[k] = {}
        for k in reads:
            self.readers.setdefault(k, {})[sem] = tok
        self.ops[eng].append((waits, fn, sem, inc))
        return tok

    def barrier(self):
        for e in ENGS:
            waits = []
            for sem, val in self.cnt.items():
                if val > 0 and self.seen[e].get(sem, 0) < val and sem != 'S_' + e:
                    self.seen[e][sem] = val
                    waits.append((sem, val))
            self.ops[e].append((waits, None, None, 0))

    def emit(self, block):
        decos = {'pe': block.tensor, 'act': block.scalar, 'dve': block.vector,
                 'pool': block.gpsimd, 'sp': block.sync}
        for e in ENGS:
            ops = self.ops[e]

            def body(engh, ops=ops):
                for waits, fn, sem, inc in ops:
                    for ws, wv in waits:
                        engh.wait_ge(self.semh[ws], wv)
                    if fn is not None:
                        ins = fn(engh)
                        ins.then_inc(self.semh[sem], inc)

            decos[e](body)


def build(nt=NT):
    S_ = nt * TT
    NBLKS = 4 * nt
    nc = bass.Bass("TRN2", target_bir_lowering=False)
    xT = nc.dram_tensor("xT", [D, S_], F32, kind="ExternalInput").ap()
    memT = nc.dram_tensor("memT", [D, NM], F32, kind="ExternalInput").ap()
    wst = nc.dram_tensor("wst", [128, (NBT + NBP) * 128], F32, kind="ExternalInput").ap()
    smp_d = nc.dram_tensor("smp", [128, SMP_N], F32, kind="ExternalInput").ap()
    smt_d = nc.dram_tensor("smt", [128, SMT_N], F32, kind="ExternalInput").ap()
    yT = nc.dram_tensor("yT", [D, S_], F32, kind="ExternalOutput").ap()
    wscr = nc.dram_tensor("wscr", [128, (NBT + NBP) * 128], BF16).ap()
    escr = nc.dram_tensor("escr", [128, 16 * 2 * 512], BF16).ap().rearrange("p (a c b) -> p a c b", c=2, b=512)
    xT_v = xT.rearrange("(kc p) t -> p kc t", p=128)
    yT_v = yT.rearrange("(kc p) t -> p kc t", p=128)
    memT_v = memT.rearrange("(kc p) t -> p kc t", p=128)

    with ExitStack() as es:
        P = Prog(nc, es)
        T = lambda name, shape, dt: es.enter_context(nc.sbuf_tensor(name, shape, dt))
        op = P.op

        smp = T("smp_s", [128, SMP_N], F32)
        negm = smp[:, 0:128]
        identf = smp[:, 128:256]
        pv = smp[:, 256:328]
        fb = smp[:, 328:329]
        g_mixpre, g_ssm, g_fox, g_mixpost = pv[:, 0:8], pv[:, 8:12], pv[:, 12:16], pv[:, 16:24]
        g_xapre, g_mem, g_xapost, g_ffnpre, g_ffnpost = pv[:, 24:32], pv[:, 32:40], pv[:, 40:48], pv[:, 48:56], pv[:, 56:64]
        d_skip, glu_b = pv[:, 64:68], pv[:, 68:72]
        cvec = T("cvec", [128, 8], F32)
        onesb = T("onesb", [128, 128], BF16)
        onesf = T("onesf", [128, 128], F32)
        identb = T("identb", [128, 128], BF16)
        selb = T("selb", [128, 8, 128], BF16)
        wfb = T("wfb", [128, 8, 8], BF16)
        ring = T("ring", [128, NSLOT, CH * 128], BF16)
        KT = T("KT", [128, 4, S_], BF16)
        VC = T("VC", [128, NBLKS, 8, 64], BF16)
        Ebuf = T("Ebuf", [128, 3, 2, 512], BF16)
        s5x = T("s5x", [128, 4, 512], F32)
        Btab = T("Btab", [128, 16, 2, 128], BF16)
        Ctab = T("Ctab", [128, 16, 2, 128], BF16)
        Rr = T("Rr", [128, 16], F32)
        E512 = T("E512", [128, 2, 16], F32)
        sinit = T("sinit", [128, 2, 16], F32)
        zl = T("zl", [128, 2, 16], F32)
        ztmp = T("ztmp", [128, 4, 16], F32)
        KM = T("KM", [128, 8, NM], BF16)
        VM = T("VM", [128, 2, D], BF16)
        sq = T("sq", [128, 2, 512], BF16)
        rstd = T("rstd", [128, 512], F32)
        pT = T("pT", [128, 4, 512], BF16)
        xr = T("xr", [128, 2, 2, 512], BF16)
        negmb = T("negmb", [128, 128], BF16)
        rl = T("rl", [128, 512], F32)
        rlb = T("rlb", [128, 512], F32)
        negF = T("negF", [128, NBLKS, 8], F32)
        biasT = T("biasT", [128, NBLKS, 8], F32)
        frbc = T("frbc", [128, 8], F32)
        fref = T("fref", [8, 1], F32)
        dg8 = T("dg8", [8, 8], F32)
        ft1 = T("ft1", [8, 512], F32)
        fabs_ = T("fabs", [8, 512], F32)
        ga = T("ga", [128, 512], BF16)
        gml = T("gml", [8, 2, 512], BF16)
        arena = T("arena", [128, 13824], F32)
        ps = [es.enter_context(nc.psum_tensor("ps%d" % i, [128, 512], F32)) for i in range(8)]
        PK = lambda b: ('ps', b)

        xs = arena[:, 0:4096].rearrange("p (a b) -> p a b", b=512)
        hid = arena[:, 4096:9728].bitcast(BF16).rearrange("p (a b) -> p a b", b=512)
        SC = arena[:, 4096:9728].rearrange("p (a b) -> p a b", b=512)
        ACT_ = arena[:, 9728:13824].bitcast(BF16).rearrange("p (a b) -> p a b", b=512)
        O3 = arena[:, 9728:13824].rearrange("p (a b) -> p a b", b=512)
        XK = lambda kc: ('xs', kc)
        HK = lambda c: ('H', c)
        SK = lambda i: [('H', 2 * i), ('H', 2 * i + 1)]
        AK = lambda c: ('A', c)
        O3K = lambda i: [('A', 2 * i), ('A', 2 * i + 1)]
        stage = arena[:, 0:4096].rearrange("p (a b) -> p a b", b=2048)
        cbuf = arena[:, 4096:6144].bitcast(BF16).rearrange("p (a b) -> p a b", b=2048)
        smt = arena[:, 6144:6144 + SMT_N]
        pw = arena[:, 8832:13824]
        iota = smt[:, 0:512]
        self_f = smt[:, 512:1536]
        wf_f = smt[:, 1536:1600]
        s5o = 1600
        are, aim, ldt = smt[:, s5o:s5o + 16], smt[:, s5o + 16:s5o + 32], smt[:, s5o + 32:s5o + 48]
        s5v = lambda k: smt[:, s5o + 48 + 256 * k: s5o + 48 + 256 * (k + 1)].rearrange("p (a b) -> p a b", b=16)
        bre_, bim_, cre_, cim_ = s5v(0), s5v(1), s5v(2), s5v(3)

        if NBLKS >= 20:
            lstage = VC[:, 4:20, :, :].rearrange("p a h d -> p (a h d)").bitcast(F32).rearrange("p (a b) -> p a b", b=2048)
        else:
            lstage = T("lstage", [128, 2, 2048], F32)
        class Ring:
            def __init__(self):
                self.seq = [NCH_T + i for i in range(NCH_P)] + [c for _ in range(nt) for c in range(NCH_T)]
                self.next_dma = 0
                self.cur = 0
                self.casted = set()
                self.ncast = 0

            def ensure(self, q_lo):
                lim = min(len(self.seq), q_lo + NSLOT)
                while self.next_dma < lim:
                    q = self.next_dma
                    slot = q % NSLOT
                    dc = self.seq[q]
                    if dc not in self.casted:
                        self.casted.add(dc)
                        b = self.ncast % 2
                        eng = ['act', 'dve'][self.ncast % 2]
                        self.ncast += 1
                        op('sp', lambda e, dc=dc, b=b: e.dma_start(out=lstage[:, b, :], in_=wst[:, dc * 2048:(dc + 1) * 2048]),
                           writes=[('stage', b)], chan='ST%d' % b)
                        if eng == 'act':
                            op('act', lambda e, b=b, slot=slot: e.activation(ring[:, slot, :], lstage[:, b, :], AF.Copy), reads=[('stage', b)], writes=[('ring', slot)])
                        else:
                            op(eng, lambda e, b=b, slot=slot: e.tensor_copy(ring[:, slot, :], lstage[:, b, :]), reads=[('stage', b)], writes=[('ring', slot)])
                        if dc < NCH_T and nt > 1:
                            op('pool', lambda e, dc=dc, slot=slot: e.dma_start(out=wscr[:, dc * 2048:(dc + 1) * 2048], in_=ring[:, slot, :]),
                               reads=[('ring', slot)], writes=[('wscr', dc)], chan='SO%d' % slot)
                    else:
                        op('sp', lambda e, slot=slot, dc=dc: e.dma_start(out=ring[:, slot, :], in_=wscr[:, dc * 2048:(dc + 1) * 2048]),
                           reads=[('wscr', dc)], writes=[('ring', slot)], chan='W%d' % slot)
                    self.next_dma += 1

            def take(self, n, span=0):
                g0 = self.cur
                self.cur += n
                self.ensure(g0 // CH)
                aps, keys = [], set()
                step = span if span else 1
                for g in range(g0, g0 + n, step):
                    q = g // CH
                    slot = q % NSLOT
                    j = g % CH
                    aps.append(ring[:, slot, j * 128:(j + step) * 128])
                    keys.add(('ring', slot))
                    if span:
                        assert (g + step - 1) // CH == q
                return aps, list(keys)

        R = Ring()

        def mm(out, pairs, reads, writes, start=True, stop=True):
            def fn(e):
                n = len(pairs)
                ins = None
                for i, (l, r) in enumerate(pairs):
                    ins = e.matmul(out, l, r, start=(start and i == 0), stop=(stop and i == n - 1))
                return ins
            op('pe', fn, reads, writes)

        op('sp', lambda e: e.dma_start(out=smp[:], in_=smp_d), writes=['smp'], chan='C0')
        op('sp', lambda e: e.dma_start(out=smt, in_=smt_d), writes=['smt'], chan='C1')
        op('dve', lambda e: e.memset(cvec[:], EPS), writes=['cvec'])
        op('dve', lambda e: e.tensor_scalar(cvec[0:8, 1:2], fb[0:8, :], -1.0, None, ALU.mult), reads=['smp', 'cvec'], writes=['cvec'])
        op('dve', lambda e: e.memset(cvec[:, 2:3], 0.25), reads=['cvec'], writes=['cvec'])
        op('dve', lambda e: e.memset(onesb[:], 1.0), writes=['onesb'])
        op('pool', lambda e: e.memset(onesf[:], 1.0), writes=['onesf'])
        op('dve', lambda e: e.tensor_copy(identb[:], identf), reads=['smp'], writes=['identb'])
        op('dve', lambda e: e.tensor_copy(negmb[:], negm), reads=['smp'], writes=['negmb'])
        op('dve', lambda e: e.tensor_copy(selb[:].rearrange("p a b -> p (a b)"), self_f), reads=['smt'], writes=['selb'])
        op('dve', lambda e: e.tensor_copy(wfb[:].rearrange("p a b -> p (a b)"), wf_f), reads=['smt'], writes=['wfb'])
        op('pool', lambda e: e.memset(fref[:], 0.0), writes=['fref'])
        op('pool', lambda e: e.memset(sinit[:], 0.0), writes=['sinit'])
        op('pool', lambda e: e.memset(ga[:], 0.0), writes=['ga'])

        ss = lambda i: pw[:, 16 * i:16 * (i + 1)]
        big = lambda i: pw[:, 1024 + 512 * i: 1024 + 512 * (i + 1)]
        bigi = pw[:, 1024 + 512 * 6: 1024 + 512 * 7].bitcast(I32)
        DT_, LRE, TH, TQ, C1, S1, P1R, P1I, DEN, QRE, QIM, TA, TB, TI_, Y5 = range(15)
        sI = pw[:, 16 * 20:16 * 21].bitcast(I32)

        def dv(fn, r=('pw',), w=('pw',)):
            op('dve', fn, reads=list(r), writes=list(w))

        def frac_reduce(y, tf, ti):
            dv(lambda e: e.tensor_copy(ti, y))
            dv(lambda e: e.tensor_copy(tf, ti))
            dv(lambda e: e.tensor_tensor(y, y, tf, ALU.subtract))
            dv(lambda e: e.tensor_single_scalar(tf, y, 0.5, ALU.is_gt))
            dv(lambda e: e.tensor_tensor(y, y, tf, ALU.subtract))
            dv(lambda e: e.tensor_single_scalar(tf, y, -0.5, ALU.is_lt))
            dv(lambda e: e.tensor_tensor(y, y, tf, ALU.add))

        def act_pw(fn, r=('pw',), w=('pw',)):
            op('act', fn, reads=list(r), writes=list(w))

        act_pw(lambda e: e.activation(ss(DT_), ldt, AF.Exp), r=('smt', 'pw'))
        dv(lambda e: e.tensor_tensor(ss(LRE), are, ss(DT_), ALU.mult), r=('smt', 'pw'))
        dv(lambda e: e.tensor_tensor(ss(TH), aim, ss(DT_), ALU.mult), r=('smt', 'pw'))
        act_pw(lambda e: e.activation(Rr[:], ss(LRE), AF.Exp), w=('Rr', 'pw'))
        dv(lambda e: e.tensor_scalar(ss(TQ), ss(TH), 1.0 / TWO_PI, None, ALU.mult))
        frac_reduce(ss(TQ), ss(TA), sI)
        act_pw(lambda e: e.activation(ss(S1), ss(TQ), AF.Sin, scale=TWO_PI))
        dv(lambda e: e.tensor_scalar(ss(Y5), ss(TQ), 0.25, None, ALU.add))
        frac_reduce(ss(Y5), ss(TA), sI)
        act_pw(lambda e: e.activation(ss(C1), ss(Y5), AF.Sin, scale=TWO_PI))
        dv(lambda e: e.tensor_scalar(ss(Y5), ss(TQ), 512.0, None, ALU.mult))
        frac_reduce(ss(Y5), ss(TA), sI)
        act_pw(lambda e: e.activation(E512[:, 1, :], ss(Y5), AF.Sin, scale=TWO_PI), w=('E512', 'pw'))
        dv(lambda e: e.tensor_scalar(ss(Y5), ss(Y5), 0.25, None, ALU.add))
        frac_reduce(ss(Y5), ss(TA), sI)
        act_pw(lambda e: e.activation(E512[:, 0, :], ss(Y5), AF.Sin, scale=TWO_PI), w=('E512', 'pw'))
        dv(lambda e: e.tensor_tensor(ss(P1R), Rr[:], ss(C1), ALU.mult), r=('Rr', 'pw'))
        dv(lambda e: e.tensor_tensor(ss(P1I), Rr[:], ss(S1), ALU.mult), r=('Rr', 'pw'))
        dv(lambda e: e.tensor_scalar(ss(P1R), ss(P1R), -1.0, None, ALU.add))
        dv(lambda e: e.tensor_tensor(ss(DEN), are, are, ALU.mult), r=('smt', 'pw'))
        dv(lambda e: e.tensor_tensor(ss(TA), aim, aim, ALU.mult), r=('smt', 'pw'))
        dv(lambda e: e.tensor_tensor(ss(DEN), ss(DEN), ss(TA), ALU.add))
        dv(lambda e: e.reciprocal(ss(DEN), ss(DEN)))
        dv(lambda e: e.tensor_tensor(ss(QRE), ss(P1R), are, ALU.mult), r=('smt', 'pw'))
        dv(lambda e: e.tensor_tensor(ss(TA), ss(P1I), aim, ALU.mult), r=('smt', 'pw'))
        dv(lambda e: e.tensor_tensor(ss(QRE), ss(QRE), ss(TA), ALU.add))
        dv(lambda e: e.tensor_tensor(ss(QRE), ss(QRE), ss(DEN), ALU.mult))
        dv(lambda e: e.tensor_tensor(ss(QIM), ss(P1I), are, ALU.mult), r=('smt', 'pw'))
        dv(lambda e: e.tensor_tensor(ss(TA), ss(P1R), aim, ALU.mult), r=('smt', 'pw'))
        dv(lambda e: e.tensor_tensor(ss(QIM), ss(QIM), ss(TA), ALU.subtract))
        dv(lambda e: e.tensor_tensor(ss(QIM), ss(QIM), ss(DEN), ALU.mult))
        bbre = pw[:, 512:768].rearrange("p (a b) -> p a b", b=16)
        bbim = pw[:, 768:1024].rearrange("p (a b) -> p a b", b=16)
        tb16 = pw[:, 400:416]
        BLK = arena[:, 0:2048].bitcast(BF16).rearrange("p (a c b) -> p a c b", c=2, b=128)
        dv(lambda e: e.memset(BLK, 0.0))
        dv(lambda e: e.memset(Ctab[:], 0.0), w=('Ctab',))
        for pt in range(16):
            q = pt % 4
            dv(lambda e, pt=pt: e.tensor_scalar(tb16, bim_[:, pt, :], ss(QIM)[:, pt:pt + 1], None, ALU.mult), r=('smt', 'pw'))
            dv(lambda e, pt=pt: e.scalar_tensor_tensor(bbre[:, pt, :], bre_[:, pt, :], ss(QRE)[:, pt:pt + 1], tb16, ALU.mult, ALU.subtract), r=('smt', 'pw'))
            dv(lambda e, pt=pt: e.tensor_scalar(tb16, bre_[:, pt, :], ss(QIM)[:, pt:pt + 1], None, ALU.mult), r=('smt', 'pw'))
            dv(lambda e, pt=pt: e.scalar_tensor_tensor(bbim[:, pt, :], bim_[:, pt, :], ss(QRE)[:, pt:pt + 1], tb16, ALU.mult, ALU.add), r=('smt', 'pw'))
            for c, src in ((0, bbre), (1, bbim)):
                dv(lambda e, pt=pt, c=c, src=src, q=q: e.tensor_copy(BLK[0:64, pt, c, 32 * q:32 * q + 16], src[0:64, pt, :]))
                dv(lambda e, pt=pt, c=c, src=src, q=q: e.tensor_copy(BLK[64:128, pt, c, 32 * q + 16:32 * q + 32], src[64:128, pt, :]))
            dv(lambda e, pt=pt, q=q: e.tensor_copy(Ctab[0:64, pt, 0, 32 * q:32 * q + 16], cre_[0:64, pt, :]), r=('smt',), w=('Ctab',))
            dv(lambda e, pt=pt, q=q: e.tensor_copy(Ctab[64:128, pt, 0, 32 * q + 16:32 * q + 32], cre_[64:128, pt, :]), r=('smt',), w=('Ctab',))
            dv(lambda e, pt=pt, q=q: e.tensor_scalar(Ctab[0:64, pt, 1, 32 * q:32 * q + 16], cim_[0:64, pt, :], -1.0, None, ALU.mult), r=('smt',), w=('Ctab',))
            dv(lambda e, pt=pt, q=q: e.tensor_scalar(Ctab[64:128, pt, 1, 32 * q + 16:32 * q + 32], cim_[64:128, pt, :], -1.0, None, ALU.mult), r=('smt',), w=('Ctab',))
        for grp in range(8):
            bank = grp % 2
            for k in range(4):
                idx = grp * 4 + k
                pt, c = idx // 2, idx % 2
                mm(ps[bank][:, k * 128:(k + 1) * 128], [(BLK[:, pt, c, :], identb[:])], reads=['pw', 'identb'], writes=[PK(bank)])
            op('act', lambda e, grp=grp, bank=bank: e.activation(
                Btab[:].rearrange("p a c b -> p (a c b)")[:, grp * 512:(grp + 1) * 512], ps[bank][:], AF.Copy),
                reads=[PK(bank)], writes=['Btab'])
        for pt in range(16):
            for which, dst in ((0, 1), (1, 0)):
                if pt % 2 == 0:
                    y = big(which * 3)
                    tf = big(which * 3 + 1)
                    ti = bigi if which == 0 else pw[:, 1024 + 512 * 2: 1024 + 512 * 3].bitcast(I32)
                    key = ('big', which)
                else:
                    pb_ = lambda i: arena[:, 2048 + 512 * i: 2048 + 512 * (i + 1)]
                    y = pb_(which * 3)
                    tf = pb_(which * 3 + 1)
                    ti = pb_(which * 3 + 2).bitcast(I32)
                    key = ('bigp', which)
                if which == 0:
                    op('act', lambda e, y=y, pt=pt: e.activation(y, iota, AF.Copy, scale=ss(TQ)[:, pt:pt + 1]), reads=['smt', 'pw'], writes=[key])
                else:
                    op('act', lambda e, y=y, pt=pt: e.activation(y, iota, AF.Identity, scale=ss(TQ)[:, pt:pt + 1], bias=cvec[:, 2:3]), reads=['smt', 'pw', 'cvec'], writes=[key])
                for fn_ in (
                    lambda e, y=y, tf=tf, ti=ti: e.tensor_copy(ti, y),
                    lambda e, y=y, tf=tf, ti=ti: e.tensor_copy(tf, ti),
                    lambda e, y=y, tf=tf, ti=ti: e.tensor_tensor(y, y, tf, ALU.subtract),
                ):
                    op('dve', fn_, reads=[key], writes=[key])
                op('act', lambda e, y=y, dst=dst, pt=pt: e.activation(Ebuf[:, pt % 2, dst, :], y, AF.Sin, scale=TWO_PI), reads=[key], writes=[('E', pt % 2)])
            op('sp', lambda e, pt=pt: e.dma_start(out=escr[:, pt, :, :], in_=Ebuf[:, pt % 2, :, :]), reads=[('E', pt % 2)], writes=[('escr', pt)], chan='EO%d' % (pt % 2))

        mT = arena[:, 2048:4096].rearrange("p (a b) -> p a b", b=256)
        mnT = arena[:, 4096:5120].bitcast(BF16).rearrange("p (a b) -> p a b", b=256)
        op('sp', lambda e: e.dma_start(out=mT, in_=memT_v), writes=['mT', 'mnT', ('bigp', 0), ('bigp', 1)], chan='C2')
        for kc in range(KC):
            b = kc % 2
            op('act', lambda e, kc=kc, b=b: e.activation(sq[:, b, 0:256], mT[:, kc, :], AF.Square), reads=['mT'], writes=[('sq', b)])
            mm(ps[7][:, 0:256], [(onesb[:], sq[:, b, 0:256])], reads=[('sq', b), 'onesb'], writes=[PK(7)], start=(kc == 0), stop=(kc == KC - 1))
        op('act', lambda e: e.activation(rstd[:, 0:256], ps[7][:, 0:256], AF.Sqrt, bias=cvec[:, 0:1], scale=1.0 / D), reads=[PK(7), 'cvec'], writes=['rstd'])
        op('dve', lambda e: e.reciprocal(rstd[:, 0:256], rstd[:, 0:256]), reads=['rstd'], writes=['rstd'])
        for kc in range(KC):
            op('dve', lambda e, kc=kc: e.scalar_tensor_tensor(mnT[:, kc, :], mT[:, kc, :], g_mem[:, kc:kc + 1], rstd[:, 0:256], ALU.mult, ALU.mult),
               reads=['mT', 'rstd', 'smp'], writes=['mnT'])
        for m in range(8):
            blks, rk = R.take(8)
            bank = m % 2
            mm(ps[bank][:, 0:256], [(blks[kc], mnT[:, kc, :]) for kc in range(KC)], reads=rk + ['mnT'], writes=[PK(bank)])
            op('act', lambda e, m=m, bank=bank: e.activation(KM[:, m, :], ps[bank][:, 0:256], AF.Copy), reads=[PK(bank)], writes=['KM'])
        spans, rk = R.take(64, span=4)
        for mb in range(2):
            for hf in range(2):
                bank = (mb * 2 + hf) % 2
                mm(ps[bank][:], [(mnT[:, kc, mb * 128:(mb + 1) * 128], spans[kc * 2 + hf]) for kc in range(KC)], reads=rk + ['mnT'], writes=[PK(bank)])
                op('dve', lambda e, mb=mb, hf=hf, bank=bank: e.tensor_copy(VM[:, mb, hf * 512:(hf + 1) * 512], ps[bank][:]), reads=[PK(bank)], writes=['VM'])
        P.barrier()

        def rms_rstd(srcs, nfeat):
            n = len(srcs)
            for i, (ap, keys) in enumerate(srcs):
                b = i % 2
                op('act', lambda e, ap=ap, b=b: e.activation(sq[:, b, :], ap, AF.Square), reads=keys, writes=[('sq', b)])
                mm(ps[7][:], [(onesb[:], sq[:, b, :])], reads=[('sq', b)], writes=[PK(7)], start=(i == 0), stop=(i == n - 1))
            op('act', lambda e: e.activation(rstd[:], ps[7][:], AF.Ln, bias=cvec[:, 0:1], scale=1.0 / nfeat), reads=[PK(7)], writes=['rstd'])
            op('act', lambda e: e.activation(rstd[:], rstd[:], AF.Exp, scale=-0.5), reads=['rstd'], writes=['rstd'])

        def pre_norm(gain, dst_base):
            rms_rstd([(xs[:, kc, :], [XK(kc)]) for kc in range(KC)], D)
            for kc in range(KC):
                eng = 'dve'
                op(eng, lambda e, kc=kc: e.scalar_tensor_tensor(ACT_[:, dst_base + kc, :], xs[:, kc, :], gain[:, kc:kc + 1], rstd[:], ALU.mult, ALU.mult),
                   reads=[XK(kc), 'rstd'], writes=[AK(dst_base + kc)])

        def post_norm_residual(osrc, okeys, gain, final=False):
            rms_rstd([(osrc(kc), okeys(kc)) for kc in range(KC)], D)
            for kc in range(KC):
                op('dve', lambda e, kc=kc: e.scalar_tensor_tensor(osrc(kc), osrc(kc), gain[:, kc:kc + 1], rstd[:], ALU.mult, ALU.mult),
                   reads=okeys(kc) + ['rstd'], writes=okeys(kc))
                if final:
                    op('pool', lambda e, kc=kc: e.tensor_tensor(osrc(kc), xs[:, kc, :], osrc(kc), ALU.add),
                       reads=okeys(kc) + [XK(kc)], writes=okeys(kc))
                else:
                    op('pool', lambda e, kc=kc: e.tensor_tensor(xs[:, kc, :], xs[:, kc, :], osrc(kc), ALU.add),
                       reads=okeys(kc) + [XK(kc)], writes=[XK(kc)])

        def proj8(src_base, evac):
            for m in range(8):
                blks, rk = R.take(8)
                bank = m % 2
                mm(ps[bank][:], [(blks[kc], ACT_[:, src_base + kc, :]) for kc in range(KC)],
                   reads=rk + [AK(src_base + kc) for kc in range(KC)], writes=[PK(bank)])
                evac(m, bank)

        def evac_copy(dst, dkeys, bank, i, scale=None):
            if i % 2 == 0:
                if scale is None:
                    op('act', lambda e: e.activation(dst, ps[bank][:], AF.Copy), reads=[PK(bank)], writes=dkeys)
                else:
                    op('act', lambda e: e.activation(dst, ps[bank][:], AF.Copy, scale=scale), reads=[PK(bank)], writes=dkeys)
            else:
                if scale is None:
                    op('dve', lambda e: e.tensor_copy(dst, ps[bank][:]), reads=[PK(bank)], writes=dkeys)
                else:
                    op('dve', lambda e: e.tensor_scalar(dst, ps[bank][:], scale, None, ALU.mult), reads=[PK(bank)], writes=dkeys)

        for ti in range(nt):
            T0 = ti * TT
            op('pool', lambda e, T0=T0: e.dma_start(out=xs, in_=xT_v[:, :, T0:T0 + TT]), writes=[XK(kc) for kc in range(KC)], chan='XL')
            pre_norm(g_mixpre, 0)
            hkeys = [AK(kc) for kc in range(KC)]
            for m in range(12):
                blks, rk = R.take(8)
                bank = m % 2
                mm(ps[bank][:], [(blks[kc], ACT_[:, kc, :]) for kc in range(KC)], reads=rk + hkeys, writes=[PK(bank)])
                if m < 4:
                    evac_copy(ACT_[:, 8 + m, :], [AK(8 + m)], bank, m)
                elif m < 8:
                    evac_copy(ACT_[:, 8 + m, :], [AK(8 + m)], bank, m, scale=0.125)
                else:
                    evac_copy(KT[:, m - 8, T0:T0 + TT], [('KT', m - 8, ti)], bank, m)

            def load_E(pt):
                op('sp', lambda e, pt=pt: e.dma_start(out=Ebuf[:, pt % 3, :, :], in_=escr[:, pt, :, :]), reads=[('escr', pt)], writes=[('E', pt % 3)], chan='EL%d' % (pt % 3))

            def s5A(pt):
                ut = pt // 4
                uk = [AK(8 + ut)]
                mm(ps[0][:], [(Btab[:, pt, 0, :], ACT_[:, 8 + ut, :])], reads=uk, writes=[PK(0)])
                mm(ps[1][:], [(Btab[:, pt, 1, :], ACT_[:, 8 + ut, :])], reads=uk, writes=[PK(1)])

            def s5set(pt):
                par = pt % 2
                st = pt % 3
                if st < 2:
                    S = [SC[:, 4 * st + i, :] for i in range(4)]
                    K_ = [SK(4 * st + i) for i in range(4)]
                else:
                    S = [s5x[:, i, :] for i in range(4)]
                    K_ = [[('s5x', i)] for i in range(4)]
                return (par, S, K_, Ebuf[:, st, 0, :], Ebuf[:, st, 1, :], [('E', st)])

            def s5B1(pt):
                par, S, K, c_, s_, EK = s5set(pt)
                op('dve', lambda e: e.tensor_tensor(S[0], ps[0][:], c_, ALU.mult), reads=[PK(0)] + EK, writes=K[0])
                op('dve', lambda e: e.tensor_tensor(S[1], ps[1][:], s_, ALU.mult), reads=[PK(1)] + EK, writes=K[1])
                op('dve', lambda e: e.tensor_tensor(S[2], ps[1][:], c_, ALU.mult), reads=[PK(1)] + EK, writes=K[2])
                op('dve', lambda e: e.tensor_tensor(S[3], ps[0][:], s_, ALU.mult), reads=[PK(0)] + EK, writes=K[3])

            def s5B2(pt):
                par, S, K, c_, s_, EK = s5set(pt)
                op('pool', lambda e: e.tensor_tensor(S[0], S[0], S[1], ALU.add), reads=K[0] + K[1], writes=K[0])
                op('pool', lambda e: e.tensor_tensor(S[2], S[2], S[3], ALU.subtract), reads=K[2] + K[3], writes=K[2])

            def s5B3(pt):
                par, S, K, c_, s_, EK = s5set(pt)
                op('dve', lambda e: e.tensor_tensor_scan(S[1], Rr[:, pt:pt + 1].to_broadcast([128, 512]), S[0], sinit[:, 0, pt:pt + 1], ALU.mult, ALU.add),
                   reads=K[0] + ['sinit'], writes=K[1])
                op('dve', lambda e: e.tensor_tensor_scan(S[3], Rr[:, pt:pt + 1].to_broadcast([128, 512]), S[2], sinit[:, 1, pt:pt + 1], ALU.mult, ALU.add),
                   reads=K[2] + ['sinit'], writes=K[3])
                op('dve', lambda e: e.tensor_copy(zl[:, 0, pt:pt + 1], S[1][:, 511:512]), reads=K[1], writes=['zl'])
                op('dve', lambda e: e.tensor_copy(zl[:, 1, pt:pt + 1], S[3][:, 511:512]), reads=K[3], writes=['zl'])

            def s5C1(pt):
                par, S, K, c_, s_, EK = s5set(pt)
                op('pool', lambda e: e.tensor_tensor(S[0], S[1], c_, ALU.mult), reads=K[1] + EK, writes=K[0])
                op('pool', lambda e: e.tensor_tensor(S[2], S[3], s_, ALU.mult), reads=K[3] + EK, writes=K[2])

            def s5C1b(pt):
                par, S, K, c_, s_, EK = s5set(pt)
                op('pool', lambda e: e.tensor_tensor(S[3], S[3], c_, ALU.mult), reads=K[3] + EK, writes=K[3])
                op('pool', lambda e: e.tensor_tensor(S[1], S[1], s_, ALU.mult), reads=K[1] + EK, writes=K[1])
                if pt + 3 < 16:
                    load_E(pt + 3)

            def s5C2a(pt):
                par, S, K, c_, s_, EK = s5set(pt)
                op('dve', lambda e: e.tensor_tensor(xr[:, par, 0, :], S[0], S[2], ALU.subtract), reads=K[0] + K[2], writes=[('xr', par, 0)])

            def s5C2b(pt):
                par, S, K, c_, s_, EK = s5set(pt)
                op('dve', lambda e: e.tensor_tensor(xr[:, par, 1, :], S[3], S[1], ALU.add), reads=K[3] + K[1], writes=[('xr', par, 1)])

            def s5D(pt):
                par = pt % 2
                ut = pt // 4
                mm(ps[2][:], [(Ctab[:, pt, 0, :], xr[:, par, 0, :]), (Ctab[:, pt, 1, :], xr[:, par, 1, :])],
                   reads=[('xr', par, 0), ('xr', par, 1)], writes=[PK(2)], start=(pt % 4 == 0), stop=(pt % 4 == 3))
                if pt % 4 == 3:
                    Y = rstd[:]
                    W = sq[:].rearrange("p a b -> p (a b)").bitcast(F32)
                    YK, WK = ['rstd'], [('sq', 0), ('sq', 1)]
                    op('dve', lambda e: e.scalar_tensor_tensor(Y, ACT_[:, 8 + ut, :], d_skip[:, ut:ut + 1], ps[2][:], ALU.mult, ALU.add),
                       reads=[PK(2), AK(8 + ut)], writes=YK)
                    op('dve', lambda e: e.tensor_tensor(W, Y, Y, ALU.mult), reads=YK, writes=WK)
                    op('dve', lambda e: e.tensor_scalar(W, W, 0.044715, 1.0, ALU.mult, ALU.add), reads=WK, writes=WK)
                    op('dve', lambda e: e.tensor_tensor(W, W, Y, ALU.mult), reads=WK + YK, writes=WK)
                    gelq.append([1,
                                 lambda: op('act', lambda e: e.activation(W, W, AF.Tanh, scale=float(np.sqrt(2.0 / np.pi))), reads=WK, writes=WK),
                                 lambda: op('dve', lambda e: e.scalar_tensor_tensor(W, W, 1.0, Y, ALU.add, ALU.mult), reads=WK + YK, writes=WK),
                                 lambda ut=ut: op('act', lambda e: e.activation(ACT_[:, ut, :], W, AF.Copy, scale=0.5), reads=WK, writes=[AK(ut)])])

            gelq = []

            def gelu_act():
                for g in gelq:
                    if g[0] == 1:
                        g[1]()
                        g[0] = 2
                    elif g[0] == 3:
                        g[3]()
                        g[0] = 4

            def gelu_dve():
                for g in gelq:
                    if g[0] == 2:
                        g[2]()
                        g[0] = 3

            nkb = 4 * ti + 4
            YFt = [SC[:, 8, :], SC[:, 9, :], SC[:, 10, :], arena[:, 9728 + 1024:9728 + 1536]]
            YFk = [SK(8), SK(9), SK(10), [AK(4), AK(5)]]
            fox_items = []
            rot = [0, 0]
            for m_ in range(4):
                hA, hB = 2 * m_, 2 * m_ + 1
                qk = AK(12 + m_)

                def stageA(kb, m_=m_, hA=hA, hB=hB, qk=qk):
                    sa = 3 + (rot[0] % 3)
                    sb = 3 + ((rot[0] + 1) % 3)
                    rot[0] += 2
                    ra = rot[1] % 4
                    rb = (rot[1] + 1) % 4
                    rot[1] += 2
                    intile = kb >= 4 * ti
                    c0 = 128 * (kb - 4 * ti) if intile else 0

                    def fn(e):
                        e.matmul(ps[sa][:, c0:512], KT[0:64, m_, kb * 128:(kb + 1) * 128], ACT_[0:64, 12 + m_, c0:512], start=True, stop=False)
                        e.matmul(ps[sb][:, c0:512], KT[64:128, m_, kb * 128:(kb + 1) * 128], ACT_[64:128, 12 + m_, c0:512], start=True, stop=False)
                        e.matmul(ps[sa][:, c0:512], selb[0:24, hA, :], ga[0:24, c0:512], start=False, stop=not intile)
                        i2 = e.matmul(ps[sb][:, c0:512], selb[64:88, hB, :], ga[64:88, c0:512], start=False, stop=not intile)
                        if intile:
                            e.matmul(ps[sa][:, c0:c0 + 128], identb[:], negmb[:], start=False, stop=True)
                            i2 = e.matmul(ps[sb][:, c0:c0 + 128], identb[:], negmb[:], start=False, stop=True)
                        return i2
                    op('pe', fn, reads=[('KT', m_, kb // 4), qk, 'ga'], writes=[PK(sa), PK(sb)])
                    for (sx, rx, hx) in ((sa, ra, hA), (sb, rb, hB)):
                        op('act', lambda e, sx=sx, rx=rx, hx=hx: e.activation(pT[:, rx, c0:512], ps[sx][:, c0:512], AF.Exp, bias=biasT[:, kb, hx:hx + 1], scale=1.0),
                           reads=[PK(sx), 'biasT'], writes=[('pT', rx)])
                    return ra, rb, c0

                def stageB(kb, st, hA=hA, hB=hB):
                    ra, rb, c0 = st
                    first, last = (kb == 0), (kb == nkb - 1)

                    def fn(e):
                        e.matmul(ps[6][0:64, c0:512], VC[:, kb, hA, :], pT[:, ra, c0:512], start=first, stop=last, tile_position=(0, 0))
                        e.matmul(ps[6][64:128, c0:512], VC[:, kb, hB, :], pT[:, rb, c0:512], start=first, stop=last, tile_position=(0, 64))
                        e.matmul(ps[7][0:64, c0:512], onesb[:, 0:64], pT[:, ra, c0:512], start=first, stop=last, tile_position=(0, 0))
                        return e.matmul(ps[7][64:128, c0:512], onesb[:, 64:128], pT[:, rb, c0:512], start=first, stop=last, tile_position=(0, 64))
                    op('pe', fn, reads=[('pT', ra), ('pT', rb), ('VC', kb)], writes=[PK(6), PK(7)])

                def fin(m_=m_):
                    op('act', lambda e: e.activation(rl[:], ps[7][:], AF.Copy), reads=[PK(7)], writes=['rl'])
                    op('act', lambda e: e.activation(rlb[:], ps[6][:], AF.Copy), reads=[PK(6)], writes=['rlb'])
                    op('dve', lambda e: e.reciprocal(rl[:], rl[:]), reads=['rl'], writes=['rl'])
                    op('dve', lambda e: e.tensor_tensor(YFt[m_], rlb[:], rl[:], ALU.mult), reads=['rlb', 'rl'], writes=YFk[m_])

                state = {}

                def item_first(stageA=stageA, state=state):
                    state[0] = stageA(0)

                def item_mid(kb, stageA=stageA, stageB=stageB, state=state):
                    if kb + 1 < nkb:
                        state[kb + 1] = stageA(kb + 1)
                    stageB(kb, state[kb])

                fox_items.append(item_first)
                if m_ > 0:
                    fox_items.append(prev_fin[0])
                for kb in range(nkb):
                    fox_items.append(lambda kb=kb, item_mid=item_mid: item_mid(kb))
                prev_fin = [fin]
            fox_items.append(prev_fin[0])

            def s5_slot(k):
                if ok(k):
                    s5A(k)
                    s5B1(k)
                    s5B2(k)
                if ok(k - 2):
                    s5C2a(k - 2)
                    s5C2b(k - 2)
                if ok(k - 1):
                    s5B3(k - 1)
                    s5C1(k - 1)
                    s5C1b(k - 1)
                gelu_dve()
                if ok(k - 3):
                    s5D(k - 3)

            ok = lambda p: 0 <= p < 16
            load_E(0)
            load_E(1)
            load_E(2)
            spans, rk = R.take(32, span=4)

            def vproj(tb):
                bank = 3 + tb % 2
                mm(ps[bank][:], [(ACT_[:, kc, tb * 128:(tb + 1) * 128], spans[kc]) for kc in range(KC)], reads=rk + hkeys, writes=[PK(bank)])
                blk = 4 * ti + tb
                op('dve' if tb % 2 else 'act',
                   (lambda e, blk=blk, bank=bank: e.tensor_copy(VC[:, blk, :, :], ps[bank][:].rearrange("p (h d) -> p h d", d=64))) if tb % 2 else
                   (lambda e, blk=blk, bank=bank: e.activation(VC[:, blk, :, :], ps[bank][:].rearrange("p (h d) -> p h d", d=64), AF.Copy)),
                   reads=[PK(bank)], writes=[('VC', blk)] + ([('stage', 0), ('stage', 1)] if 4 <= blk < 20 else []))
            s5_slot(0)
            vproj(0)
            vproj(1)
            s5_slot(1)
            vproj(2)
            vproj(3)
            s5_slot(2)
            mm(ps[5][0:8, :], [(wfb[:, kc, :], ACT_[:, kc, :]) for kc in range(KC)], reads=hkeys + ['wfb'], writes=[PK(5)])
            op('act', lambda e: e.activation(ft1[:], ps[5][0:8, :], AF.Exp, bias=cvec[0:8, 1:2], scale=-1.0), reads=[PK(5)], writes=['ft1'])
            op('act', lambda e: e.activation(ft1[:], ft1[:], AF.Ln, bias=1.0, scale=1.0), reads=['ft1'], writes=['ft1'])
            op('dve', lambda e: e.tensor_tensor_scan(fabs_[:], onesf[0:8, 0:1].to_broadcast([8, 512]), ft1[:], fref[:, 0:1], ALU.mult, ALU.subtract),
               reads=['ft1', 'fref'], writes=['fabs'])
            op('dve', lambda e: e.tensor_scalar(ft1[:], fabs_[:], fref[:, 0:1], None, ALU.subtract), reads=['fabs', 'fref'], writes=['ft1'])
            op('dve', lambda e: e.tensor_copy(ga[0:8, :], ft1[:]), reads=['ft1'], writes=['ga'])
            op('dve', lambda e: e.tensor_copy(ga[64:72, :], ft1[:]), reads=['ft1'], writes=['ga'])
            op('dve', lambda e: e.tensor_tensor(ft1[:], ft1[:], ga[0:8, :], ALU.subtract), reads=['ft1', 'ga'], writes=['ft1'])
            op('dve', lambda e: e.tensor_copy(gml[:, 0, :], ft1[:]), reads=['ft1'], writes=['gml'])
            op('dve', lambda e: e.tensor_tensor(ft1[:], ft1[:], gml[:, 0, :], ALU.subtract), reads=['ft1', 'gml'], writes=['ft1'])
            op('dve', lambda e: e.tensor_copy(gml[:, 1, :], ft1[:]), reads=['ft1'], writes=['gml'])
            op('pool', lambda e: e.dma_start(out=ga[8:16, :], in_=gml[:, 0, :]), reads=['gml'], writes=['ga'], chan='GA0')
            op('pool', lambda e: e.dma_start(out=ga[16:24, :], in_=gml[:, 1, :]), reads=['gml'], writes=['ga'], chan='GA1')
            op('pool', lambda e: e.dma_start(out=ga[72:80, :], in_=gml[:, 0, :]), reads=['gml'], writes=['ga'], chan='GA2')
            op('pool', lambda e: e.dma_start(out=ga[80:88, :], in_=gml[:, 1, :]), reads=['gml'], writes=['ga'], chan='GA3')
            for jb in range(4):
                mm(ps[4][:, jb * 8:(jb + 1) * 8], [(fabs_[:, jb * 128:(jb + 1) * 128], identf[0:8, 0:8])], reads=['fabs'], writes=[PK(4)])
            op('dve', lambda e, ti=ti: e.tensor_scalar(negF[:, 4 * ti:4 * ti + 4, :], ps[4][:, 0:32].rearrange("p (a b) -> p a b", b=8), -1.0, None, ALU.mult),
               reads=[PK(4)], writes=['negF'])
            op('dve', lambda e: e.tensor_scalar(dg8[:], identf[0:8, 0:8], fref[:, 0:1], None, ALU.mult), reads=['fref'], writes=['dg8'])
            mm(ps[4][:, 64:72], [(onesf[0:8, 0:128], dg8[:])], reads=['dg8'], writes=[PK(4)])
            op('dve', lambda e: e.tensor_copy(frbc[:], ps[4][:, 64:72]), reads=[PK(4)], writes=['frbc'])
            for kb in range(4 * ti + 4):
                op('dve', lambda e, kb=kb: e.tensor_tensor(biasT[:, kb, :], negF[:, kb, :], frbc[:], ALU.add), reads=['negF', 'frbc'], writes=['biasT'])
            op('dve', lambda e: e.tensor_copy(fref[:], fabs_[:, 511:512]), reads=['fabs'], writes=['fref'])
            NPRE = 3
            NSL = 20
            per = -(-len(fox_items) // (NSL - NPRE))
            fi = 0
            for k in range(NPRE, NSL):
                s5_slot(k)
                for _ in range(per):
                    if fi < len(fox_items):
                        fox_items[fi]()
                        fi += 1
                gelu_act()
            while fi < len(fox_items):
                fox_items[fi]()
                fi += 1
            for _ in range(3):
                gelu_dve()
                gelu_act()
            assert all(g[0] == 4 for g in gelq)
            c5, s5 = E512[:, 0, :], E512[:, 1, :]
            op('dve', lambda e: e.tensor_tensor(ztmp[:, 0, :], zl[:, 0, :], c5, ALU.mult), reads=['zl'], writes=['ztmp'])
            op('dve', lambda e: e.tensor_tensor(ztmp[:, 1, :], zl[:, 1, :], s5, ALU.mult), reads=['zl'], writes=['ztmp'])
            op('dve', lambda e: e.tensor_tensor(ztmp[:, 2, :], zl[:, 1, :], c5, ALU.mult), reads=['zl'], writes=['ztmp'])
            op('dve', lambda e: e.tensor_tensor(ztmp[:, 3, :], zl[:, 0, :], s5, ALU.mult), reads=['zl'], writes=['ztmp'])
            op('dve', lambda e: e.tensor_tensor(sinit[:, 0, :], ztmp[:, 0, :], ztmp[:, 1, :], ALU.subtract), reads=['ztmp'], writes=['sinit'])
            op('dve', lambda e: e.tensor_tensor(sinit[:, 1, :], ztmp[:, 2, :], ztmp[:, 3, :], ALU.add), reads=['ztmp'], writes=['sinit'])
            for m in range(4):
                blks, rk = R.take(4)
                bank = m % 2
                mm(ps[bank][:], [(blks[kc], ACT_[:, kc, :]) for kc in range(4)], reads=rk + [AK(kc) for kc in range(4)], writes=[PK(bank)])
                op('act', lambda e, m=m, bank=bank: e.activation(SC[:, 4, :], ps[bank][:], AF.Sigmoid, bias=glu_b[:, m:m + 1], scale=1.0), reads=[PK(bank)], writes=SK(4))
                op('dve', lambda e, m=m: e.tensor_tensor(SC[:, m, :], ACT_[:, m, :], SC[:, 4, :], ALU.mult), reads=SK(4) + [AK(m)], writes=SK(m))
            rms_rstd([(SC[:, m, :], SK(m)) for m in range(4)], 512)
            for m in range(4):
                op('dve', lambda e, m=m: e.scalar_tensor_tensor(ACT_[:, m, :], SC[:, m, :], g_ssm[:, m:m + 1], rstd[:], ALU.mult, ALU.mult),
                   reads=SK(m) + ['rstd'], writes=[AK(m)])
            rms_rstd([(YFt[m], YFk[m]) for m in range(4)], 512)
            for m in (3, 0, 1, 2):
                op('dve', lambda e, m=m: e.scalar_tensor_tensor(ACT_[:, 4 + m, :], YFt[m], g_fox[:, m:m + 1], rstd[:], ALU.mult, ALU.mult),
                   reads=YFk[m] + ['rstd'], writes=[AK(4 + m)])

            proj8(0, lambda m, bank: evac_copy(SC[:, m, :], SK(m), bank, m))
            post_norm_residual(lambda kc: SC[:, kc, :], SK, g_mixpost)

            pre_norm(g_xapre, 8)
            proj8(8, lambda m, bank: evac_copy(ACT_[:, m, :], [AK(m)], bank, m, scale=1.0 / 16.0))
            for hx in range(4):
                c0_, c1_ = 2 * hx, 2 * hx + 1
                par = hx % 2
                ob = (4, 5, 6) if par == 0 else (0, 1, 7)
                rlt, rlk = (rl, 'rl') if par == 0 else (rlb, 'rlb')
                for mb in range(2):
                    sb = 2 + mb
                    pi = 2 * par + mb
                    mm(ps[sb][:], [(KM[:, c0_, mb * 128:(mb + 1) * 128], ACT_[:, c0_, :]), (KM[:, c1_, mb * 128:(mb + 1) * 128], ACT_[:, c1_, :])],
                       reads=[AK(c0_), AK(c1_)], writes=[PK(sb)])
                    op('act', lambda e, sb=sb, pi=pi: e.activation(pT[:, pi, :], ps[sb][:], AF.Exp), reads=[PK(sb)], writes=[('pT', pi)])
                for mb in range(2):
                    pi = 2 * par + mb
                    for dc in range(2):
                        mm(ps[ob[dc]][:], [(VM[:, mb, (2 * hx + dc) * 128:(2 * hx + dc + 1) * 128], pT[:, pi, :])], reads=[('pT', pi)], writes=[PK(ob[dc])],
                           start=(mb == 0), stop=(mb == 1))
                    mm(ps[ob[2]][:], [(onesb[:], pT[:, pi, :])], reads=[('pT', pi)], writes=[PK(ob[2])], start=(mb == 0), stop=(mb == 1))
                op('dve', lambda e, rlt=rlt, ob=ob: e.reciprocal(rlt[:], ps[ob[2]][:]), reads=[PK(ob[2])], writes=[rlk])
                for dc in range(2):
                    op('dve', lambda e, dc=dc, hx=hx, rlt=rlt, ob=ob: e.tensor_tensor(ACT_[:, 8 + 2 * hx + dc, :], ps[ob[dc]][:], rlt[:], ALU.mult),
                       reads=[PK(ob[dc]), rlk], writes=[AK(8 + 2 * hx + dc)])
            proj8(8, lambda m, bank: evac_copy(SC[:, m, :], SK(m), bank, m))
            post_norm_residual(lambda kc: SC[:, kc, :], SK, g_xapost)

            pre_norm(g_ffnpre, 0)
            for m in range(HC):
                blks, rk = R.take(16)
                par = m % 2
                bg, bu = 2 + 2 * par, 3 + 2 * par
                mm(ps[bg][:], [(blks[kc], ACT_[:, kc, :]) for kc in range(KC)], reads=rk + hkeys, writes=[PK(bg)])
                mm(ps[bu][:], [(blks[8 + kc], ACT_[:, kc, :]) for kc in range(KC)], reads=rk + hkeys, writes=[PK(bu)])
                tq_, tk_ = (rl, 'rl') if par == 0 else (rlb, 'rlb')
                op('act', lambda e, tq_=tq_, bg=bg: e.activation(tq_[:], ps[bg][:], AF.Silu), reads=[PK(bg)], writes=[tk_])
                op('dve', lambda e, tq_=tq_, bu=bu, m=m: e.tensor_tensor(hid[:, m, :], tq_[:], ps[bu][:], ALU.mult),
                   reads=[PK(bu), tk_], writes=[HK(m)])
            for m in range(8):
                blks, rk = R.take(HC)
                bank = m % 2
                mm(ps[bank][:], [(blks[kc], hid[:, kc, :]) for kc in range(HC)], reads=rk + [HK(kc) for kc in range(HC)], writes=[PK(bank)])
                evac_copy(O3[:, m, :], O3K(m), bank, m)
            post_norm_residual(lambda kc: O3[:, kc, :], O3K, g_ffnpost, final=True)
            op('sp', lambda e, T0=T0: e.dma_start(out=yT_v[:, :, T0:T0 + TT], in_=O3), reads=[AK(c) for c in range(16)], writes=[('yT', ti)], chan='XO')

        op('sp', None, reads=[('yT', ti) for ti in range(nt)])
        with nc.Block() as block:
            P.emit(block)
    return nc


def _blk(W, kc, m):
    return W[kc * 128:(kc + 1) * 128, m * 128:(m + 1) * 128]


def _weight_stream(w_in, glu_w, w_out, xa_wq, xa_wo, w_gate, w_up, w_down, xa_wkv):
    blocks = []
    wi = w_in[:, :1536]
    for m in range(12):
        for kc in range(8):
            blocks.append(_blk(wi, kc, m))
    wv = w_in[:, 1536:2048]
    for kc in range(8):
        for j in range(4):
            blocks.append(_blk(wv, kc, j))
    for m in range(4):
        for kc in range(4):
            blocks.append(_blk(glu_w, kc, m))
    for W in (w_out, xa_wq, xa_wo):
        for m in range(8):
            for kc in range(8):
                blocks.append(_blk(W, kc, m))
    for m in range(HC):
        for kc in range(8):
            blocks.append(_blk(w_gate, kc, m))
        for kc in range(8):
            blocks.append(_blk(w_up, kc, m))
    for m in range(8):
        for kc in range(HC):
            blocks.append(_blk(w_down, kc, m))
    assert len(blocks) == NBT
    wk, wvv = xa_wkv[:, :1024], xa_wkv[:, 1024:]
    for m in range(8):
        for kc in range(8):
            blocks.append(_blk(wk, kc, m))
    for kc in range(8):
        for j in range(8):
            blocks.append(_blk(wvv, kc, j))
    assert len(blocks) == NBT + NBP
    return np.ascontiguousarray(np.concatenate(blocks, axis=1), dtype=np.float32)


def _fm(v, n):
    return np.asarray(v, np.float32).reshape(n, 128).T


def _prep_shared(inp):
    f = lambda k: np.asarray(inp[k], np.float32)
    wst = _weight_stream(f("w_in"), f("ssm_glu_w"), f("w_out"), f("xa_wq"), f("xa_wo"), f("w_gate"), f("w_up"), f("w_down"), f("xa_wkv"))
    smp = np.zeros((128, SMP_N), np.float32)
    s_idx = np.arange(128)[:, None]
    t_idx = np.arange(128)[None, :]
    smp[:, 0:128] = np.where(s_idx <= t_idx, 0.0, -30000.0)
    smp[:, 128:256] = np.eye(128, dtype=np.float32)
    pvs = [(_fm(f("mix_pre_g"), 8)), _fm(f("ssm_out_g"), 4), _fm(f("fox_out_g"), 4), _fm(f("mix_post_g"), 8), _fm(f("xa_pre_g"), 8),
           _fm(f("mem_g"), 8), _fm(f("xa_post_g"), 8), _fm(f("ffn_pre_g"), 8), _fm(f("ffn_post_g"), 8), _fm(f("ssm_d"), 4), _fm(f("ssm_glu_b"), 4)]
    smp[:, 256:328] = np.concatenate(pvs, axis=1)
    smp[0:8, 328] = f("fox_f_bias")
    smt = np.zeros((128, SMT_N), np.float32)
    smt[:, 0:512] = np.arange(512, dtype=np.float32)[None, :]
    sel = np.zeros((24, 8, 128), np.float32)
    for r in range(24):
        sel[r, r % 8, :] = 1.0
    smt[0:24, 512:1536] = sel.reshape(24, 1024)
    smt[64:88, 512:1536] = sel.reshape(24, 1024)
    wf = f("w_in")[:, 2048:2056]
    smt[:, 1536:1600] = wf.reshape(8, 128, 8).transpose(1, 0, 2).reshape(128, 64)
    o = 1600

    def gl(a):
        a = np.asarray(a, np.float32)
        tail = a.shape[2:]
        a = a.reshape(16, 2, 64, *tail)
        a = np.moveaxis(a, 0, 2)
        return a.reshape(128, 16, *tail)
    smt[:, o:o + 16] = gl(f("ssm_a_re"))
    smt[:, o + 16:o + 32] = gl(f("ssm_a_im"))
    smt[:, o + 32:o + 48] = gl(np.repeat(f("ssm_log_dt")[:, None], 64, axis=1))
    smt[:, o + 48:o + 304] = gl(f("ssm_b_re")).reshape(128, 256)
    smt[:, o + 304:o + 560] = gl(f("ssm_b_im")).reshape(128, 256)
    smt[:, o + 560:o + 816] = gl(np.transpose(f("ssm_c_re"), (0, 2, 1))).reshape(128, 256)
    smt[:, o + 816:o + 1072] = gl(np.transpose(f("ssm_c_im"), (0, 2, 1))).reshape(128, 256)
    return wst, smp, smt


_NC_CACHE = {}


def kernel(**inputs):
    x = np.asarray(inputs["x"], np.float32)
    mem = np.asarray(inputs["mem"], np.float32)
    B = x.shape[0]
    nt = x.shape[1] // TT
    wst, smp, smt = _prep_shared(inputs)
    if nt not in _NC_CACHE:
        _NC_CACHE[nt] = build(nt)
    nc = _NC_CACHE[nt]
    in_maps = []
    for b in range(B):
        in_maps.append({"xT": np.ascontiguousarray(x[b].T), "memT": np.ascontiguousarray(mem[b].T),
                        "wst": wst, "smp": smp, "smt": smt})
    res = run_bass_kernel_spmd(nc, in_maps, core_ids=list(range(B)))
    out = np.stack([np.ascontiguousarray(r["yT"].T) for r in res.results], axis=0)
    return out.astype(np.float32)
```

```python
import numpy as np
from contextlib import ExitStack
import concourse.bass as bass
import concourse.mybir as mybir
from concourse.bass_utils import run_bass_kernel_spmd

F32 = mybir.dt.float32
BF16 = mybir.dt.bfloat16
I32 = mybir.dt.int32
ALU = mybir.AluOpType
AF = mybir.ActivationFunctionType
ENGS = ['pe', 'act', 'dve', 'pool', 'sp']

D = 1024
KC = 8
TT = 512
NT = 8
HC = 22
NM = 256
NBT = 864
NBP = 128
CH = 16
NSLOT = 5
NCH_T = NBT // CH
NCH_P = NBP // CH
EPS = 1e-6
TWO_PI = float(2 * np.pi)
SMP_N = 329 + 64
SMT_N = 2672


class Prog:
    def __init__(self, nc, es):
        self.nc = nc
        self.es = es
        self.ops = {e: [] for e in ENGS}
        self.cnt = {}
        self.semh = {}
        self.lastw = {}
        self.readers = {}
        self.seen = {e: {} for e in ENGS}
        for e in ENGS:
            self.newsem('S_' + e)

    def newsem(self, name):
        if name not in self.semh:
            self.semh[name] = self.es.enter_context(self.nc.semaphore(name))
            self.cnt[name] = 0

    def op(self, eng, fn, reads=(), writes=(), chan=None):
        need = {}

        def add(tok, raw):
            if tok is None:
                return
            sem, val, teng, isdma = tok
            if (not isdma) and teng == eng:
                if eng == 'pe':
                    return
            if need.get(sem, 0) < val:
                need[sem] = val

        for k in reads:
            add(self.lastw.get(k), True)
        for k in writes:
            add(self.lastw.get(k), False)
            for t in self.readers.get(k, {}).values():
                add(t, False)
        waits = []
        for sem, val in need.items():
            if self.seen[eng].get(sem, 0) < val:
                self.seen[eng][sem] = val
                waits.append((sem, val))
        if fn is None:
            self.ops[eng].append((waits, None, None, 0))
            return None
        if chan is not None:
            self.newsem(chan)
            sem, inc, isdma = chan, 16, True
        else:
            sem, inc, isdma = 'S_' + eng, 1, False
        self.cnt[sem] += inc
        tok = (sem, self.cnt[sem], eng, isdma)
        for k in writes:
            self.lastw[k] = tok
            self.readers[k] = {}
        for k in reads:
            self.readers.setdefault(k, {})[sem] = tok
        self.ops[eng].append((waits, fn, sem, inc))
        return tok

    def barrier(self):
        for e in ENGS:
            waits = []
            for sem, val in self.cnt.items():
                if val > 0 and self.seen[e].get(sem, 0) < val and sem != 'S_' + e:
                    self.seen[e][sem] = val
                    waits.append((sem, val))
            self.ops[e].append((waits, None, None, 0))

    def emit(self, block):
        decos = {'pe': block.tensor, 'act': block.scalar, 'dve': block.vector,
                 'pool': block.gpsimd, 'sp': block.sync}
        for e in ENGS:
            ops = self.ops[e]

            def body(engh, ops=ops):
                for waits, fn, sem, inc in ops:
                    for ws, wv in waits:
                        engh.wait_ge(self.semh[ws], wv)
                    if fn is not None:
                        ins = fn(engh)
                        ins.then_inc(self.semh[sem], inc)

            decos[e](body)


def build(nt=NT):
    S_ = nt * TT
    NBLKS = 4 * nt
    nc = bass.Bass("TRN2", target_bir_lowering=False)
    xT = nc.dram_tensor("xT", [D, S_], F32, kind="ExternalInput").ap()
    memT = nc.dram_tensor("memT", [D, NM], F32, kind="ExternalInput").ap()
    wst = nc.dram_tensor("wst", [128, (NBT + NBP) * 128], F32, kind="ExternalInput").ap()
    smp_d = nc.dram_tensor("smp", [128, SMP_N], F32, kind="ExternalInput").ap()
    smt_d = nc.dram_tensor("smt", [128, SMT_N], F32, kind="ExternalInput").ap()
    yT = nc.dram_tensor("yT", [D, S_], F32, kind="ExternalOutput").ap()
    wscr = nc.dram_tensor("wscr", [128, (NBT + NBP) * 128], BF16).ap()
    escr = nc.dram_tensor("escr", [128, 16 * 2 * 512], BF16).ap().rearrange("p (a c b) -> p a c b", c=2, b=512)
    xT_v = xT.rearrange("(kc p) t -> p kc t", p=128)
    yT_v = yT.rearrange("(kc p) t -> p kc t", p=128)
    memT_v = memT.rearrange("(kc p) t -> p kc t", p=128)

    with ExitStack() as es:
        P = Prog(nc, es)
        T = lambda name, shape, dt: es.enter_context(nc.sbuf_tensor(name, shape, dt))
        op = P.op

        smp = T("smp_s", [128, SMP_N], F32)
        negm = smp[:, 0:128]
        identf = smp[:, 128:256]
        pv = smp[:, 256:328]
        fb = smp[:, 328:329]
        g_mixpre, g_ssm, g_fox, g_mixpost = pv[:, 0:8], pv[:, 8:12], pv[:, 12:16], pv[:, 16:24]
        g_xapre, g_mem, g_xapost, g_ffnpre, g_ffnpost = pv[:, 24:32], pv[:, 32:40], pv[:, 40:48], pv[:, 48:56], pv[:, 56:64]
        d_skip, glu_b = pv[:, 64:68], pv[:, 68:72]
        cvec = T("cvec", [128, 8], F32)
        onesb = T("onesb", [128, 128], BF16)
        onesf = T("onesf", [128, 128], F32)
        identb = T("identb", [128, 128], BF16)
        selb = T("selb", [128, 8, 128], BF16)
        wfb = T("wfb", [128, 8, 8], BF16)
        ring = T("ring", [128, NSLOT, CH * 128], BF16)
        KT = T("KT", [128, 4, S_], BF16)
        VC = T("VC", [128, NBLKS, 8, 64], BF16)
        Ebuf = T("Ebuf", [128, 3, 2, 512], BF16)
        s5x = T("s5x", [128, 4, 512], F32)
        Btab = T("Btab", [128, 16, 2, 128], BF16)
        Ctab = T("Ctab", [128, 16, 2, 128], BF16)
        Rr = T("Rr", [128, 16], F32)
        E512 = T("E512", [128, 2, 16], F32)
        sinit = T("sinit", [128, 2, 16], F32)
        zl = T("zl", [128, 2, 16], F32)
        ztmp = T("ztmp", [128, 4, 16], F32)
        KM = T("KM", [128, 8, NM], BF16)
        VM = T("VM", [128, 2, D], BF16)
        sq = T("sq", [128, 2, 512], BF16)
        rstd = T("rstd", [128, 512], F32)
        pT = T("pT", [128, 4, 512], BF16)
        xr = T("xr", [128, 2, 2, 512], BF16)
        negmb = T("negmb", [128, 128], BF16)
        rl = T("rl", [128, 512], F32)
        rlb = T("rlb", [128, 512], F32)
        negF = T("negF", [128, NBLKS, 8], F32)
        biasT = T("biasT", [128, NBLKS, 8], F32)
        frbc = T("frbc", [128, 8], F32)
        fref = T("fref", [8, 1], F32)
        dg8 = T("dg8", [8, 8], F32)
        ft1 = T("ft1", [8, 512], F32)
        fabs_ = T("fabs", [8, 512], F32)
        ga = T("ga", [128, 512], BF16)
        gml = T("gml", [8, 2, 512], BF16)
        arena = T("arena", [128, 13824], F32)
        ps = [es.enter_context(nc.psum_tensor("ps%d" % i, [128, 512], F32)) for i in range(8)]
        PK = lambda b: ('ps', b)

        xs = arena[:, 0:4096].rearrange("p (a b) -> p a b", b=512)
        hid = arena[:, 4096:9728].bitcast(BF16).rearrange("p (a b) -> p a b", b=512)
        SC = arena[:, 4096:9728].rearrange("p (a b) -> p a b", b=512)
        ACT_ = arena[:, 9728:13824].bitcast(BF16).rearrange("p (a b) -> p a b", b=512)
        O3 = arena[:, 9728:13824].rearrange("p (a b) -> p a b", b=512)
        XK = lambda kc: ('xs', kc)
        HK = lambda c: ('H', c)
        SK = lambda i: [('H', 2 * i), ('H', 2 * i + 1)]
        AK = lambda c: ('A', c)
        O3K = lambda i: [('A', 2 * i), ('A', 2 * i + 1)]
        stage = arena[:, 0:4096].rearrange("p (a b) -> p a b", b=2048)
        cbuf = arena[:, 4096:6144].bitcast(BF16).rearrange("p (a b) -> p a b", b=2048)
        smt = arena[:, 6144:6144 + SMT_N]
        pw = arena[:, 8832:13824]
        iota = smt[:, 0:512]
        self_f = smt[:, 512:1536]
        wf_f = smt[:, 1536:1600]
        s5o = 1600
        are, aim, ldt = smt[:, s5o:s5o + 16], smt[:, s5o + 16:s5o + 32], smt[:, s5o + 32:s5o + 48]
        s5v = lambda k: smt[:, s5o + 48 + 256 * k: s5o + 48 + 256 * (k + 1)].rearrange("p (a b) -> p a b", b=16)
        bre_, bim_, cre_, cim_ = s5v(0), s5v(1), s5v(2), s5v(3)

        if NBLKS >= 20:
            lstage = VC[:, 4:20, :, :].rearrange("p a h d -> p (a h d)").bitcast(F32).rearrange("p (a b) -> p a b", b=2048)
        else:
            lstage = T("lstage", [128, 2, 2048], F32)
        class Ring:
            def __init__(self):
                self.seq = [NCH_T + i for i in range(NCH_P)] + [c for _ in range(nt) for c in range(NCH_T)]
                self.next_dma = 0
                self.cur = 0
                self.casted = set()
                self.ncast = 0

            def ensure(self, q_lo):
                lim = min(len(self.seq), q_lo + NSLOT)
                while self.next_dma < lim:
                    q = self.next_dma
                    slot = q % NSLOT
                    dc = self.seq[q]
                    if dc not in self.casted:
                        self.casted.add(dc)
                        b = self.ncast % 2
                        eng = ['act', 'dve'][self.ncast % 2]
                        self.ncast += 1
                        op('sp', lambda e, dc=dc, b=b: e.dma_start(out=lstage[:, b, :], in_=wst[:, dc * 2048:(dc + 1) * 2048]),
                           writes=[('stage', b)], chan='ST%d' % b)
                        if eng == 'act':
                            op('act', lambda e, b=b, slot=slot: e.activation(ring[:, slot, :], lstage[:, b, :], AF.Copy), reads=[('stage', b)], writes=[('ring', slot)])
                        else:
                            op(eng, lambda e, b=b, slot=slot: e.tensor_copy(ring[:, slot, :], lstage[:, b, :]), reads=[('stage', b)], writes=[('ring', slot)])
                        if dc < NCH_T and nt > 1:
                            op('pool', lambda e, dc=dc, slot=slot: e.dma_start(out=wscr[:, dc * 2048:(dc + 1) * 2048], in_=ring[:, slot, :]),
                               reads=[('ring', slot)], writes=[('wscr', dc)], chan='SO%d' % slot)
                    else:
                        op('sp', lambda e, slot=slot, dc=dc: e.dma_start(out=ring[:, slot, :], in_=wscr[:, dc * 2048:(dc + 1) * 2048]),
                           reads=[('wscr', dc)], writes=[('ring', slot)], chan='W%d' % slot)
                    self.next_dma += 1

            def take(self, n, span=0):
                g0 = self.cur
                self.cur += n
                self.ensure(g0 // CH)
                aps, keys = [], set()
                step = span if span else 1
                for g in range(g0, g0 + n, step):
                    q = g // CH
                    slot = q % NSLOT
                    j = g % CH
                    aps.append(ring[:, slot, j * 128:(j + step) * 128])
                    keys.add(('ring', slot))
                    if span:
                        assert (g + step - 1) // CH == q
                return aps, list(keys)

        R = Ring()

        def mm(out, pairs, reads, writes, start=True, stop=True):
            def fn(e):
                n = len(pairs)
                ins = None
                for i, (l, r) in enumerate(pairs):
                    ins = e.matmul(out, l, r, start=(start and i == 0), stop=(stop and i == n - 1))
                return ins
            op('pe', fn, reads, writes)

        op('sp', lambda e: e.dma_start(out=smp[:], in_=smp_d), writes=['smp'], chan='C0')
        op('sp', lambda e: e.dma_start(out=smt, in_=smt_d), writes=['smt'], chan='C1')
        op('dve', lambda e: e.memset(cvec[:], EPS), writes=['cvec'])
        op('dve', lambda e: e.tensor_scalar(cvec[0:8, 1:2], fb[0:8, :], -1.0, None, ALU.mult), reads=['smp', 'cvec'], writes=['cvec'])
        op('dve', lambda e: e.memset(cvec[:, 2:3], 0.25), reads=['cvec'], writes=['cvec'])
        op('dve', lambda e: e.memset(onesb[:], 1.0), writes=['onesb'])
        op('pool', lambda e: e.memset(onesf[:], 1.0), writes=['onesf'])
        op('dve', lambda e: e.tensor_copy(identb[:], identf), reads=['smp'], writes=['identb'])
        op('dve', lambda e: e.tensor_copy(negmb[:], negm), reads=['smp'], writes=['negmb'])
        op('dve', lambda e: e.tensor_copy(selb[:].rearrange("p a b -> p (a b)"), self_f), reads=['smt'], writes=['selb'])
        op('dve', lambda e: e.tensor_copy(wfb[:].rearrange("p a b -> p (a b)"), wf_f), reads=['smt'], writes=['wfb'])
        op('pool', lambda e: e.memset(fref[:], 0.0), writes=['fref'])
        op('pool', lambda e: e.memset(sinit[:], 0.0), writes=['sinit'])
        op('pool', lambda e: e.memset(ga[:], 0.0), writes=['ga'])

        ss = lambda i: pw[:, 16 * i:16 * (i + 1)]
        big = lambda i: pw[:, 1024 + 512 * i: 1024 + 512 * (i + 1)]
        bigi = pw[:, 1024 + 512 * 6: 1024 + 512 * 7].bitcast(I32)
        DT_, LRE, TH, TQ, C1, S1, P1R, P1I, DEN, QRE, QIM, TA, TB, TI_, Y5 = range(15)
        sI = pw[:, 16 * 20:16 * 21].bitcast(I32)

        def dv(fn, r=('pw',), w=('pw',)):
            op('dve', fn, reads=list(r), writes=list(w))

        def frac_reduce(y, tf, ti):
            dv(lambda e: e.tensor_copy(ti, y))
            dv(lambda e: e.tensor_copy(tf, ti))
            dv(lambda e: e.tensor_tensor(y, y, tf, ALU.subtract))
            dv(lambda e: e.tensor_single_scalar(tf, y, 0.5, ALU.is_gt))
            dv(lambda e: e.tensor_tensor(y, y, tf, ALU.subtract))
            dv(lambda e: e.tensor_single_scalar(tf, y, -0.5, ALU.is_lt))
            dv(lambda e: e.tensor_tensor(y, y, tf, ALU.add))

        def act_pw(fn, r=('pw',), w=('pw',)):
            op('act', fn, reads=list(r), writes=list(w))

        act_pw(lambda e: e.activation(ss(DT_), ldt, AF.Exp), r=('smt', 'pw'))
        dv(lambda e: e.tensor_tensor(ss(LRE), are, ss(DT_), ALU.mult), r=('smt', 'pw'))
        dv(lambda e: e.tensor_tensor(ss(TH), aim, ss(DT_), ALU.mult), r=('smt', 'pw'))
        act_pw(lambda e: e.activation(Rr[:], ss(LRE), AF.Exp), w=('Rr', 'pw'))
        dv(lambda e: e.tensor_scalar(ss(TQ), ss(TH), 1.0 / TWO_PI, None, ALU.mult))
        frac_reduce(ss(TQ), ss(TA), sI)
        act_pw(lambda e: e.activation(ss(S1), ss(TQ), AF.Sin, scale=TWO_PI))
        dv(lambda e: e.tensor_scalar(ss(Y5), ss(TQ), 0.25, None, ALU.add))
        frac_reduce(ss(Y5), ss(TA), sI)
        act_pw(lambda e: e.activation(ss(C1), ss(Y5), AF.Sin, scale=TWO_PI))
        dv(lambda e: e.tensor_scalar(ss(Y5), ss(TQ), 512.0, None, ALU.mult))
        frac_reduce(ss(Y5), ss(TA), sI)
        act_pw(lambda e: e.activation(E512[:, 1, :], ss(Y5), AF.Sin, scale=TWO_PI), w=('E512', 'pw'))
        dv(lambda e: e.tensor_scalar(ss(Y5), ss(Y5), 0.25, None, ALU.add))
        frac_reduce(ss(Y5), ss(TA), sI)
        act_pw(lambda e: e.activation(E512[:, 0, :], ss(Y5), AF.Sin, scale=TWO_PI), w=('E512', 'pw'))
        dv(lambda e: e.tensor_tensor(ss(P1R), Rr[:], ss(C1), ALU.mult), r=('Rr', 'pw'))
        dv(lambda e: e.tensor_tensor(ss(P1I), Rr[:], ss(S1), ALU.mult), r=('Rr', 'pw'))
        dv(lambda e: e.tensor_scalar(ss(P1R), ss(P1R), -1.0, None, ALU.add))
        dv(lambda e: e.tensor_tensor(ss(DEN), are, are, ALU.mult), r=('smt', 'pw'))
        dv(lambda e: e.tensor_tensor(ss(TA), aim, aim, ALU.mult), r=('smt', 'pw'))
        dv(lambda e: e.tensor_tensor(ss(DEN), ss(DEN), ss(TA), ALU.add))
        dv(lambda e: e.reciprocal(ss(DEN), ss(DEN)))
        dv(lambda e: e.tensor_tensor(ss(QRE), ss(P1R), are, ALU.mult), r=('smt', 'pw'))
        dv(lambda e: e.tensor_tensor(ss(TA), ss(P1I), aim, ALU.mult), r=('smt', 'pw'))
        dv(lambda e: e.tensor_tensor(ss(QRE), ss(QRE), ss(TA), ALU.add))
        dv(lambda e: e.tensor_tensor(ss(QRE), ss(QRE), ss(DEN), ALU.mult))
        dv(lambda e: e.tensor_tensor(ss(QIM), ss(P1I), are, ALU.mult), r=('smt', 'pw'))
        dv(lambda e: e.tensor_tensor(ss(TA), ss(P1R), aim, ALU.mult), r=('smt', 'pw'))
        dv(lambda e: e.tensor_tensor(ss(QIM), ss(QIM), ss(TA), ALU.subtract))
        dv(lambda e: e.tensor_tensor(ss(QIM), ss(QIM), ss(DEN), ALU.mult))
        bbre = pw[:, 512:768].rearrange("p (a b) -> p a b", b=16)
        bbim = pw[:, 768:1024].rearrange("p (a b) -> p a b", b=16)
        tb16 = pw[:, 400:416]
        BLK = arena[:, 0:2048].bitcast(BF16).rearrange("p (a c b) -> p a c b", c=2, b=128)
        dv(lambda e: e.memset(BLK, 0.0))
        dv(lambda e: e.memset(Ctab[:], 0.0), w=('Ctab',))
        for pt in range(16):
            q = pt % 4
            dv(lambda e, pt=pt: e.tensor_scalar(tb16, bim_[:, pt, :], ss(QIM)[:, pt:pt + 1], None, ALU.mult), r=('smt', 'pw'))
            dv(lambda e, pt=pt: e.scalar_tensor_tensor(bbre[:, pt, :], bre_[:, pt, :], ss(QRE)[:, pt:pt + 1], tb16, ALU.mult, ALU.subtract), r=('smt', 'pw'))
            dv(lambda e, pt=pt: e.tensor_scalar(tb16, bre_[:, pt, :], ss(QIM)[:, pt:pt + 1], None, ALU.mult), r=('smt', 'pw'))
            dv(lambda e, pt=pt: e.scalar_tensor_tensor(bbim[:, pt, :], bim_[:, pt, :], ss(QRE)[:, pt:pt + 1], tb16, ALU.mult, ALU.add), r=('smt', 'pw'))
            for c, src in ((0, bbre), (1, bbim)):
                dv(lambda e, pt=pt, c=c, src=src, q=q: e.tensor_copy(BLK[0:64, pt, c, 32 * q:32 * q + 16], src[0:64, pt, :]))
                dv(lambda e, pt=pt, c=c, src=src, q=q: e.tensor_copy(BLK[64:128, pt, c, 32 * q + 16:32 * q + 32], src[64:128, pt, :]))
            dv(lambda e, pt=pt, q=q: e.tensor_copy(Ctab[0:64, pt, 0, 32 * q:32 * q + 16], cre_[0:64, pt, :]), r=('smt',), w=('Ctab',))
            dv(lambda e, pt=pt, q=q: e.tensor_copy(Ctab[64:128, pt, 0, 32 * q + 16:32 * q + 32], cre_[64:128, pt, :]), r=('smt',), w=('Ctab',))
            dv(lambda e, pt=pt, q=q: e.tensor_scalar(Ctab[0:64, pt, 1, 32 * q:32 * q + 16], cim_[0:64, pt, :], -1.0, None, ALU.mult), r=('smt',), w=('Ctab',))
            dv(lambda e, pt=pt, q=q: e.tensor_scalar(Ctab[64:128, pt, 1, 32 * q + 16:32 * q + 32], cim_[64:128, pt, :], -1.0, None, ALU.mult), r=('smt',), w=('Ctab',))
        for grp in range(8):
            bank = grp % 2
            for k in range(4):
                idx = grp * 4 + k
                pt, c = idx // 2, idx % 2
                mm(ps[bank][:, k * 128:(k + 1) * 128], [(BLK[:, pt, c, :], identb[:])], reads=['pw', 'identb'], writes=[PK(bank)])
            op('act', lambda e, grp=grp, bank=bank: e.activation(
                Btab[:].rearrange("p a c b -> p (a c b)")[:, grp * 512:(grp + 1) * 512], ps[bank][:], AF.Copy),
                reads=[PK(bank)], writes=['Btab'])
        for pt in range(16):
            for which, dst in ((0, 1), (1, 0)):
                if pt % 2 == 0:
                    y = big(which * 3)
                    tf = big(which * 3 + 1)
                    ti = bigi if which == 0 else pw[:, 1024 + 512 * 2: 1024 + 512 * 3].bitcast(I32)
                    key = ('big', which)
                else:
                    pb_ = lambda i: arena[:, 2048 + 512 * i: 2048 + 512 * (i + 1)]
                    y = pb_(which * 3)
                    tf = pb_(which * 3 + 1)
                    ti = pb_(which * 3 + 2).bitcast(I32)
                    key = ('bigp', which)
                if which == 0:
                    op('act', lambda e, y=y, pt=pt: e.activation(y, iota, AF.Copy, scale=ss(TQ)[:, pt:pt + 1]), reads=['smt', 'pw'], writes=[key])
                else:
                    op('act', lambda e, y=y, pt=pt: e.activation(y, iota, AF.Identity, scale=ss(TQ)[:, pt:pt + 1], bias=cvec[:, 2:3]), reads=['smt', 'pw', 'cvec'], writes=[key])
                for fn_ in (
                    lambda e, y=y, tf=tf, ti=ti: e.tensor_copy(ti, y),
                    lambda e, y=y, tf=tf, ti=ti: e.tensor_copy(tf, ti),
                    lambda e, y=y, tf=tf, ti=ti: e.tensor_tensor(y, y, tf, ALU.subtract),
                ):
                    op('dve', fn_, reads=[key], writes=[key])
                op('act', lambda e, y=y, dst=dst, pt=pt: e.activation(Ebuf[:, pt % 2, dst, :], y, AF.Sin, scale=TWO_PI), reads=[key], writes=[('E', pt % 2)])
            op('sp', lambda e, pt=pt: e.dma_start(out=escr[:, pt, :, :], in_=Ebuf[:, pt % 2, :, :]), reads=[('E', pt % 2)], writes=[('escr', pt)], chan='EO%d' % (pt % 2))

        mT = arena[:, 2048:4096].rearrange("p (a b) -> p a b", b=256)
        mnT = arena[:, 4096:5120].bitcast(BF16).rearrange("p (a b) -> p a b", b=256)
        op('sp', lambda e: e.dma_start(out=mT, in_=memT_v), writes=['mT', 'mnT', ('bigp', 0), ('bigp', 1)], chan='C2')
        for kc in range(KC):
            b = kc % 2
            op('act', lambda e, kc=kc, b=b: e.activation(sq[:, b, 0:256], mT[:, kc, :], AF.Square), reads=['mT'], writes=[('sq', b)])
            mm(ps[7][:, 0:256], [(onesb[:], sq[:, b, 0:256])], reads=[('sq', b), 'onesb'], writes=[PK(7)], start=(kc == 0), stop=(kc == KC - 1))
        op('act', lambda e: e.activation(rstd[:, 0:256], ps[7][:, 0:256], AF.Sqrt, bias=cvec[:, 0:1], scale=1.0 / D), reads=[PK(7), 'cvec'], writes=['rstd'])
        op('dve', lambda e: e.reciprocal(rstd[:, 0:256], rstd[:, 0:256]), reads=['rstd'], writes=['rstd'])
        for kc in range(KC):
            op('dve', lambda e, kc=kc: e.scalar_tensor_tensor(mnT[:, kc, :], mT[:, kc, :], g_mem[:, kc:kc + 1], rstd[:, 0:256], ALU.mult, ALU.mult),
               reads=['mT', 'rstd', 'smp'], writes=['mnT'])
        for m in range(8):
            blks, rk = R.take(8)
            bank = m % 2
            mm(ps[bank][:, 0:256], [(blks[kc], mnT[:, kc, :]) for kc in range(KC)], reads=rk + ['mnT'], writes=[PK(bank)])
            op('act', lambda e, m=m, bank=bank: e.activation(KM[:, m, :], ps[bank][:, 0:256], AF.Copy), reads=[PK(bank)], writes=['KM'])
        spans, rk = R.take(64, span=4)
        for mb in range(2):
            for hf in range(2):
                bank = (mb * 2 + hf) % 2
                mm(ps[bank][:], [(mnT[:, kc, mb * 128:(mb + 1) * 128], spans[kc * 2 + hf]) for kc in range(KC)], reads=rk + ['mnT'], writes=[PK(bank)])
                op('dve', lambda e, mb=mb, hf=hf, bank=bank: e.tensor_copy(VM[:, mb, hf * 512:(hf + 1) * 512], ps[bank][:]), reads=[PK(bank)], writes=['VM'])
        P.barrier()

        def rms_rstd(srcs, nfeat):
            n = len(srcs)
            for i, (ap, keys) in enumerate(srcs):
                b = i % 2
                op('act', lambda e, ap=ap, b=b: e.activation(sq[:, b, :], ap, AF.Square), reads=keys, writes=[('sq', b)])
                mm(ps[7][:], [(onesb[:], sq[:, b, :])], reads=[('sq', b)], writes=[PK(7)], start=(i == 0), stop=(i == n - 1))
            op('act', lambda e: e.activation(rstd[:], ps[7][:], AF.Ln, bias=cvec[:, 0:1], scale=1.0 / nfeat), reads=[PK(7)], writes=['rstd'])
            op('act', lambda e: e.activation(rstd[:], rstd[:], AF.Exp, scale=-0.5), reads=['rstd'], writes=['rstd'])

        def pre_norm(gain, dst_base):
            rms_rstd([(xs[:, kc, :], [XK(kc)]) for kc in range(KC)], D)
            for kc in range(KC):
                eng = 'dve'
                op(eng, lambda e, kc=kc: e.scalar_tensor_tensor(ACT_[:, dst_base + kc, :], xs[:, kc, :], gain[:, kc:kc + 1], rstd[:], ALU.mult, ALU.mult),
                   reads=[XK(kc), 'rstd'], writes=[AK(dst_base + kc)])

        def post_norm_residual(osrc, okeys, gain, final=False):
            rms_rstd([(osrc(kc), okeys(kc)) for kc in range(KC)], D)
            for kc in range(KC):
                op('dve', lambda e, kc=kc: e.scalar_tensor_tensor(osrc(kc), osrc(kc), gain[:, kc:kc + 1], rstd[:], ALU.mult, ALU.mult),
                   reads=okeys(kc) + ['rstd'], writes=okeys(kc))
                if final:
                    op('pool', lambda e, kc=kc: e.tensor_tensor(osrc(kc), xs[:, kc, :], osrc(kc), ALU.add),
                       reads=okeys(kc) + [XK(kc)], writes=okeys(kc))
                else:
                    op('pool', lambda e, kc=kc: e.tensor_tensor(xs[:, kc, :], xs[:, kc, :], osrc(kc), ALU.add),
                       reads=okeys(kc) + [XK(kc)], writes=[XK(kc)])

        def proj8(src_base, evac):
            for m in range(8):
                blks, rk = R.take(8)
                bank = m % 2
                mm(ps[bank][:], [(blks[kc], ACT_[:, src_base + kc, :]) for kc in range(KC)],
                   reads=rk + [AK(src_base + kc) for kc in range(KC)], writes=[PK(bank)])
                evac(m, bank)

        def evac_copy(dst, dkeys, bank, i, scale=None):
            if i % 2 == 0:
                if scale is None:
                    op('act', lambda e: e.activation(dst, ps[bank][:], AF.Copy), reads=[PK(bank)], writes=dkeys)
                else:
                    op('act', lambda e: e.activation(dst, ps[bank][:], AF.Copy, scale=scale), reads=[PK(bank)], writes=dkeys)
            else:
                if scale is None:
                    op('dve', lambda e: e.tensor_copy(dst, ps[bank][:]), reads=[PK(bank)], writes=dkeys)
                else:
                    op('dve', lambda e: e.tensor_scalar(dst, ps[bank][:], scale, None, ALU.mult), reads=[PK(bank)], writes=dkeys)

        for ti in range(nt):
            T0 = ti * TT
            op('pool', lambda e, T0=T0: e.dma_start(out=xs, in_=xT_v[:, :, T0:T0 + TT]), writes=[XK(kc) for kc in range(KC)], chan='XL')
            pre_norm(g_mixpre, 0)
            hkeys = [AK(kc) for kc in range(KC)]
            for m in range(12):
                blks, rk = R.take(8)
                bank = m % 2
                mm(ps[bank][:], [(blks[kc], ACT_[:, kc, :]) for kc in range(KC)], reads=rk + hkeys, writes=[PK(bank)])
                if m < 4:
                    evac_copy(ACT_[:, 8 + m, :], [AK(8 + m)], bank, m)
                elif m < 8:
                    evac_copy(ACT_[:, 8 + m, :], [AK(8 + m)], bank, m, scale=0.125)
                else:
                    evac_copy(KT[:, m - 8, T0:T0 + TT], [('KT', m - 8, ti)], bank, m)

            def load_E(pt):
                op('sp', lambda e, pt=pt: e.dma_start(out=Ebuf[:, pt % 3, :, :], in_=escr[:, pt, :, :]), reads=[('escr', pt)], writes=[('E', pt % 3)], chan='EL%d' % (pt % 3))

            def s5A(pt):
                ut = pt // 4
                uk = [AK(8 + ut)]
                mm(ps[0][:], [(Btab[:, pt, 0, :], ACT_[:, 8 + ut, :])], reads=uk, writes=[PK(0)])
                mm(ps[1][:], [(Btab[:, pt, 1, :], ACT_[:, 8 + ut, :])], reads=uk, writes=[PK(1)])

            def s5set(pt):
                par = pt % 2
                st = pt % 3
                if st < 2:
                    S = [SC[:, 4 * st + i, :] for i in range(4)]
                    K_ = [SK(4 * st + i) for i in range(4)]
                else:
                    S = [s5x[:, i, :] for i in range(4)]
                    K_ = [[('s5x', i)] for i in range(4)]
                return (par, S, K_, Ebuf[:, st, 0, :], Ebuf[:, st, 1, :], [('E', st)])

            def s5B1(pt):
                par, S, K, c_, s_, EK = s5set(pt)
                op('dve', lambda e: e.tensor_tensor(S[0], ps[0][:], c_, ALU.mult), reads=[PK(0)] + EK, writes=K[0])
                op('dve', lambda e: e.tensor_tensor(S[1], ps[1][:], s_, ALU.mult), reads=[PK(1)] + EK, writes=K[1])
                op('dve', lambda e: e.tensor_tensor(S[2], ps[1][:], c_, ALU.mult), reads=[PK(1)] + EK, writes=K[2])
                op('dve', lambda e: e.tensor_tensor(S[3], ps[0][:], s_, ALU.mult), reads=[PK(0)] + EK, writes=K[3])

            def s5B2(pt):
                par, S, K, c_, s_, EK = s5set(pt)
                op('pool', lambda e: e.tensor_tensor(S[0], S[0], S[1], ALU.add), reads=K[0] + K[1], writes=K[0])
                op('pool', lambda e: e.tensor_tensor(S[2], S[2], S[3], ALU.subtract), reads=K[2] + K[3], writes=K[2])

            def s5B3(pt):
                par, S, K, c_, s_, EK = s5set(pt)
                op('dve', lambda e: e.tensor_tensor_scan(S[1], Rr[:, pt:pt + 1].to_broadcast([128, 512]), S[0], sinit[:, 0, pt:pt + 1], ALU.mult, ALU.add),
                   reads=K[0] + ['sinit'], writes=K[1])
                op('dve', lambda e: e.tensor_tensor_scan(S[3], Rr[:, pt:pt + 1].to_broadcast([128, 512]), S[2], sinit[:, 1, pt:pt + 1], ALU.mult, ALU.add),
                   reads=K[2] + ['sinit'], writes=K[3])
                op('dve', lambda e: e.tensor_copy(zl[:, 0, pt:pt + 1], S[1][:, 511:512]), reads=K[1], writes=['zl'])
                op('dve', lambda e: e.tensor_copy(zl[:, 1, pt:pt + 1], S[3][:, 511:512]), reads=K[3], writes=['zl'])

            def s5C1(pt):
                par, S, K, c_, s_, EK = s5set(pt)
                op('pool', lambda e: e.tensor_tensor(S[0], S[1], c_, ALU.mult), reads=K[1] + EK, writes=K[0])
                op('pool', lambda e: e.tensor_tensor(S[2], S[3], s_, ALU.mult), reads=K[3] + EK, writes=K[2])

            def s5C1b(pt):
                par, S, K, c_, s_, EK = s5set(pt)
                op('pool', lambda e: e.tensor_tensor(S[3], S[3], c_, ALU.mult), reads=K[3] + EK, writes=K[3])
                op('pool', lambda e: e.tensor_tensor(S[1], S[1], s_, ALU.mult), reads=K[1] + EK, writes=K[1])
                if pt + 3 < 16:
                    load_E(pt + 3)

            def s5C2a(pt):
                par, S, K, c_, s_, EK = s5set(pt)
                op('dve', lambda e: e.tensor_tensor(xr[:, par, 0, :], S[0], S[2], ALU.subtract), reads=K[0] + K[2], writes=[('xr', par, 0)])

            def s5C2b(pt):
                par, S, K, c_, s_, EK = s5set(pt)
                op('dve', lambda e: e.tensor_tensor(xr[:, par, 1, :], S[3], S[1], ALU.add), reads=K[3] + K[1], writes=[('xr', par, 1)])

            def s5D(pt):
                par = pt % 2
                ut = pt // 4
                mm(ps[2][:], [(Ctab[:, pt, 0, :], xr[:, par, 0, :]), (Ctab[:, pt, 1, :], xr[:, par, 1, :])],
                   reads=[('xr', par, 0), ('xr', par, 1)], writes=[PK(2)], start=(pt % 4 == 0), stop=(pt % 4 == 3))
                if pt % 4 == 3:
                    Y = rstd[:]
                    W = sq[:].rearrange("p a b -> p (a b)").bitcast(F32)
                    YK, WK = ['rstd'], [('sq', 0), ('sq', 1)]
                    op('dve', lambda e: e.scalar_tensor_tensor(Y, ACT_[:, 8 + ut, :], d_skip[:, ut:ut + 1], ps[2][:], ALU.mult, ALU.add),
                       reads=[PK(2), AK(8 + ut)], writes=YK)
                    op('dve', lambda e: e.tensor_tensor(W, Y, Y, ALU.mult), reads=YK, writes=WK)
                    op('dve', lambda e: e.tensor_scalar(W, W, 0.044715, 1.0, ALU.mult, ALU.add), reads=WK, writes=WK)
                    op('dve', lambda e: e.tensor_tensor(W, W, Y, ALU.mult), reads=WK + YK, writes=WK)
                    gelq.append([1,
                                 lambda: op('act', lambda e: e.activation(W, W, AF.Tanh, scale=float(np.sqrt(2.0 / np.pi))), reads=WK, writes=WK),
                                 lambda: op('dve', lambda e: e.scalar_tensor_tensor(W, W, 1.0, Y, ALU.add, ALU.mult), reads=WK + YK, writes=WK),
                                 lambda ut=ut: op('act', lambda e: e.activation(ACT_[:, ut, :], W, AF.Copy, scale=0.5), reads=WK, writes=[AK(ut)])])

            gelq = []

            def gelu_act():
                for g in gelq:
                    if g[0] == 1:
                        g[1]()
                        g[0] = 2
                    elif g[0] == 3:
                        g[3]()
                        g[0] = 4

            late_dve = []

            def drain_late_dve():
                while late_dve:
                    late_dve.pop(0)()

            def gelu_dve():
                drain_late_dve()
                for g in gelq:
                    if g[0] == 2:
                        g[2]()
                        g[0] = 3

            nkb = 4 * ti + 4
            YFt = [SC[:, 8, :], SC[:, 9, :], SC[:, 10, :], arena[:, 9728 + 1024:9728 + 1536]]
            YFk = [SK(8), SK(9), SK(10), [AK(4), AK(5)]]
            fox_items = []
            rot = [0, 0]
            for m_ in range(4):
                hA, hB = 2 * m_, 2 * m_ + 1
                qk = AK(12 + m_)

                def stageA(kb, m_=m_, hA=hA, hB=hB, qk=qk):
                    sa = 3 + (rot[0] % 3)
                    sb = 3 + ((rot[0] + 1) % 3)
                    rot[0] += 2
                    ra = rot[1] % 4
                    rb = (rot[1] + 1) % 4
                    rot[1] += 2
                    intile = kb >= 4 * ti
                    c0 = 128 * (kb - 4 * ti) if intile else 0

                    def fn(e):
                        e.matmul(ps[sa][:, c0:512], KT[0:64, m_, kb * 128:(kb + 1) * 128], ACT_[0:64, 12 + m_, c0:512], start=True, stop=False)
                        e.matmul(ps[sb][:, c0:512], KT[64:128, m_, kb * 128:(kb + 1) * 128], ACT_[64:128, 12 + m_, c0:512], start=True, stop=False)
                        e.matmul(ps[sa][:, c0:512], selb[0:24, hA, :], ga[0:24, c0:512], start=False, stop=not intile)
                        i2 = e.matmul(ps[sb][:, c0:512], selb[64:88, hB, :], ga[64:88, c0:512], start=False, stop=not intile)
                        if intile:
                            e.matmul(ps[sa][:, c0:c0 + 128], identb[:], negmb[:], start=False, stop=True)
                            i2 = e.matmul(ps[sb][:, c0:c0 + 128], identb[:], negmb[:], start=False, stop=True)
                        return i2
                    op('pe', fn, reads=[('KT', m_, kb // 4), qk, 'ga'], writes=[PK(sa), PK(sb)])
                    for (sx, rx, hx) in ((sa, ra, hA), (sb, rb, hB)):
                        op('act', lambda e, sx=sx, rx=rx, hx=hx: e.activation(pT[:, rx, c0:512], ps[sx][:, c0:512], AF.Exp, bias=biasT[:, kb, hx:hx + 1], scale=1.0),
                           reads=[PK(sx), 'biasT'], writes=[('pT', rx)])
                    return ra, rb, c0

                def stageB(kb, st, hA=hA, hB=hB):
                    ra, rb, c0 = st
                    first, last = (kb == 0), (kb == nkb - 1)

                    def fn(e):
                        e.matmul(ps[6][0:64, c0:512], VC[:, kb, hA, :], pT[:, ra, c0:512], start=first, stop=last, tile_position=(0, 0))
                        e.matmul(ps[6][64:128, c0:512], VC[:, kb, hB, :], pT[:, rb, c0:512], start=first, stop=last, tile_position=(0, 64))
                        e.matmul(ps[7][0:64, c0:512], onesb[:, 0:64], pT[:, ra, c0:512], start=first, stop=last, tile_position=(0, 0))
                        return e.matmul(ps[7][64:128, c0:512], onesb[:, 64:128], pT[:, rb, c0:512], start=first, stop=last, tile_position=(0, 64))
                    op('pe', fn, reads=[('pT', ra), ('pT', rb), ('VC', kb)], writes=[PK(6), PK(7)])

                def fin(m_=m_):
                    drain_late_dve()
                    op('act', lambda e: e.activation(rl[:], ps[7][:], AF.Copy), reads=[PK(7)], writes=['rl'])
                    op('act', lambda e: e.activation(rlb[:], ps[6][:], AF.Copy), reads=[PK(6)], writes=['rlb'])

                    def fin_dve(m_=m_):
                        op('dve', lambda e: e.reciprocal(rl[:], rl[:]), reads=['rl'], writes=['rl'])
                        op('dve', lambda e: e.tensor_tensor(YFt[m_], rlb[:], rl[:], ALU.mult), reads=['rlb', 'rl'], writes=YFk[m_])
                    late_dve.append(fin_dve)

                state = {}

                def item_first(stageA=stageA, state=state):
                    state[0] = stageA(0)

                def item_mid(kb, stageA=stageA, stageB=stageB, state=state):
                    if kb + 1 < nkb:
                        state[kb + 1] = stageA(kb + 1)
                    stageB(kb, state[kb])

                fox_items.append(item_first)
                if m_ > 0:
                    fox_items.append(prev_fin[0])
                for kb in range(nkb):
                    fox_items.append(lambda kb=kb, item_mid=item_mid: item_mid(kb))
                prev_fin = [fin]
            fox_items.append(prev_fin[0])

            def s5_slot(k):
                if ok(k):
                    s5A(k)
                    s5B1(k)
                    s5B2(k)
                if ok(k - 2):
                    s5C2a(k - 2)
                    s5C2b(k - 2)
                if ok(k - 1):
                    s5B3(k - 1)
                    s5C1(k - 1)
                    s5C1b(k - 1)
                gelu_dve()
                if ok(k - 3):
                    s5D(k - 3)

            ok = lambda p: 0 <= p < 16
            load_E(0)
            load_E(1)
            load_E(2)
            spans, rk = R.take(32, span=4)

            def vproj(tb):
                bank = 3 + tb % 2
                mm(ps[bank][:], [(ACT_[:, kc, tb * 128:(tb + 1) * 128], spans[kc]) for kc in range(KC)], reads=rk + hkeys, writes=[PK(bank)])
                blk = 4 * ti + tb
                op('dve' if tb % 2 else 'act',
                   (lambda e, blk=blk, bank=bank: e.tensor_copy(VC[:, blk, :, :], ps[bank][:].rearrange("p (h d) -> p h d", d=64))) if tb % 2 else
                   (lambda e, blk=blk, bank=bank: e.activation(VC[:, blk, :, :], ps[bank][:].rearrange("p (h d) -> p h d", d=64), AF.Copy)),
                   reads=[PK(bank)], writes=[('VC', blk)] + ([('stage', 0), ('stage', 1)] if 4 <= blk < 20 else []))
            s5_slot(0)
            vproj(0)
            vproj(1)
            s5_slot(1)
            vproj(2)
            vproj(3)
            s5_slot(2)
            mm(ps[5][0:8, :], [(wfb[:, kc, :], ACT_[:, kc, :]) for kc in range(KC)], reads=hkeys + ['wfb'], writes=[PK(5)])
            op('act', lambda e: e.activation(ft1[:], ps[5][0:8, :], AF.Exp, bias=cvec[0:8, 1:2], scale=-1.0), reads=[PK(5)], writes=['ft1'])
            op('act', lambda e: e.activation(ft1[:], ft1[:], AF.Ln, bias=1.0, scale=1.0), reads=['ft1'], writes=['ft1'])
            op('dve', lambda e: e.tensor_tensor_scan(fabs_[:], onesf[0:8, 0:1].to_broadcast([8, 512]), ft1[:], fref[:, 0:1], ALU.mult, ALU.subtract),
               reads=['ft1', 'fref'], writes=['fabs'])
            op('dve', lambda e: e.tensor_scalar(ft1[:], fabs_[:], fref[:, 0:1], None, ALU.subtract), reads=['fabs', 'fref'], writes=['ft1'])
            op('dve', lambda e: e.tensor_copy(ga[0:8, :], ft1[:]), reads=['ft1'], writes=['ga'])
            op('dve', lambda e: e.tensor_copy(ga[64:72, :], ft1[:]), reads=['ft1'], writes=['ga'])
            op('dve', lambda e: e.tensor_tensor(ft1[:], ft1[:], ga[0:8, :], ALU.subtract), reads=['ft1', 'ga'], writes=['ft1'])
            op('dve', lambda e: e.tensor_copy(gml[:, 0, :], ft1[:]), reads=['ft1'], writes=['gml'])
            op('dve', lambda e: e.tensor_tensor(ft1[:], ft1[:], gml[:, 0, :], ALU.subtract), reads=['ft1', 'gml'], writes=['ft1'])
            op('dve', lambda e: e.tensor_copy(gml[:, 1, :], ft1[:]), reads=['ft1'], writes=['gml'])
            op('pool', lambda e: e.dma_start(out=ga[8:16, :], in_=gml[:, 0, :]), reads=['gml'], writes=['ga'], chan='GA0')
            op('pool', lambda e: e.dma_start(out=ga[16:24, :], in_=gml[:, 1, :]), reads=['gml'], writes=['ga'], chan='GA1')
            op('pool', lambda e: e.dma_start(out=ga[72:80, :], in_=gml[:, 0, :]), reads=['gml'], writes=['ga'], chan='GA2')
            op('pool', lambda e: e.dma_start(out=ga[80:88, :], in_=gml[:, 1, :]), reads=['gml'], writes=['ga'], chan='GA3')
            for jb in range(4):
                mm(ps[4][:, jb * 8:(jb + 1) * 8], [(fabs_[:, jb * 128:(jb + 1) * 128], identf[0:8, 0:8])], reads=['fabs'], writes=[PK(4)])
            op('dve', lambda e, ti=ti: e.tensor_scalar(negF[:, 4 * ti:4 * ti + 4, :], ps[4][:, 0:32].rearrange("p (a b) -> p a b", b=8), -1.0, None, ALU.mult),
               reads=[PK(4)], writes=['negF'])
            op('dve', lambda e: e.tensor_scalar(dg8[:], identf[0:8, 0:8], fref[:, 0:1], None, ALU.mult), reads=['fref'], writes=['dg8'])
            mm(ps[4][:, 64:72], [(onesf[0:8, 0:128], dg8[:])], reads=['dg8'], writes=[PK(4)])
            op('dve', lambda e: e.tensor_copy(frbc[:], ps[4][:, 64:72]), reads=[PK(4)], writes=['frbc'])
            for kb in range(4 * ti + 4):
                op('dve', lambda e, kb=kb: e.tensor_tensor(biasT[:, kb, :], negF[:, kb, :], frbc[:], ALU.add), reads=['negF', 'frbc'], writes=['biasT'])
            op('dve', lambda e: e.tensor_copy(fref[:], fabs_[:, 511:512]), reads=['fabs'], writes=['fref'])
            NPRE = 3
            NSL = 20
            per = -(-len(fox_items) // (NSL - NPRE))
            fi = 0
            for k in range(NPRE, NSL):
                s5_slot(k)
                for _ in range(per):
                    if fi < len(fox_items):
                        fox_items[fi]()
                        fi += 1
                gelu_act()
            while fi < len(fox_items):
                fox_items[fi]()
                fi += 1
            for _ in range(3):
                gelu_dve()
                gelu_act()
            assert all(g[0] == 4 for g in gelq)
            c5, s5 = E512[:, 0, :], E512[:, 1, :]
            op('dve', lambda e: e.tensor_tensor(ztmp[:, 0, :], zl[:, 0, :], c5, ALU.mult), reads=['zl'], writes=['ztmp'])
            op('dve', lambda e: e.tensor_tensor(ztmp[:, 1, :], zl[:, 1, :], s5, ALU.mult), reads=['zl'], writes=['ztmp'])
            op('dve', lambda e: e.tensor_tensor(ztmp[:, 2, :], zl[:, 1, :], c5, ALU.mult), reads=['zl'], writes=['ztmp'])
            op('dve', lambda e: e.tensor_tensor(ztmp[:, 3, :], zl[:, 0, :], s5, ALU.mult), reads=['zl'], writes=['ztmp'])
            op('dve', lambda e: e.tensor_tensor(sinit[:, 0, :], ztmp[:, 0, :], ztmp[:, 1, :], ALU.subtract), reads=['ztmp'], writes=['sinit'])
            op('dve', lambda e: e.tensor_tensor(sinit[:, 1, :], ztmp[:, 2, :], ztmp[:, 3, :], ALU.add), reads=['ztmp'], writes=['sinit'])
            for m in range(4):
                blks, rk = R.take(4)
                bank = m % 2
                mm(ps[bank][:], [(blks[kc], ACT_[:, kc, :]) for kc in range(4)], reads=rk + [AK(kc) for kc in range(4)], writes=[PK(bank)])
                op('act', lambda e, m=m, bank=bank: e.activation(SC[:, 4, :], ps[bank][:], AF.Sigmoid, bias=glu_b[:, m:m + 1], scale=1.0), reads=[PK(bank)], writes=SK(4))
                op('dve', lambda e, m=m: e.tensor_tensor(SC[:, m, :], ACT_[:, m, :], SC[:, 4, :], ALU.mult), reads=SK(4) + [AK(m)], writes=SK(m))
            rms_rstd([(SC[:, m, :], SK(m)) for m in range(4)], 512)
            for m in range(4):
                op('dve', lambda e, m=m: e.scalar_tensor_tensor(ACT_[:, m, :], SC[:, m, :], g_ssm[:, m:m + 1], rstd[:], ALU.mult, ALU.mult),
                   reads=SK(m) + ['rstd'], writes=[AK(m)])
            rms_rstd([(YFt[m], YFk[m]) for m in range(4)], 512)
            for m in (3, 0, 1, 2):
                op('dve', lambda e, m=m: e.scalar_tensor_tensor(ACT_[:, 4 + m, :], YFt[m], g_fox[:, m:m + 1], rstd[:], ALU.mult, ALU.mult),
                   reads=YFk[m] + ['rstd'], writes=[AK(4 + m)])

            proj8(0, lambda m, bank: evac_copy(SC[:, m, :], SK(m), bank, m))
            post_norm_residual(lambda kc: SC[:, kc, :], SK, g_mixpost)

            pre_norm(g_xapre, 8)
            proj8(8, lambda m, bank: evac_copy(ACT_[:, m, :], [AK(m)], bank, m, scale=1.0 / 16.0))
            for hx in range(4):
                c0_, c1_ = 2 * hx, 2 * hx + 1
                par = hx % 2
                ob = (4, 5, 6) if par == 0 else (0, 1, 7)
                rlt, rlk = (rl, 'rl') if par == 0 else (rlb, 'rlb')
                for mb in range(2):
                    sb = 2 + mb
                    pi = 2 * par + mb
                    mm(ps[sb][:], [(KM[:, c0_, mb * 128:(mb + 1) * 128], ACT_[:, c0_, :]), (KM[:, c1_, mb * 128:(mb + 1) * 128], ACT_[:, c1_, :])],
                       reads=[AK(c0_), AK(c1_)], writes=[PK(sb)])
                    op('act', lambda e, sb=sb, pi=pi: e.activation(pT[:, pi, :], ps[sb][:], AF.Exp), reads=[PK(sb)], writes=[('pT', pi)])
                for mb in range(2):
                    pi = 2 * par + mb
                    for dc in range(2):
                        mm(ps[ob[dc]][:], [(VM[:, mb, (2 * hx + dc) * 128:(2 * hx + dc + 1) * 128], pT[:, pi, :])], reads=[('pT', pi)], writes=[PK(ob[dc])],
                           start=(mb == 0), stop=(mb == 1))
                    mm(ps[ob[2]][:], [(onesb[:], pT[:, pi, :])], reads=[('pT', pi)], writes=[PK(ob[2])], start=(mb == 0), stop=(mb == 1))
                op('dve', lambda e, rlt=rlt, ob=ob: e.reciprocal(rlt[:], ps[ob[2]][:]), reads=[PK(ob[2])], writes=[rlk])
                for dc in range(2):
                    op('dve', lambda e, dc=dc, hx=hx, rlt=rlt, ob=ob: e.tensor_tensor(ACT_[:, 8 + 2 * hx + dc, :], ps[ob[dc]][:], rlt[:], ALU.mult),
                       reads=[PK(ob[dc]), rlk], writes=[AK(8 + 2 * hx + dc)])
            proj8(8, lambda m, bank: evac_copy(SC[:, m, :], SK(m), bank, m))
            post_norm_residual(lambda kc: SC[:, kc, :], SK, g_xapost)

            pre_norm(g_ffnpre, 0)
            for m in range(HC):
                blks, rk = R.take(16)
                par = m % 2
                bg, bu = 2 + 2 * par, 3 + 2 * par
                mm(ps[bg][:], [(blks[kc], ACT_[:, kc, :]) for kc in range(KC)], reads=rk + hkeys, writes=[PK(bg)])
                mm(ps[bu][:], [(blks[8 + kc], ACT_[:, kc, :]) for kc in range(KC)], reads=rk + hkeys, writes=[PK(bu)])
                tq_, tk_ = (rl, 'rl') if par == 0 else (rlb, 'rlb')
                op('act', lambda e, tq_=tq_, bg=bg: e.activation(tq_[:], ps[bg][:], AF.Silu), reads=[PK(bg)], writes=[tk_])
                op('dve', lambda e, tq_=tq_, bu=bu, m=m: e.tensor_tensor(hid[:, m, :], tq_[:], ps[bu][:], ALU.mult),
                   reads=[PK(bu), tk_], writes=[HK(m)])
            for m in range(8):
                blks, rk = R.take(HC)
                bank = m % 2
                mm(ps[bank][:], [(blks[kc], hid[:, kc, :]) for kc in range(HC)], reads=rk + [HK(kc) for kc in range(HC)], writes=[PK(bank)])
                evac_copy(O3[:, m, :], O3K(m), bank, m)
            post_norm_residual(lambda kc: O3[:, kc, :], O3K, g_ffnpost, final=True)
            op('sp', lambda e, T0=T0: e.dma_start(out=yT_v[:, :, T0:T0 + TT], in_=O3), reads=[AK(c) for c in range(16)], writes=[('yT', ti)], chan='XO')

        op('sp', None, reads=[('yT', ti) for ti in range(nt)])
        with nc.Block() as block:
            P.emit(block)
    return nc


def _blk(W, kc, m):
    return W[kc * 128:(kc + 1) * 128, m * 128:(m + 1) * 128]


def _weight_stream(w_in, glu_w, w_out, xa_wq, xa_wo, w_gate, w_up, w_down, xa_wkv):
    blocks = []
    wi = w_in[:, :1536]
    for m in range(12):
        for kc in range(8):
            blocks.append(_blk(wi, kc, m))
    wv = w_in[:, 1536:2048]
    for kc in range(8):
        for j in range(4):
            blocks.append(_blk(wv, kc, j))
    for m in range(4):
        for kc in range(4):
            blocks.append(_blk(glu_w, kc, m))
    for W in (w_out, xa_wq, xa_wo):
        for m in range(8):
            for kc in range(8):
                blocks.append(_blk(W, kc, m))
    for m in range(HC):
        for kc in range(8):
            blocks.append(_blk(w_gate, kc, m))
        for kc in range(8):
            blocks.append(_blk(w_up, kc, m))
    for m in range(8):
        for kc in range(HC):
            blocks.append(_blk(w_down, kc, m))
    assert len(blocks) == NBT
    wk, wvv = xa_wkv[:, :1024], xa_wkv[:, 1024:]
    for m in range(8):
        for kc in range(8):
            blocks.append(_blk(wk, kc, m))
    for kc in range(8):
        for j in range(8):
            blocks.append(_blk(wvv, kc, j))
    assert len(blocks) == NBT + NBP
    return np.ascontiguousarray(np.concatenate(blocks, axis=1), dtype=np.float32)


def _fm(v, n):
    return np.asarray(v, np.float32).reshape(n, 128).T


def _prep_shared(inp):
    f = lambda k: np.asarray(inp[k], np.float32)
    wst = _weight_stream(f("w_in"), f("ssm_glu_w"), f("w_out"), f("xa_wq"), f("xa_wo"), f("w_gate"), f("w_up"), f("w_down"), f("xa_wkv"))
    smp = np.zeros((128, SMP_N), np.float32)
    s_idx = np.arange(128)[:, None]
    t_idx = np.arange(128)[None, :]
    smp[:, 0:128] = np.where(s_idx <= t_idx, 0.0, -30000.0)
    smp[:, 128:256] = np.eye(128, dtype=np.float32)
    pvs = [(_fm(f("mix_pre_g"), 8)), _fm(f("ssm_out_g"), 4), _fm(f("fox_out_g"), 4), _fm(f("mix_post_g"), 8), _fm(f("xa_pre_g"), 8),
           _fm(f("mem_g"), 8), _fm(f("xa_post_g"), 8), _fm(f("ffn_pre_g"), 8), _fm(f("ffn_post_g"), 8), _fm(f("ssm_d"), 4), _fm(f("ssm_glu_b"), 4)]
    smp[:, 256:328] = np.concatenate(pvs, axis=1)
    smp[0:8, 328] = f("fox_f_bias")
    smt = np.zeros((128, SMT_N), np.float32)
    smt[:, 0:512] = np.arange(512, dtype=np.float32)[None, :]
    sel = np.zeros((24, 8, 128), np.float32)
    for r in range(24):
        sel[r, r % 8, :] = 1.0
    smt[0:24, 512:1536] = sel.reshape(24, 1024)
    smt[64:88, 512:1536] = sel.reshape(24, 1024)
    wf = f("w_in")[:, 2048:2056]
    smt[:, 1536:1600] = wf.reshape(8, 128, 8).transpose(1, 0, 2).reshape(128, 64)
    o = 1600

    def gl(a):
        a = np.asarray(a, np.float32)
        tail = a.shape[2:]
        a = a.reshape(16, 2, 64, *tail)
        a = np.moveaxis(a, 0, 2)
        return a.reshape(128, 16, *tail)
    smt[:, o:o + 16] = gl(f("ssm_a_re"))
    smt[:, o + 16:o + 32] = gl(f("ssm_a_im"))
    smt[:, o + 32:o + 48] = gl(np.repeat(f("ssm_log_dt")[:, None], 64, axis=1))
    smt[:, o + 48:o + 304] = gl(f("ssm_b_re")).reshape(128, 256)
    smt[:, o + 304:o + 560] = gl(f("ssm_b_im")).reshape(128, 256)
    smt[:, o + 560:o + 816] = gl(np.transpose(f("ssm_c_re"), (0, 2, 1))).reshape(128, 256)
    smt[:, o + 816:o + 1072] = gl(np.transpose(f("ssm_c_im"), (0, 2, 1))).reshape(128, 256)
    return wst, smp, smt


_NC_CACHE = {}


def kernel(**inputs):
    x = np.asarray(inputs["x"], np.float32)
    mem = np.asarray(inputs["mem"], np.float32)
    B = x.shape[0]
    nt = x.shape[1] // TT
    wst, smp, smt = _prep_shared(inputs)
    if nt not in _NC_CACHE:
        _NC_CACHE[nt] = build(nt)
    nc = _NC_CACHE[nt]
    in_maps = []
    for b in range(B):
        in_maps.append({"xT": np.ascontiguousarray(x[b].T), "memT": np.ascontiguousarray(mem[b].T),
                        "wst": wst, "smp": smp, "smt": smt})
    res = run_bass_kernel_spmd(nc, in_maps, core_ids=list(range(B)))
    out = np.stack([np.ascontiguousarray(r["yT"].T) for r in res.results], axis=0)
    return out.astype(np.float32)
```

```python
import numpy as np
from contextlib import ExitStack
import concourse.bass as bass
import concourse.mybir as mybir
from concourse.bass_utils import run_bass_kernel_spmd

F32 = mybir.dt.float32
BF16 = mybir.dt.bfloat16
I32 = mybir.dt.int32
ALU = mybir.AluOpType
AF = mybir.ActivationFunctionType
ENGS = ['pe', 'act', 'dve', 'pool', 'sp']

D = 1024
KC = 8
TT = 512
NT = 8
HC = 22
NM = 256
NBT = 864
NBP = 128
CH = 16
NSLOT = 5
NCH_T = NBT // CH
NCH_P = NBP // CH
EPS = 1e-6
TWO_PI = float(2 * np.pi)
SMP_N = 329 + 64
SMT_N = 2672


class Prog:
    def __init__(self, nc, es):
        self.nc = nc
        self.es = es
        self.ops = {e: [] for e in ENGS}
        self.cnt = {}
        self.semh = {}
        self.lastw = {}
        self.readers = {}
        self.seen = {e: {} for e in ENGS}
        for e in ENGS:
            self.newsem('S_' + e)

    def newsem(self, name):
        if name not in self.semh:
            self.semh[name] = self.es.enter_context(self.nc.semaphore(name))
            self.cnt[name] = 0

    def op(self, eng, fn, reads=(), writes=(), chan=None):
        need = {}

        def add(tok, raw):
            if tok is None:
                return
            sem, val, teng, isdma = tok
            if (not isdma) and teng == eng:
                if eng == 'pe':
                    return
            if need.get(sem, 0) < val:
                need[sem] = val

        for k in reads:
            add(self.lastw.get(k), True)
        for k in writes:
            add(self.lastw.get(k), False)
            for t in self.readers.get(k, {}).values():
                add(t, False)
        waits = []
        for sem, val in need.items():
            if self.seen[eng].get(sem, 0) < val:
                self.seen[eng][sem] = val
                waits.append((sem, val))
        if fn is None:
            self.ops[eng].append((waits, None, None, 0))
            return None
        if chan is not None:
            self.newsem(chan)
            sem, inc, isdma = chan, 16, True
        else:
            sem, inc, isdma = 'S_' + eng, 1, False
        self.cnt[sem] += inc
        tok = (sem, self.cnt[sem], eng, isdma)
        for k in writes:
            self.lastw[k] = tok
            self.readers[k] = {}
        for k in reads:
            self.readers.setdefault(k, {})[sem] = tok
        self.ops[eng].append((waits, fn, sem, inc))
        return tok

    def barrier(self):
        for e in ENGS:
            waits = []
            for sem, val in self.cnt.items():
                if val > 0 and self.seen[e].get(sem, 0) < val and sem != 'S_' + e:
                    self.seen[e][sem] = val
                    waits.append((sem, val))
            self.ops[e].append((waits, None, None, 0))

    def emit(self, block):
        decos = {'pe': block.tensor, 'act': block.scalar, 'dve': block.vector,
                 'pool': block.gpsimd, 'sp': block.sync}
        for e in ENGS:
            ops = self.ops[e]

            def body(engh, ops=ops):
                for waits, fn, sem, inc in ops:
                    for ws, wv in waits:
                        engh.wait_ge(self.semh[ws], wv)
                    if fn is not None:
                        ins = fn(engh)
                        ins.then_inc(self.semh[sem], inc)

            decos[e](body)


def build(nt=NT):
    S_ = nt * TT
    NBLKS = 4 * nt
    nc = bass.Bass("TRN2", target_bir_lowering=False)
    xT = nc.dram_tensor("xT", [D, S_], F32, kind="ExternalInput").ap()
    memT = nc.dram_tensor("memT", [D, NM], F32, kind="ExternalInput").ap()
    wst = nc.dram_tensor("wst", [128, (NBT + NBP) * 128], F32, kind="ExternalInput").ap()
    smp_d = nc.dram_tensor("smp", [128, SMP_N], F32, kind="ExternalInput").ap()
    smt_d = nc.dram_tensor("smt", [128, SMT_N], F32, kind="ExternalInput").ap()
    yT = nc.dram_tensor("yT", [D, S_], F32, kind="ExternalOutput").ap()
    wscr = nc.dram_tensor("wscr", [128, (NBT + NBP) * 128], BF16).ap()
    escr = nc.dram_tensor("escr", [128, 16 * 2 * 512], BF16).ap().rearrange("p (a c b) -> p a c b", c=2, b=512)
    xT_v = xT.rearrange("(kc p) t -> p kc t", p=128)
    yT_v = yT.rearrange("(kc p) t -> p kc t", p=128)
    memT_v = memT.rearrange("(kc p) t -> p kc t", p=128)

    with ExitStack() as es:
        P = Prog(nc, es)
        T = lambda name, shape, dt: es.enter_context(nc.sbuf_tensor(name, shape, dt))
        op = P.op

        smp = T("smp_s", [128, SMP_N], F32)
        negm = smp[:, 0:128]
        identf = smp[:, 128:256]
        pv = smp[:, 256:328]
        fb = smp[:, 328:329]
        g_mixpre, g_ssm, g_fox, g_mixpost = pv[:, 0:8], pv[:, 8:12], pv[:, 12:16], pv[:, 16:24]
        g_xapre, g_mem, g_xapost, g_ffnpre, g_ffnpost = pv[:, 24:32], pv[:, 32:40], pv[:, 40:48], pv[:, 48:56], pv[:, 56:64]
        d_skip, glu_b = pv[:, 64:68], pv[:, 68:72]
        cvec = T("cvec", [128, 8], F32)
        onesb = T("onesb", [128, 128], BF16)
        onesf = T("onesf", [128, 128], F32)
        identb = T("identb", [128, 128], BF16)
        selb = T("selb", [128, 8, 128], BF16)
        wfb = T("wfb", [128, 8, 8], BF16)
        ring = T("ring", [128, NSLOT, CH * 128], BF16)
        KT = T("KT", [128, 4, S_], BF16)
        VC = T("VC", [128, NBLKS, 8, 64], BF16)
        Ebuf = T("Ebuf", [128, 3, 2, 512], BF16)
        s5x = T("s5x", [128, 4, 512], F32)
        Btab = T("Btab", [128, 16, 2, 128], BF16)
        Ctab = T("Ctab", [128, 16, 2, 128], BF16)
        Rr = T("Rr", [128, 16], F32)
        E512 = T("E512", [128, 2, 16], F32)
        sinit = T("sinit", [128, 2, 16], F32)
        zl = T("zl", [128, 2, 16], F32)
        ztmp = T("ztmp", [128, 4, 16], F32)
        KM = T("KM", [128, 8, NM], BF16)
        VM = T("VM", [128, 2, D], BF16)
        sq = T("sq", [128, 2, 512], BF16)
        rstd = T("rstd", [128, 512], F32)
        pT = T("pT", [128, 4, 512], BF16)
        xr = T("xr", [128, 2, 2, 512], BF16)
        negmb = T("negmb", [128, 128], BF16)
        rl = T("rl", [128, 512], F32)
        rlb = T("rlb", [128, 512], F32)
        negF = T("negF", [128, NBLKS, 8], F32)
        biasT = T("biasT", [128, NBLKS, 8], F32)
        frbc = T("frbc", [128, 8], F32)
        fref = T("fref", [8, 1], F32)
        dg8 = T("dg8", [8, 8], F32)
        ft1 = T("ft1", [8, 512], F32)
        fabs_ = T("fabs", [8, 512], F32)
        ga = T("ga", [128, 512], BF16)
        gml = T("gml", [8, 2, 512], BF16)
        arena = T("arena", [128, 13824], F32)
        ps = [es.enter_context(nc.psum_tensor("ps%d" % i, [128, 512], F32)) for i in range(8)]
        PK = lambda b: ('ps', b)

        xs = arena[:, 0:4096].rearrange("p (a b) -> p a b", b=512)
        hid = arena[:, 4096:9728].bitcast(BF16).rearrange("p (a b) -> p a b", b=512)
        SC = arena[:, 4096:9728].rearrange("p (a b) -> p a b", b=512)
        ACT_ = arena[:, 9728:13824].bitcast(BF16).rearrange("p (a b) -> p a b", b=512)
        O3 = arena[:, 9728:13824].rearrange("p (a b) -> p a b", b=512)
        XK = lambda kc: ('xs', kc)
        HK = lambda c: ('H', c)
        SK = lambda i: [('H', 2 * i), ('H', 2 * i + 1)]
        AK = lambda c: ('A', c)
        O3K = lambda i: [('A', 2 * i), ('A', 2 * i + 1)]
        stage = arena[:, 0:4096].rearrange("p (a b) -> p a b", b=2048)
        cbuf = arena[:, 4096:6144].bitcast(BF16).rearrange("p (a b) -> p a b", b=2048)
        smt = arena[:, 6144:6144 + SMT_N]
        pw = arena[:, 8832:13824]
        iota = smt[:, 0:512]
        self_f = smt[:, 512:1536]
        wf_f = smt[:, 1536:1600]
        s5o = 1600
        are, aim, ldt = smt[:, s5o:s5o + 16], smt[:, s5o + 16:s5o + 32], smt[:, s5o + 32:s5o + 48]
        s5v = lambda k: smt[:, s5o + 48 + 256 * k: s5o + 48 + 256 * (k + 1)].rearrange("p (a b) -> p a b", b=16)
        bre_, bim_, cre_, cim_ = s5v(0), s5v(1), s5v(2), s5v(3)

        if NBLKS >= 20:
            lstage = VC[:, 4:20, :, :].rearrange("p a h d -> p (a h d)").bitcast(F32).rearrange("p (a b) -> p a b", b=2048)
        else:
            lstage = T("lstage", [128, 2, 2048], F32)
        class Ring:
            def __init__(self):
                self.seq = [NCH_T + i for i in range(NCH_P)] + [c for _ in range(nt) for c in range(NCH_T)]
                self.next_dma = 0
                self.cur = 0
                self.casted = set()
                self.ncast = 0

            def ensure(self, q_lo):
                lim = min(len(self.seq), q_lo + NSLOT)
                while self.next_dma < lim:
                    q = self.next_dma
                    slot = q % NSLOT
                    dc = self.seq[q]
                    if dc not in self.casted:
                        self.casted.add(dc)
                        b = self.ncast % 2
                        eng = ['act', 'dve'][self.ncast % 2]
                        self.ncast += 1
                        op('sp', lambda e, dc=dc, b=b: e.dma_start(out=lstage[:, b, :], in_=wst[:, dc * 2048:(dc + 1) * 2048]),
                           writes=[('stage', b)], chan='ST%d' % b)
                        if eng == 'act':
                            op('act', lambda e, b=b, slot=slot: e.activation(ring[:, slot, :], lstage[:, b, :], AF.Copy), reads=[('stage', b)], writes=[('ring', slot)])
                        else:
                            op(eng, lambda e, b=b, slot=slot: e.tensor_copy(ring[:, slot, :], lstage[:, b, :]), reads=[('stage', b)], writes=[('ring', slot)])
                        if dc < NCH_T and nt > 1:
                            op('pool', lambda e, dc=dc, slot=slot: e.dma_start(out=wscr[:, dc * 2048:(dc + 1) * 2048], in_=ring[:, slot, :]),
                               reads=[('ring', slot)], writes=[('wscr', dc)], chan='SO%d' % slot)
                    else:
                        op('sp', lambda e, slot=slot, dc=dc: e.dma_start(out=ring[:, slot, :], in_=wscr[:, dc * 2048:(dc + 1) * 2048]),
                           reads=[('wscr', dc)], writes=[('ring', slot)], chan='W%d' % slot)
                    self.next_dma += 1

            def take(self, n, span=0):
                g0 = self.cur
                self.cur += n
                self.ensure(g0 // CH)
                aps, keys = [], set()
                step = span if span else 1
                for g in range(g0, g0 + n, step):
                    q = g // CH
                    slot = q % NSLOT
                    j = g % CH
                    aps.append(ring[:, slot, j * 128:(j + step) * 128])
                    keys.add(('ring', slot))
                    if span:
                        assert (g + step - 1) // CH == q
                return aps, list(keys)

        R = Ring()

        def mm(out, pairs, reads, writes, start=True, stop=True):
            def fn(e):
                n = len(pairs)
                ins = None
                for i, (l, r) in enumerate(pairs):
                    ins = e.matmul(out, l, r, start=(start and i == 0), stop=(stop and i == n - 1))
                return ins
            op('pe', fn, reads, writes)

        op('sp', lambda e: e.dma_start(out=smp[:], in_=smp_d), writes=['smp'], chan='C0')
        op('sp', lambda e: e.dma_start(out=smt, in_=smt_d), writes=['smt'], chan='C1')
        op('dve', lambda e: e.memset(cvec[:], EPS), writes=['cvec'])
        op('dve', lambda e: e.tensor_scalar(cvec[0:8, 1:2], fb[0:8, :], -1.0, None, ALU.mult), reads=['smp', 'cvec'], writes=['cvec'])
        op('dve', lambda e: e.memset(cvec[:, 2:3], 0.25), reads=['cvec'], writes=['cvec'])
        op('dve', lambda e: e.memset(onesb[:], 1.0), writes=['onesb'])
        op('pool', lambda e: e.memset(onesf[:], 1.0), writes=['onesf'])
        op('dve', lambda e: e.tensor_copy(identb[:], identf), reads=['smp'], writes=['identb'])
        op('dve', lambda e: e.tensor_copy(negmb[:], negm), reads=['smp'], writes=['negmb'])
        op('dve', lambda e: e.tensor_copy(selb[:].rearrange("p a b -> p (a b)"), self_f), reads=['smt'], writes=['selb'])
        op('dve', lambda e: e.tensor_copy(wfb[:].rearrange("p a b -> p (a b)"), wf_f), reads=['smt'], writes=['wfb'])
        op('pool', lambda e: e.memset(fref[:], 0.0), writes=['fref'])
        op('pool', lambda e: e.memset(sinit[:], 0.0), writes=['sinit'])
        op('pool', lambda e: e.memset(ga[:], 0.0), writes=['ga'])

        ss = lambda i: pw[:, 16 * i:16 * (i + 1)]
        big = lambda i: pw[:, 1024 + 512 * i: 1024 + 512 * (i + 1)]
        bigi = pw[:, 1024 + 512 * 6: 1024 + 512 * 7].bitcast(I32)
        DT_, LRE, TH, TQ, C1, S1, P1R, P1I, DEN, QRE, QIM, TA, TB, TI_, Y5 = range(15)
        sI = pw[:, 16 * 20:16 * 21].bitcast(I32)

        def dv(fn, r=('pw',), w=('pw',)):
            op('dve', fn, reads=list(r), writes=list(w))

        def frac_reduce(y, tf, ti):
            dv(lambda e: e.tensor_copy(ti, y))
            dv(lambda e: e.tensor_copy(tf, ti))
            dv(lambda e: e.tensor_tensor(y, y, tf, ALU.subtract))
            dv(lambda e: e.tensor_single_scalar(tf, y, 0.5, ALU.is_gt))
            dv(lambda e: e.tensor_tensor(y, y, tf, ALU.subtract))
            dv(lambda e: e.tensor_single_scalar(tf, y, -0.5, ALU.is_lt))
            dv(lambda e: e.tensor_tensor(y, y, tf, ALU.add))

        def act_pw(fn, r=('pw',), w=('pw',)):
            op('act', fn, reads=list(r), writes=list(w))

        act_pw(lambda e: e.activation(ss(DT_), ldt, AF.Exp), r=('smt', 'pw'))
        dv(lambda e: e.tensor_tensor(ss(LRE), are, ss(DT_), ALU.mult), r=('smt', 'pw'))
        dv(lambda e: e.tensor_tensor(ss(TH), aim, ss(DT_), ALU.mult), r=('smt', 'pw'))
        act_pw(lambda e: e.activation(Rr[:], ss(LRE), AF.Exp), w=('Rr', 'pw'))
        dv(lambda e: e.tensor_scalar(ss(TQ), ss(TH), 1.0 / TWO_PI, None, ALU.mult))
        frac_reduce(ss(TQ), ss(TA), sI)
        act_pw(lambda e: e.activation(ss(S1), ss(TQ), AF.Sin, scale=TWO_PI))
        dv(lambda e: e.tensor_scalar(ss(Y5), ss(TQ), 0.25, None, ALU.add))
        frac_reduce(ss(Y5), ss(TA), sI)
        act_pw(lambda e: e.activation(ss(C1), ss(Y5), AF.Sin, scale=TWO_PI))
        dv(lambda e: e.tensor_scalar(ss(Y5), ss(TQ), 512.0, None, ALU.mult))
        frac_reduce(ss(Y5), ss(TA), sI)
        act_pw(lambda e: e.activation(E512[:, 1, :], ss(Y5), AF.Sin, scale=TWO_PI), w=('E512', 'pw'))
        dv(lambda e: e.tensor_scalar(ss(Y5), ss(Y5), 0.25, None, ALU.add))
        frac_reduce(ss(Y5), ss(TA), sI)
        act_pw(lambda e: e.activation(E512[:, 0, :], ss(Y5), AF.Sin, scale=TWO_PI), w=('E512', 'pw'))
        dv(lambda e: e.tensor_tensor(ss(P1R), Rr[:], ss(C1), ALU.mult), r=('Rr', 'pw'))
        dv(lambda e: e.tensor_tensor(ss(P1I), Rr[:], ss(S1), ALU.mult), r=('Rr', 'pw'))
        dv(lambda e: e.tensor_scalar(ss(P1R), ss(P1R), -1.0, None, ALU.add))
        dv(lambda e: e.tensor_tensor(ss(DEN), are, are, ALU.mult), r=('smt', 'pw'))
        dv(lambda e: e.tensor_tensor(ss(TA), aim, aim, ALU.mult), r=('smt', 'pw'))
        dv(lambda e: e.tensor_tensor(ss(DEN), ss(DEN), ss(TA), ALU.add))
        dv(lambda e: e.reciprocal(ss(DEN), ss(DEN)))
        dv(lambda e: e.tensor_tensor(ss(QRE), ss(P1R), are, ALU.mult), r=('smt', 'pw'))
        dv(lambda e: e.tensor_tensor(ss(TA), ss(P1I), aim, ALU.mult), r=('smt', 'pw'))
        dv(lambda e: e.tensor_tensor(ss(QRE), ss(QRE), ss(TA), ALU.add))
        dv(lambda e: e.tensor_tensor(ss(QRE), ss(QRE), ss(DEN), ALU.mult))
        dv(lambda e: e.tensor_tensor(ss(QIM), ss(P1I), are, ALU.mult), r=('smt', 'pw'))
        dv(lambda e: e.tensor_tensor(ss(TA), ss(P1R), aim, ALU.mult), r=('smt', 'pw'))
        dv(lambda e: e.tensor_tensor(ss(QIM), ss(QIM), ss(TA), ALU.subtract))
        dv(lambda e: e.tensor_tensor(ss(QIM), ss(QIM), ss(DEN), ALU.mult))
        bbre = pw[:, 512:768].rearrange("p (a b) -> p a b", b=16)
        bbim = pw[:, 768:1024].rearrange("p (a b) -> p a b", b=16)
        tb16 = pw[:, 400:416]
        BLK = arena[:, 0:2048].bitcast(BF16).rearrange("p (a c b) -> p a c b", c=2, b=128)
        dv(lambda e: e.memset(BLK, 0.0))
        dv(lambda e: e.memset(Ctab[:], 0.0), w=('Ctab',))
        for pt in range(16):
            q = pt % 4
            dv(lambda e, pt=pt: e.tensor_scalar(tb16, bim_[:, pt, :], ss(QIM)[:, pt:pt + 1], None, ALU.mult), r=('smt', 'pw'))
            dv(lambda e, pt=pt: e.scalar_tensor_tensor(bbre[:, pt, :], bre_[:, pt, :], ss(QRE)[:, pt:pt + 1], tb16, ALU.mult, ALU.subtract), r=('smt', 'pw'))
            dv(lambda e, pt=pt: e.tensor_scalar(tb16, bre_[:, pt, :], ss(QIM)[:, pt:pt + 1], None, ALU.mult), r=('smt', 'pw'))
            dv(lambda e, pt=pt: e.scalar_tensor_tensor(bbim[:, pt, :], bim_[:, pt, :], ss(QRE)[:, pt:pt + 1], tb16, ALU.mult, ALU.add), r=('smt', 'pw'))
            for c, src in ((0, bbre), (1, bbim)):
                dv(lambda e, pt=pt, c=c, src=src, q=q: e.tensor_copy(BLK[0:64, pt, c, 32 * q:32 * q + 16], src[0:64, pt, :]))
                dv(lambda e, pt=pt, c=c, src=src, q=q: e.tensor_copy(BLK[64:128, pt, c, 32 * q + 16:32 * q + 32], src[64:128, pt, :]))
            dv(lambda e, pt=pt, q=q: e.tensor_copy(Ctab[0:64, pt, 0, 32 * q:32 * q + 16], cre_[0:64, pt, :]), r=('smt',), w=('Ctab',))
            dv(lambda e, pt=pt, q=q: e.tensor_copy(Ctab[64:128, pt, 0, 32 * q + 16:32 * q + 32], cre_[64:128, pt, :]), r=('smt',), w=('Ctab',))
            dv(lambda e, pt=pt, q=q: e.tensor_scalar(Ctab[0:64, pt, 1, 32 * q:32 * q + 16], cim_[0:64, pt, :], -1.0, None, ALU.mult), r=('smt',), w=('Ctab',))
            dv(lambda e, pt=pt, q=q: e.tensor_scalar(Ctab[64:128, pt, 1, 32 * q + 16:32 * q + 32], cim_[64:128, pt, :], -1.0, None, ALU.mult), r=('smt',), w=('Ctab',))
        for grp in range(8):
            bank = grp % 2
            for k in range(4):
                idx = grp * 4 + k
                pt, c = idx // 2, idx % 2
                mm(ps[bank][:, k * 128:(k + 1) * 128], [(BLK[:, pt, c, :], identb[:])], reads=['pw', 'identb'], writes=[PK(bank)])
            op('act', lambda e, grp=grp, bank=bank: e.activation(
                Btab[:].rearrange("p a c b -> p (a c b)")[:, grp * 512:(grp + 1) * 512], ps[bank][:], AF.Copy),
                reads=[PK(bank)], writes=['Btab'])
        for pt in range(16):
            for which, dst in ((0, 1), (1, 0)):
                if pt % 2 == 0:
                    y = big(which * 3)
                    tf = big(which * 3 + 1)
                    ti = bigi if which == 0 else pw[:, 1024 + 512 * 2: 1024 + 512 * 3].bitcast(I32)
                    key = ('big', which)
                else:
                    pb_ = lambda i: arena[:, 2048 + 512 * i: 2048 + 512 * (i + 1)]
                    y = pb_(which * 3)
                    tf = pb_(which * 3 + 1)
                    ti = pb_(which * 3 + 2).bitcast(I32)
                    key = ('bigp', which)
                if which == 0:
                    op('act', lambda e, y=y, pt=pt: e.activation(y, iota, AF.Copy, scale=ss(TQ)[:, pt:pt + 1]), reads=['smt', 'pw'], writes=[key])
                else:
                    op('act', lambda e, y=y, pt=pt: e.activation(y, iota, AF.Identity, scale=ss(TQ)[:, pt:pt + 1], bias=cvec[:, 2:3]), reads=['smt', 'pw', 'cvec'], writes=[key])
                for fn_ in (
                    lambda e, y=y, tf=tf, ti=ti: e.tensor_copy(ti, y),
                    lambda e, y=y, tf=tf, ti=ti: e.tensor_copy(tf, ti),
                    lambda e, y=y, tf=tf, ti=ti: e.tensor_tensor(y, y, tf, ALU.subtract),
                ):
                    op('dve', fn_, reads=[key], writes=[key])
                op('act', lambda e, y=y, dst=dst, pt=pt: e.activation(Ebuf[:, pt % 2, dst, :], y, AF.Sin, scale=TWO_PI), reads=[key], writes=[('E', pt % 2)])
            op('sp', lambda e, pt=pt: e.dma_start(out=escr[:, pt, :, :], in_=Ebuf[:, pt % 2, :, :]), reads=[('E', pt % 2)], writes=[('escr', pt)], chan='EO%d' % (pt % 2))

        mT = arena[:, 2048:4096].rearrange("p (a b) -> p a b", b=256)
        mnT = arena[:, 4096:5120].bitcast(BF16).rearrange("p (a b) -> p a b", b=256)
        op('sp', lambda e: e.dma_start(out=mT, in_=memT_v), writes=['mT', 'mnT', ('bigp', 0), ('bigp', 1)], chan='C2')
        for kc in range(KC):
            b = kc % 2
            op('act', lambda e, kc=kc, b=b: e.activation(sq[:, b, 0:256], mT[:, kc, :], AF.Square), reads=['mT'], writes=[('sq', b)])
            mm(ps[7][:, 0:256], [(onesb[:], sq[:, b, 0:256])], reads=[('sq', b), 'onesb'], writes=[PK(7)], start=(kc == 0), stop=(kc == KC - 1))
        op('act', lambda e: e.activation(rstd[:, 0:256], ps[7][:, 0:256], AF.Sqrt, bias=cvec[:, 0:1], scale=1.0 / D), reads=[PK(7), 'cvec'], writes=['rstd'])
        op('dve', lambda e: e.reciprocal(rstd[:, 0:256], rstd[:, 0:256]), reads=['rstd'], writes=['rstd'])
        for kc in range(KC):
            op('dve', lambda e, kc=kc: e.scalar_tensor_tensor(mnT[:, kc, :], mT[:, kc, :], g_mem[:, kc:kc + 1], rstd[:, 0:256], ALU.mult, ALU.mult),
               reads=['mT', 'rstd', 'smp'], writes=['mnT'])
        for m in range(8):
            blks, rk = R.take(8)
            bank = m % 2
            mm(ps[bank][:, 0:256], [(blks[kc], mnT[:, kc, :]) for kc in range(KC)], reads=rk + ['mnT'], writes=[PK(bank)])
            op('act', lambda e, m=m, bank=bank: e.activation(KM[:, m, :], ps[bank][:, 0:256], AF.Copy), reads=[PK(bank)], writes=['KM'])
        spans, rk = R.take(64, span=4)
        for mb in range(2):
            for hf in range(2):
                bank = (mb * 2 + hf) % 2
                mm(ps[bank][:], [(mnT[:, kc, mb * 128:(mb + 1) * 128], spans[kc * 2 + hf]) for kc in range(KC)], reads=rk + ['mnT'], writes=[PK(bank)])
                op('dve', lambda e, mb=mb, hf=hf, bank=bank: e.tensor_copy(VM[:, mb, hf * 512:(hf + 1) * 512], ps[bank][:]), reads=[PK(bank)], writes=['VM'])
        P.barrier()

        def rms_rstd(srcs, nfeat):
            n = len(srcs)
            for i, (ap, keys) in enumerate(srcs):
                b = i % 2
                op('act', lambda e, ap=ap, b=b: e.activation(sq[:, b, :], ap, AF.Square), reads=keys, writes=[('sq', b)])
                mm(ps[7][:], [(onesb[:], sq[:, b, :])], reads=[('sq', b)], writes=[PK(7)], start=(i == 0), stop=(i == n - 1))
            op('act', lambda e: e.activation(rstd[:], ps[7][:], AF.Ln, bias=cvec[:, 0:1], scale=1.0 / nfeat), reads=[PK(7)], writes=['rstd'])
            op('act', lambda e: e.activation(rstd[:], rstd[:], AF.Exp, scale=-0.5), reads=['rstd'], writes=['rstd'])

        def pre_norm(gain, dst_base):
            rms_rstd([(xs[:, kc, :], [XK(kc)]) for kc in range(KC)], D)
            for kc in range(KC):
                eng = 'dve'
                op(eng, lambda e, kc=kc: e.scalar_tensor_tensor(ACT_[:, dst_base + kc, :], xs[:, kc, :], gain[:, kc:kc + 1], rstd[:], ALU.mult, ALU.mult),
                   reads=[XK(kc), 'rstd'], writes=[AK(dst_base + kc)])

        def post_norm_residual(osrc, okeys, gain, final=False):
            rms_rstd([(osrc(kc), okeys(kc)) for kc in range(KC)], D)
            for kc in range(KC):
                op('dve', lambda e, kc=kc: e.scalar_tensor_tensor(osrc(kc), osrc(kc), gain[:, kc:kc + 1], rstd[:], ALU.mult, ALU.mult),
                   reads=okeys(kc) + ['rstd'], writes=okeys(kc))
                if final:
                    op('pool', lambda e, kc=kc: e.tensor_tensor(osrc(kc), xs[:, kc, :], osrc(kc), ALU.add),
                       reads=okeys(kc) + [XK(kc)], writes=okeys(kc))
                else:
                    op('pool', lambda e, kc=kc: e.tensor_tensor(xs[:, kc, :], xs[:, kc, :], osrc(kc), ALU.add),
                       reads=okeys(kc) + [XK(kc)], writes=[XK(kc)])

        def proj8(src_base, evac):
            for m in range(8):
                blks, rk = R.take(8)
                bank = m % 2
                mm(ps[bank][:], [(blks[kc], ACT_[:, src_base + kc, :]) for kc in range(KC)],
                   reads=rk + [AK(src_base + kc) for kc in range(KC)], writes=[PK(bank)])
                evac(m, bank)

        def evac_copy(dst, dkeys, bank, i, scale=None):
            if i % 2 == 0:
                if scale is None:
                    op('act', lambda e: e.activation(dst, ps[bank][:], AF.Copy), reads=[PK(bank)], writes=dkeys)
                else:
                    op('act', lambda e: e.activation(dst, ps[bank][:], AF.Copy, scale=scale), reads=[PK(bank)], writes=dkeys)
            else:
                if scale is None:
                    op('dve', lambda e: e.tensor_copy(dst, ps[bank][:]), reads=[PK(bank)], writes=dkeys)
                else:
                    op('dve', lambda e: e.tensor_scalar(dst, ps[bank][:], scale, None, ALU.mult), reads=[PK(bank)], writes=dkeys)

        for ti in range(nt):
            T0 = ti * TT
            op('pool', lambda e, T0=T0: e.dma_start(out=xs, in_=xT_v[:, :, T0:T0 + TT]), writes=[XK(kc) for kc in range(KC)], chan='XL')
            pre_norm(g_mixpre, 0)
            hkeys = [AK(kc) for kc in range(KC)]
            def fg1():
                mm(ps[5][0:8, :], [(wfb[:, kc, :], ACT_[:, kc, :]) for kc in range(KC)], reads=hkeys + ['wfb'], writes=[PK(5)])
                op('act', lambda e: e.activation(ft1[:], ps[5][0:8, :], AF.Exp, bias=cvec[0:8, 1:2], scale=-1.0), reads=[PK(5)], writes=['ft1'])
                op('act', lambda e: e.activation(ft1[:], ft1[:], AF.Ln, bias=1.0, scale=1.0), reads=['ft1'], writes=['ft1'])
                op('dve', lambda e: e.tensor_tensor_scan(fabs_[:], onesf[0:8, 0:1].to_broadcast([8, 512]), ft1[:], fref[:, 0:1], ALU.mult, ALU.subtract),
                   reads=['ft1', 'fref'], writes=['fabs'])
                op('dve', lambda e: e.tensor_scalar(ft1[:], fabs_[:], fref[:, 0:1], None, ALU.subtract), reads=['fabs', 'fref'], writes=['ft1'])
                op('dve', lambda e: e.tensor_copy(ga[0:8, :], ft1[:]), reads=['ft1'], writes=['ga'])
                op('dve', lambda e: e.tensor_copy(ga[64:72, :], ft1[:]), reads=['ft1'], writes=['ga'])
                op('dve', lambda e: e.tensor_tensor(ft1[:], ft1[:], ga[0:8, :], ALU.subtract), reads=['ft1', 'ga'], writes=['ft1'])
                op('dve', lambda e: e.tensor_copy(gml[:, 0, :], ft1[:]), reads=['ft1'], writes=['gml'])
                op('dve', lambda e: e.tensor_tensor(ft1[:], ft1[:], gml[:, 0, :], ALU.subtract), reads=['ft1', 'gml'], writes=['ft1'])
                op('dve', lambda e: e.tensor_copy(gml[:, 1, :], ft1[:]), reads=['ft1'], writes=['gml'])
                op('pool', lambda e: e.dma_start(out=ga[8:16, :], in_=gml[:, 0, :]), reads=['gml'], writes=['ga'], chan='GA0')
                op('pool', lambda e: e.dma_start(out=ga[16:24, :], in_=gml[:, 1, :]), reads=['gml'], writes=['ga'], chan='GA1')
                op('pool', lambda e: e.dma_start(out=ga[72:80, :], in_=gml[:, 0, :]), reads=['gml'], writes=['ga'], chan='GA2')
                op('pool', lambda e: e.dma_start(out=ga[80:88, :], in_=gml[:, 1, :]), reads=['gml'], writes=['ga'], chan='GA3')

            def fg2():
                for jb in range(4):
                    mm(ps[4][:, jb * 8:(jb + 1) * 8], [(fabs_[:, jb * 128:(jb + 1) * 128], identf[0:8, 0:8])], reads=['fabs'], writes=[PK(4)])
                op('dve', lambda e, ti=ti: e.tensor_scalar(negF[:, 4 * ti:4 * ti + 4, :], ps[4][:, 0:32].rearrange("p (a b) -> p a b", b=8), -1.0, None, ALU.mult),
                   reads=[PK(4)], writes=['negF'])

            def fg3():
                op('dve', lambda e: e.tensor_scalar(dg8[:], identf[0:8, 0:8], fref[:, 0:1], None, ALU.mult), reads=['fref'], writes=['dg8'])
                mm(ps[4][:, 64:72], [(onesf[0:8, 0:128], dg8[:])], reads=['dg8'], writes=[PK(4)])
                op('dve', lambda e: e.tensor_copy(frbc[:], ps[4][:, 64:72]), reads=[PK(4)], writes=['frbc'])
                for kb in range(4 * ti + 4):
                    op('dve', lambda e, kb=kb: e.tensor_tensor(biasT[:, kb, :], negF[:, kb, :], frbc[:], ALU.add), reads=['negF', 'frbc'], writes=['biasT'])
                op('dve', lambda e: e.tensor_copy(fref[:], fabs_[:, 511:512]), reads=['fabs'], writes=['fref'])

            fg1()
            for m in range(12):
                blks, rk = R.take(8)
                bank = m % 2
                mm(ps[bank][:], [(blks[kc], ACT_[:, kc, :]) for kc in range(KC)], reads=rk + hkeys, writes=[PK(bank)])
                if m < 4:
                    evac_copy(ACT_[:, 8 + m, :], [AK(8 + m)], bank, m)
                elif m < 8:
                    evac_copy(ACT_[:, 8 + m, :], [AK(8 + m)], bank, m, scale=0.125)
                else:
                    evac_copy(KT[:, m - 8, T0:T0 + TT], [('KT', m - 8, ti)], bank, m)
                if m == 3:
                    fg2()
                if m == 7:
                    fg3()

            def load_E(pt):
                op('sp', lambda e, pt=pt: e.dma_start(out=Ebuf[:, pt % 3, :, :], in_=escr[:, pt, :, :]), reads=[('escr', pt)], writes=[('E', pt % 3)], chan='EL%d' % (pt % 3))

            def s5A(pt):
                ut = pt // 4
                uk = [AK(8 + ut)]
                mm(ps[0][:], [(Btab[:, pt, 0, :], ACT_[:, 8 + ut, :])], reads=uk, writes=[PK(0)])
                mm(ps[1][:], [(Btab[:, pt, 1, :], ACT_[:, 8 + ut, :])], reads=uk, writes=[PK(1)])

            def s5set(pt):
                par = pt % 2
                st = pt % 3
                if st < 2:
                    S = [SC[:, 4 * st + i, :] for i in range(4)]
                    K_ = [SK(4 * st + i) for i in range(4)]
                else:
                    S = [s5x[:, i, :] for i in range(4)]
                    K_ = [[('s5x', i)] for i in range(4)]
                return (par, S, K_, Ebuf[:, st, 0, :], Ebuf[:, st, 1, :], [('E', st)])

            def s5B1(pt):
                par, S, K, c_, s_, EK = s5set(pt)
                op('dve', lambda e: e.tensor_tensor(S[0], ps[0][:], c_, ALU.mult), reads=[PK(0)] + EK, writes=K[0])
                op('dve', lambda e: e.tensor_tensor(S[1], ps[1][:], s_, ALU.mult), reads=[PK(1)] + EK, writes=K[1])
                op('dve', lambda e: e.tensor_tensor(S[2], ps[1][:], c_, ALU.mult), reads=[PK(1)] + EK, writes=K[2])
                op('dve', lambda e: e.tensor_tensor(S[3], ps[0][:], s_, ALU.mult), reads=[PK(0)] + EK, writes=K[3])

            def s5B2(pt):
                par, S, K, c_, s_, EK = s5set(pt)
                op('pool', lambda e: e.tensor_tensor(S[0], S[0], S[1], ALU.add), reads=K[0] + K[1], writes=K[0])
                op('pool', lambda e: e.tensor_tensor(S[2], S[2], S[3], ALU.subtract), reads=K[2] + K[3], writes=K[2])

            def s5B3(pt):
                par, S, K, c_, s_, EK = s5set(pt)
                op('dve', lambda e: e.tensor_tensor_scan(S[1], Rr[:, pt:pt + 1].to_broadcast([128, 512]), S[0], sinit[:, 0, pt:pt + 1], ALU.mult, ALU.add),
                   reads=K[0] + ['sinit'], writes=K[1])
                op('dve', lambda e: e.tensor_tensor_scan(S[3], Rr[:, pt:pt + 1].to_broadcast([128, 512]), S[2], sinit[:, 1, pt:pt + 1], ALU.mult, ALU.add),
                   reads=K[2] + ['sinit'], writes=K[3])
                op('dve', lambda e: e.tensor_copy(zl[:, 0, pt:pt + 1], S[1][:, 511:512]), reads=K[1], writes=['zl'])
                op('dve', lambda e: e.tensor_copy(zl[:, 1, pt:pt + 1], S[3][:, 511:512]), reads=K[3], writes=['zl'])

            def s5C1(pt):
                par, S, K, c_, s_, EK = s5set(pt)
                op('pool', lambda e: e.tensor_tensor(S[0], S[1], c_, ALU.mult), reads=K[1] + EK, writes=K[0])
                op('pool', lambda e: e.tensor_tensor(S[2], S[3], s_, ALU.mult), reads=K[3] + EK, writes=K[2])

            def s5C1b(pt):
                par, S, K, c_, s_, EK = s5set(pt)
                op('pool', lambda e: e.tensor_tensor(S[3], S[3], c_, ALU.mult), reads=K[3] + EK, writes=K[3])
                op('pool', lambda e: e.tensor_tensor(S[1], S[1], s_, ALU.mult), reads=K[1] + EK, writes=K[1])
                if pt + 3 < 16:
                    load_E(pt + 3)

            def s5C2a(pt):
                par, S, K, c_, s_, EK = s5set(pt)
                op('dve', lambda e: e.tensor_tensor(xr[:, par, 0, :], S[0], S[2], ALU.subtract), reads=K[0] + K[2], writes=[('xr', par, 0)])

            def s5C2b(pt):
                par, S, K, c_, s_, EK = s5set(pt)
                op('dve', lambda e: e.tensor_tensor(xr[:, par, 1, :], S[3], S[1], ALU.add), reads=K[3] + K[1], writes=[('xr', par, 1)])

            def s5D(pt):
                par = pt % 2
                ut = pt // 4
                mm(ps[2][:], [(Ctab[:, pt, 0, :], xr[:, par, 0, :]), (Ctab[:, pt, 1, :], xr[:, par, 1, :])],
                   reads=[('xr', par, 0), ('xr', par, 1)], writes=[PK(2)], start=(pt % 4 == 0), stop=(pt % 4 == 3))
                if pt % 4 == 3:
                    Y = rstd[:]
                    W = sq[:].rearrange("p a b -> p (a b)").bitcast(F32)
                    YK, WK = ['rstd'], [('sq', 0), ('sq', 1)]
                    op('dve', lambda e: e.scalar_tensor_tensor(Y, ACT_[:, 8 + ut, :], d_skip[:, ut:ut + 1], ps[2][:], ALU.mult, ALU.add),
                       reads=[PK(2), AK(8 + ut)], writes=YK)
                    op('dve', lambda e: e.tensor_tensor(W, Y, Y, ALU.mult), reads=YK, writes=WK)
                    op('dve', lambda e: e.tensor_scalar(W, W, 0.044715, 1.0, ALU.mult, ALU.add), reads=WK, writes=WK)
                    op('dve', lambda e: e.tensor_tensor(W, W, Y, ALU.mult), reads=WK + YK, writes=WK)
                    gelq.append([1,
                                 lambda: op('act', lambda e: e.activation(W, W, AF.Tanh, scale=float(np.sqrt(2.0 / np.pi))), reads=WK, writes=WK),
                                 lambda: op('dve', lambda e: e.scalar_tensor_tensor(W, W, 1.0, Y, ALU.add, ALU.mult), reads=WK + YK, writes=WK),
                                 lambda ut=ut: op('act', lambda e: e.activation(ACT_[:, ut, :], W, AF.Copy, scale=0.5), reads=WK, writes=[AK(ut)])])

            gelq = []

            def gelu_act():
                for g in gelq:
                    if g[0] == 1:
                        g[1]()
                        g[0] = 2
                    elif g[0] == 3:
                        g[3]()
                        g[0] = 4

            def gelu_dve():
                for g in gelq:
                    if g[0] == 2:
                        g[2]()
                        g[0] = 3

            nkb = 4 * ti + 4
            YFt = [SC[:, 8, :], SC[:, 9, :], SC[:, 10, :], arena[:, 9728 + 1024:9728 + 1536]]
            YFk = [SK(8), SK(9), SK(10), [AK(4), AK(5)]]
            fox_items = []
            rot = [0, 0]
            for m_ in range(4):
                hA, hB = 2 * m_, 2 * m_ + 1
                qk = AK(12 + m_)

                def stageA(kb, m_=m_, hA=hA, hB=hB, qk=qk):
                    sa = 3 + (rot[0] % 3)
                    sb = 3 + ((rot[0] + 1) % 3)
                    rot[0] += 2
                    ra = rot[1] % 4
                    rb = (rot[1] + 1) % 4
                    rot[1] += 2
                    intile = kb >= 4 * ti
                    c0 = 128 * (kb - 4 * ti) if intile else 0

                    def fn(e):
                        e.matmul(ps[sa][:, c0:512], KT[0:64, m_, kb * 128:(kb + 1) * 128], ACT_[0:64, 12 + m_, c0:512], start=True, stop=False)
                        e.matmul(ps[sb][:, c0:512], KT[64:128, m_, kb * 128:(kb + 1) * 128], ACT_[64:128, 12 + m_, c0:512], start=True, stop=False)
                        e.matmul(ps[sa][:, c0:512], selb[0:24, hA, :], ga[0:24, c0:512], start=False, stop=not intile)
                        i2 = e.matmul(ps[sb][:, c0:512], selb[64:88, hB, :], ga[64:88, c0:512], start=False, stop=not intile)
                        if intile:
                            e.matmul(ps[sa][:, c0:c0 + 128], identb[:], negmb[:], start=False, stop=True)
                            i2 = e.matmul(ps[sb][:, c0:c0 + 128], identb[:], negmb[:], start=False, stop=True)
                        return i2
                    op('pe', fn, reads=[('KT', m_, kb // 4), qk, 'ga'], writes=[PK(sa), PK(sb)])
                    for (sx, rx, hx) in ((sa, ra, hA), (sb, rb, hB)):
                        op('act', lambda e, sx=sx, rx=rx, hx=hx: e.activation(pT[:, rx, c0:512], ps[sx][:, c0:512], AF.Exp, bias=biasT[:, kb, hx:hx + 1], scale=1.0),
                           reads=[PK(sx), 'biasT'], writes=[('pT', rx)])
                    return ra, rb, c0

                def stageB(kb, st, hA=hA, hB=hB):
                    ra, rb, c0 = st
                    first, last = (kb == 0), (kb == nkb - 1)

                    def fn(e):
                        e.matmul(ps[6][0:64, c0:512], VC[:, kb, hA, :], pT[:, ra, c0:512], start=first, stop=last, tile_position=(0, 0))
                        e.matmul(ps[6][64:128, c0:512], VC[:, kb, hB, :], pT[:, rb, c0:512], start=first, stop=last, tile_position=(0, 64))
                        e.matmul(ps[7][0:64, c0:512], onesb[:, 0:64], pT[:, ra, c0:512], start=first, stop=last, tile_position=(0, 0))
                        return e.matmul(ps[7][64:128, c0:512], onesb[:, 64:128], pT[:, rb, c0:512], start=first, stop=last, tile_position=(0, 64))
                    op('pe', fn, reads=[('pT', ra), ('pT', rb), ('VC', kb)], writes=[PK(6), PK(7)])

                def fin(m_=m_):
                    op('act', lambda e: e.activation(rl[:], ps[7][:], AF.Copy), reads=[PK(7)], writes=['rl'])
                    op('act', lambda e: e.activation(rlb[:], ps[6][:], AF.Copy), reads=[PK(6)], writes=['rlb'])
                    op('dve', lambda e: e.reciprocal(rl[:], rl[:]), reads=['rl'], writes=['rl'])
                    op('dve', lambda e: e.tensor_tensor(YFt[m_], rlb[:], rl[:], ALU.mult), reads=['rlb', 'rl'], writes=YFk[m_])

                state = {}

                def item_first(stageA=stageA, state=state):
                    state[0] = stageA(0)

                def item_mid(kb, stageA=stageA, stageB=stageB, state=state):
                    if kb + 1 < nkb:
                        state[kb + 1] = stageA(kb + 1)
                    stageB(kb, state[kb])

                fox_items.append(item_first)
                if m_ > 0:
                    fox_items.append(prev_fin[0])
                for kb in range(nkb):
                    fox_items.append(lambda kb=kb, item_mid=item_mid: item_mid(kb))
                prev_fin = [fin]
            fox_items.append(prev_fin[0])

            def s5_slot(k):
                if ok(k):
                    s5A(k)
                    s5B1(k)
                    s5B2(k)
                if ok(k - 2):
                    s5C2a(k - 2)
                    s5C2b(k - 2)
                if ok(k - 1):
                    s5B3(k - 1)
                    s5C1(k - 1)
                    s5C1b(k - 1)
                gelu_dve()
                if ok(k - 3):
                    s5D(k - 3)

            ok = lambda p: 0 <= p < 16
            load_E(0)
            load_E(1)
            load_E(2)
            spans, rk = R.take(32, span=4)

            def vproj(tb):
                bank = 3 + tb % 2
                mm(ps[bank][:], [(ACT_[:, kc, tb * 128:(tb + 1) * 128], spans[kc]) for kc in range(KC)], reads=rk + hkeys, writes=[PK(bank)])
                blk = 4 * ti + tb
                op('dve' if tb % 2 else 'act',
                   (lambda e, blk=blk, bank=bank: e.tensor_copy(VC[:, blk, :, :], ps[bank][:].rearrange("p (h d) -> p h d", d=64))) if tb % 2 else
                   (lambda e, blk=blk, bank=bank: e.activation(VC[:, blk, :, :], ps[bank][:].rearrange("p (h d) -> p h d", d=64), AF.Copy)),
                   reads=[PK(bank)], writes=[('VC', blk)] + ([('stage', 0), ('stage', 1)] if 4 <= blk < 20 else []))
            NSL = 20
            per = -(-len(fox_items) // NSL)
            fi = 0
            for k in range(NSL):
                s5_slot(k)
                if k == 0:
                    vproj(0)
                    vproj(1)
                if k == 1:
                    vproj(2)
                    vproj(3)
                for _ in range(per):
                    if fi < len(fox_items):
                        fox_items[fi]()
                        fi += 1
                gelu_act()
            while fi < len(fox_items):
                fox_items[fi]()
                fi += 1
            for _ in range(3):
                gelu_dve()
                gelu_act()
            assert all(g[0] == 4 for g in gelq)
            c5, s5 = E512[:, 0, :], E512[:, 1, :]
            op('dve', lambda e: e.tensor_tensor(ztmp[:, 0, :], zl[:, 0, :], c5, ALU.mult), reads=['zl'], writes=['ztmp'])
            op('dve', lambda e: e.tensor_tensor(ztmp[:, 1, :], zl[:, 1, :], s5, ALU.mult), reads=['zl'], writes=['ztmp'])
            op('dve', lambda e: e.tensor_tensor(ztmp[:, 2, :], zl[:, 1, :], c5, ALU.mult), reads=['zl'], writes=['ztmp'])
            op('dve', lambda e: e.tensor_tensor(ztmp[:, 3, :], zl[:, 0, :], s5, ALU.mult), reads=['zl'], writes=['ztmp'])
            op('dve', lambda e: e.tensor_tensor(sinit[:, 0, :], ztmp[:, 0, :], ztmp[:, 1, :], ALU.subtract), reads=['ztmp'], writes=['sinit'])
            op('dve', lambda e: e.tensor_tensor(sinit[:, 1, :], ztmp[:, 2, :], ztmp[:, 3, :], ALU.add), reads=['ztmp'], writes=['sinit'])
            for m in range(4):
                blks, rk = R.take(4)
                bank = m % 2
                mm(ps[bank][:], [(blks[kc], ACT_[:, kc, :]) for kc in range(4)], reads=rk + [AK(kc) for kc in range(4)], writes=[PK(bank)])
                op('act', lambda e, m=m, bank=bank: e.activation(SC[:, 4, :], ps[bank][:], AF.Sigmoid, bias=glu_b[:, m:m + 1], scale=1.0), reads=[PK(bank)], writes=SK(4))
                op('dve', lambda e, m=m: e.tensor_tensor(SC[:, m, :], ACT_[:, m, :], SC[:, 4, :], ALU.mult), reads=SK(4) + [AK(m)], writes=SK(m))
            rms_rstd([(SC[:, m, :], SK(m)) for m in range(4)], 512)
            for m in range(4):
                op('dve', lambda e, m=m: e.scalar_tensor_tensor(ACT_[:, m, :], SC[:, m, :], g_ssm[:, m:m + 1], rstd[:], ALU.mult, ALU.mult),
                   reads=SK(m) + ['rstd'], writes=[AK(m)])
            rms_rstd([(YFt[m], YFk[m]) for m in range(4)], 512)
            for m in (3, 0, 1, 2):
                op('dve', lambda e, m=m: e.scalar_tensor_tensor(ACT_[:, 4 + m, :], YFt[m], g_fox[:, m:m + 1], rstd[:], ALU.mult, ALU.mult),
                   reads=YFk[m] + ['rstd'], writes=[AK(4 + m)])

            proj8(0, lambda m, bank: evac_copy(SC[:, m, :], SK(m), bank, m))
            post_norm_residual(lambda kc: SC[:, kc, :], SK, g_mixpost)

            pre_norm(g_xapre, 8)
            proj8(8, lambda m, bank: evac_copy(ACT_[:, m, :], [AK(m)], bank, m, scale=1.0 / 16.0))
            for hx in range(4):
                c0_, c1_ = 2 * hx, 2 * hx + 1
                par = hx % 2
                ob = (4, 5, 6) if par == 0 else (0, 1, 7)
                rlt, rlk = (rl, 'rl') if par == 0 else (rlb, 'rlb')
                for mb in range(2):
                    sb = 2 + mb
                    pi = 2 * par + mb
                    mm(ps[sb][:], [(KM[:, c0_, mb * 128:(mb + 1) * 128], ACT_[:, c0_, :]), (KM[:, c1_, mb * 128:(mb + 1) * 128], ACT_[:, c1_, :])],
                       reads=[AK(c0_), AK(c1_)], writes=[PK(sb)])
                    op('act', lambda e, sb=sb, pi=pi: e.activation(pT[:, pi, :], ps[sb][:], AF.Exp), reads=[PK(sb)], writes=[('pT', pi)])
                for mb in range(2):
                    pi = 2 * par + mb
                    for dc in range(2):
                        mm(ps[ob[dc]][:], [(VM[:, mb, (2 * hx + dc) * 128:(2 * hx + dc + 1) * 128], pT[:, pi, :])], reads=[('pT', pi)], writes=[PK(ob[dc])],
                           start=(mb == 0), stop=(mb == 1))
                    mm(ps[ob[2]][:], [(onesb[:], pT[:, pi, :])], reads=[('pT', pi)], writes=[PK(ob[2])], start=(mb == 0), stop=(mb == 1))
                op('dve', lambda e, rlt=rlt, ob=ob: e.reciprocal(rlt[:], ps[ob[2]][:]), reads=[PK(ob[2])], writes=[rlk])
                for dc in range(2):
                    op('dve', lambda e, dc=dc, hx=hx, rlt=rlt, ob=ob: e.tensor_tensor(ACT_[:, 8 + 2 * hx + dc, :], ps[ob[dc]][:], rlt[:], ALU.mult),
                       reads=[PK(ob[dc]), rlk], writes=[AK(8 + 2 * hx + dc)])
            proj8(8, lambda m, bank: evac_copy(SC[:, m, :], SK(m), bank, m))
            post_norm_residual(lambda kc: SC[:, kc, :], SK, g_xapost)

            pre_norm(g_ffnpre, 0)
            for m in range(HC):
                blks, rk = R.take(16)
                par = m % 2
                bg, bu = 2 + 2 * par, 3 + 2 * par
                mm(ps[bg][:], [(blks[kc], ACT_[:, kc, :]) for kc in range(KC)], reads=rk + hkeys, writes=[PK(bg)])
                mm(ps[bu][:], [(blks[8 + kc], ACT_[:, kc, :]) for kc in range(KC)], reads=rk + hkeys, writes=[PK(bu)])
                tq_, tk_ = (rl, 'rl') if par == 0 else (rlb, 'rlb')
                op('act', lambda e, tq_=tq_, bg=bg: e.activation(tq_[:], ps[bg][:], AF.Silu), reads=[PK(bg)], writes=[tk_])
                op('dve', lambda e, tq_=tq_, bu=bu, m=m: e.tensor_tensor(hid[:, m, :], tq_[:], ps[bu][:], ALU.mult),
                   reads=[PK(bu), tk_], writes=[HK(m)])
            for m in range(8):
                blks, rk = R.take(HC)
                bank = m % 2
                mm(ps[bank][:], [(blks[kc], hid[:, kc, :]) for kc in range(HC)], reads=rk + [HK(kc) for kc in range(HC)], writes=[PK(bank)])
                evac_copy(O3[:, m, :], O3K(m), bank, m)
            post_norm_residual(lambda kc: O3[:, kc, :], O3K, g_ffnpost, final=True)
            op('sp', lambda e, T0=T0: e.dma_start(out=yT_v[:, :, T0:T0 + TT], in_=O3), reads=[AK(c) for c in range(16)], writes=[('yT', ti)], chan='XO')

        op('sp', None, reads=[('yT', ti) for ti in range(nt)])
        with nc.Block() as block:
            P.emit(block)
    return nc


def _blk(W, kc, m):
    return W[kc * 128:(kc + 1) * 128, m * 128:(m + 1) * 128]


def _weight_stream(w_in, glu_w, w_out, xa_wq, xa_wo, w_gate, w_up, w_down, xa_wkv):
    blocks = []
    wi = w_in[:, :1536]
    for m in range(12):
        for kc in range(8):
            blocks.append(_blk(wi, kc, m))
    wv = w_in[:, 1536:2048]
    for kc in range(8):
        for j in range(4):
            blocks.append(_blk(wv, kc, j))
    for m in range(4):
        for kc in range(4):
            blocks.append(_blk(glu_w, kc, m))
    for W in (w_out, xa_wq, xa_wo):
        for m in range(8):
            for kc in range(8):
                blocks.append(_blk(W, kc, m))
    for m in range(HC):
        for kc in range(8):
            blocks.append(_blk(w_gate, kc, m))
        for kc in range(8):
            blocks.append(_blk(w_up, kc, m))
    for m in range(8):
        for kc in range(HC):
            blocks.append(_blk(w_down, kc, m))
    assert len(blocks) == NBT
    wk, wvv = xa_wkv[:, :1024], xa_wkv[:, 1024:]
    for m in range(8):
        for kc in range(8):
            blocks.append(_blk(wk, kc, m))
    for kc in range(8):
        for j in range(8):
            blocks.append(_blk(wvv, kc, j))
    assert len(blocks) == NBT + NBP
    return np.ascontiguousarray(np.concatenate(blocks, axis=1), dtype=np.float32)


def _fm(v, n):
    return np.asarray(v, np.float32).reshape(n, 128).T


def _prep_shared(inp):
    f = lambda k: np.asarray(inp[k], np.float32)
    wst = _weight_stream(f("w_in"), f("ssm_glu_w"), f("w_out"), f("xa_wq"), f("xa_wo"), f("w_gate"), f("w_up"), f("w_down"), f("xa_wkv"))
    smp = np.zeros((128, SMP_N), np.float32)
    s_idx = np.arange(128)[:, None]
    t_idx = np.arange(128)[None, :]
    smp[:, 0:128] = np.where(s_idx <= t_idx, 0.0, -30000.0)
    smp[:, 128:256] = np.eye(128, dtype=np.float32)
    pvs = [(_fm(f("mix_pre_g"), 8)), _fm(f("ssm_out_g"), 4), _fm(f("fox_out_g"), 4), _fm(f("mix_post_g"), 8), _fm(f("xa_pre_g"), 8),
           _fm(f("mem_g"), 8), _fm(f("xa_post_g"), 8), _fm(f("ffn_pre_g"), 8), _fm(f("ffn_post_g"), 8), _fm(f("ssm_d"), 4), _fm(f("ssm_glu_b"), 4)]
    smp[:, 256:328] = np.concatenate(pvs, axis=1)
    smp[0:8, 328] = f("fox_f_bias")
    smt = np.zeros((128, SMT_N), np.float32)
    smt[:, 0:512] = np.arange(512, dtype=np.float32)[None, :]
    sel = np.zeros((24, 8, 128), np.float32)
    for r in range(24):
        sel[r, r % 8, :] = 1.0
    smt[0:24, 512:1536] = sel.reshape(24, 1024)
    smt[64:88, 512:1536] = sel.reshape(24, 1024)
    wf = f("w_in")[:, 2048:2056]
    smt[:, 1536:1600] = wf.reshape(8, 128, 8).transpose(1, 0, 2).reshape(128, 64)
    o = 1600

    def gl(a):
        a = np.asarray(a, np.float32)
        tail = a.shape[2:]
        a = a.reshape(16, 2, 64, *tail)
        a = np.moveaxis(a, 0, 2)
        return a.reshape(128, 16, *tail)
    smt[:, o:o + 16] = gl(f("ssm_a_re"))
    smt[:, o + 16:o + 32] = gl(f("ssm_a_im"))
    smt[:, o + 32:o + 48] = gl(np.repeat(f("ssm_log_dt")[:, None], 64, axis=1))
    smt[:, o + 48:o + 304] = gl(f("ssm_b_re")).reshape(128, 256)
    smt[:, o + 304:o + 560] = gl(f("ssm_b_im")).reshape(128, 256)
    smt[:, o + 560:o + 816] = gl(np.transpose(f("ssm_c_re"), (0, 2, 1))).reshape(128, 256)
    smt[:, o + 816:o + 1072] = gl(np.transpose(f("ssm_c_im"), (0, 2, 1))).reshape(128, 256)
    return wst, smp, smt


_NC_CACHE = {}


def kernel(**inputs):
    x = np.asarray(inputs["x"], np.float32)
    mem = np.asarray(inputs["mem"], np.float32)
    B = x.shape[0]
    nt = x.shape[1] // TT
    wst, smp, smt = _prep_shared(inputs)
    if nt not in _NC_CACHE:
        _NC_CACHE[nt] = build(nt)
    nc = _NC_CACHE[nt]
    in_maps = []
    for b in range(B):
        in_maps.append({"xT": np.ascontiguousarray(x[b].T), "memT": np.ascontiguousarray(mem[b].T),
                        "wst": wst, "smp": smp, "smt": smt})
    res = run_bass_kernel_spmd(nc, in_maps, core_ids=list(range(B)))
    out = np.stack([np.ascontiguousarray(r["yT"].T) for r in res.results], axis=0)
    return out.astype(np.float32)
```

```python
import numpy as np
from contextlib import ExitStack
import concourse.bass as bass
import concourse.mybir as mybir
from concourse.bass_utils import run_bass_kernel_spmd

F32 = mybir.dt.float32
BF16 = mybir.dt.bfloat16
I32 = mybir.dt.int32
ALU = mybir.AluOpType
AF = mybir.ActivationFunctionType
ENGS = ['pe', 'act', 'dve', 'pool', 'sp']

D = 1024
KC = 8
TT = 512
NT = 8
HC = 22
NM = 256
NBT = 864
NBP = 128
CH = 16
NSLOT = 5
NCH_T = NBT // CH
NCH_P = NBP // CH
EPS = 1e-6
TWO_PI = float(2 * np.pi)
SMP_N = 329 + 64
SMT_N = 2672


class Prog:
    def __init__(self, nc, es):
        self.nc = nc
        self.es = es
        self.ops = {e: [] for e in ENGS}
        self.cnt = {}
        self.semh = {}
        self.lastw = {}
        self.readers = {}
        self.seen = {e: {} for e in ENGS}
        for e in ENGS:
            self.newsem('S_' + e)

    def newsem(self, name):
        if name not in self.semh:
            self.semh[name] = self.es.enter_context(self.nc.semaphore(name))
            self.cnt[name] = 0

    def op(self, eng, fn, reads=(), writes=(), chan=None):
        need = {}

        def add(tok, raw):
            if tok is None:
                return
            sem, val, teng, isdma = tok
            if (not isdma) and teng == eng:
                if eng == 'pe':
                    return
            if need.get(sem, 0) < val:
                need[sem] = val

        for k in reads:
            add(self.lastw.get(k), True)
        for k in writes:
            add(self.lastw.get(k), False)
            for t in self.readers.get(k, {}).values():
                add(t, False)
        waits = []
        for sem, val in need.items():
            if self.seen[eng].get(sem, 0) < val:
                self.seen[eng][sem] = val
                waits.append((sem, val))
        if fn is None:
            self.ops[eng].append((waits, None, None, 0))
            return None
        if chan is not None:
            self.newsem(chan)
            sem, inc, isdma = chan, 16, True
        else:
            sem, inc, isdma = 'S_' + eng, 1, False
        self.cnt[sem] += inc
        tok = (sem, self.cnt[sem], eng, isdma)
        for k in writes:
            self.lastw[k] = tok
            self.readers[k] = {}
        for k in reads:
            self.readers.setdefault(k, {})[sem] = tok
        self.ops[eng].append((waits, fn, sem, inc))
        return tok

    def barrier(self):
        for e in ENGS:
            waits = []
            for sem, val in self.cnt.items():
                if val > 0 and self.seen[e].get(sem, 0) < val and sem != 'S_' + e:
                    self.seen[e][sem] = val
                    waits.append((sem, val))
            self.ops[e].append((waits, None, None, 0))

    def emit(self, block):
        decos = {'pe': block.tensor, 'act': block.scalar, 'dve': block.vector,
                 'pool': block.gpsimd, 'sp': block.sync}
        for e in ENGS:
            ops = self.ops[e]

            def body(engh, ops=ops):
                for waits, fn, sem, inc in ops:
                    for ws, wv in waits:
                        engh.wait_ge(self.semh[ws], wv)
                    if fn is not None:
                        ins = fn(engh)
                        ins.then_inc(self.semh[sem], inc)

            decos[e](body)


def build(nt=NT):
    S_ = nt * TT
    NBLKS = 4 * nt
    nc = bass.Bass("TRN2", target_bir_lowering=False)
    xT = nc.dram_tensor("xT", [D, S_], F32, kind="ExternalInput").ap()
    memT = nc.dram_tensor("memT", [D, NM], F32, kind="ExternalInput").ap()
    wst = nc.dram_tensor("wst", [128, (NBT + NBP) * 128], F32, kind="ExternalInput").ap()
    smp_d = nc.dram_tensor("smp", [128, SMP_N], F32, kind="ExternalInput").ap()
    smt_d = nc.dram_tensor("smt", [128, SMT_N], F32, kind="ExternalInput").ap()
    yT = nc.dram_tensor("yT", [D, S_], F32, kind="ExternalOutput").ap()
    wscr = nc.dram_tensor("wscr", [128, (NBT + NBP) * 128], BF16).ap()
    escr = nc.dram_tensor("escr", [128, 16 * 2 * 512], BF16).ap().rearrange("p (a c b) -> p a c b", c=2, b=512)
    xT_v = xT.rearrange("(kc p) t -> p kc t", p=128)
    yT_v = yT.rearrange("(kc p) t -> p kc t", p=128)
    memT_v = memT.rearrange("(kc p) t -> p kc t", p=128)

    with ExitStack() as es:
        P = Prog(nc, es)
        T = lambda name, shape, dt: es.enter_context(nc.sbuf_tensor(name, shape, dt))
        op = P.op

        smp = T("smp_s", [128, SMP_N], F32)
        negm = smp[:, 0:128]
        identf = smp[:, 128:256]
        pv = smp[:, 256:328]
        fb = smp[:, 328:329]
        g_mixpre, g_ssm, g_fox, g_mixpost = pv[:, 0:8], pv[:, 8:12], pv[:, 12:16], pv[:, 16:24]
        g_xapre, g_mem, g_xapost, g_ffnpre, g_ffnpost = pv[:, 24:32], pv[:, 32:40], pv[:, 40:48], pv[:, 48:56], pv[:, 56:64]
        d_skip, glu_b = pv[:, 64:68], pv[:, 68:72]
        cvec = T("cvec", [128, 8], F32)
        onesb = T("onesb", [128, 128], BF16)
        onesf = T("onesf", [128, 128], F32)
        identb = T("identb", [128, 128], BF16)
        selb = T("selb", [128, 8, 128], BF16)
        wfb = T("wfb", [128, 8, 8], BF16)
        ring = T("ring", [128, NSLOT, CH * 128], BF16)
        KT = T("KT", [128, 4, S_], BF16)
        VC = T("VC", [128, NBLKS, 8, 64], BF16)
        Ebuf = T("Ebuf", [128, 3, 2, 512], BF16)
        s5x = T("s5x", [128, 4, 512], F32)
        Btab = T("Btab", [128, 16, 2, 128], BF16)
        Ctab = T("Ctab", [128, 16, 2, 128], BF16)
        Rr = T("Rr", [128, 16], F32)
        E512 = T("E512", [128, 2, 16], F32)
        sinit = T("sinit", [128, 2, 16], F32)
        zl = T("zl", [128, 2, 16], F32)
        ztmp = T("ztmp", [128, 4, 16], F32)
        KM = T("KM", [128, 8, NM], BF16)
        VM = T("VM", [128, 2, D], BF16)
        sq = T("sq", [128, 2, 512], BF16)
        rstd = T("rstd", [128, 512], F32)
        pT = T("pT", [128, 4, 512], BF16)
        xr = T("xr", [128, 2, 2, 512], BF16)
        negmb = T("negmb", [128, 128], BF16)
        rl = T("rl", [128, 512], F32)
        rlb = T("rlb", [128, 512], F32)
        negF = T("negF", [128, NBLKS, 8], F32)
        biasT = T("biasT", [128, NBLKS, 8], F32)
        frbc = T("frbc", [128, 8], F32)
        fref = T("fref", [8, 1], F32)
        dg8 = T("dg8", [8, 8], F32)
        ft1 = T("ft1", [8, 512], F32)
        fabs_ = T("fabs", [8, 512], F32)
        ga = T("ga", [128, 512], BF16)
        gml = T("gml", [8, 2, 512], BF16)
        arena = T("arena", [128, 13824], F32)
        ps = [es.enter_context(nc.psum_tensor("ps%d" % i, [128, 512], F32)) for i in range(8)]
        PK = lambda b: ('ps', b)

        xs = arena[:, 0:4096].rearrange("p (a b) -> p a b", b=512)
        hid = arena[:, 4096:9728].bitcast(BF16).rearrange("p (a b) -> p a b", b=512)
        SC = arena[:, 4096:9728].rearrange("p (a b) -> p a b", b=512)
        ACT_ = arena[:, 9728:13824].bitcast(BF16).rearrange("p (a b) -> p a b", b=512)
        O3 = arena[:, 9728:13824].rearrange("p (a b) -> p a b", b=512)
        XK = lambda kc: ('xs', kc)
        HK = lambda c: ('H', c)
        SK = lambda i: [('H', 2 * i), ('H', 2 * i + 1)]
        AK = lambda c: ('A', c)
        O3K = lambda i: [('A', 2 * i), ('A', 2 * i + 1)]
        stage = arena[:, 0:4096].rearrange("p (a b) -> p a b", b=2048)
        cbuf = arena[:, 4096:6144].bitcast(BF16).rearrange("p (a b) -> p a b", b=2048)
        smt = arena[:, 6144:6144 + SMT_N]
        pw = arena[:, 8832:13824]
        iota = smt[:, 0:512]
        self_f = smt[:, 512:1536]
        wf_f = smt[:, 1536:1600]
        s5o = 1600
        are, aim, ldt = smt[:, s5o:s5o + 16], smt[:, s5o + 16:s5o + 32], smt[:, s5o + 32:s5o + 48]
        s5v = lambda k: smt[:, s5o + 48 + 256 * k: s5o + 48 + 256 * (k + 1)].rearrange("p (a b) -> p a b", b=16)
        bre_, bim_, cre_, cim_ = s5v(0), s5v(1), s5v(2), s5v(3)

        if NBLKS >= 20:
            lstage = VC[:, 4:20, :, :].rearrange("p a h d -> p (a h d)").bitcast(F32).rearrange("p (a b) -> p a b", b=2048)
        else:
            lstage = T("lstage", [128, 2, 2048], F32)
        class Ring:
            def __init__(self):
                self.seq = [NCH_T + i for i in range(NCH_P)] + [c for _ in range(nt) for c in range(NCH_T)]
                self.next_dma = 0
                self.cur = 0
                self.casted = set()
                self.ncast = 0

            def ensure(self, q_lo):
                lim = min(len(self.seq), q_lo + NSLOT)
                while self.next_dma < lim:
                    q = self.next_dma
                    slot = q % NSLOT
                    dc = self.seq[q]
                    if dc not in self.casted:
                        self.casted.add(dc)
                        b = self.ncast % 2
                        eng = ['act', 'dve'][self.ncast % 2]
                        self.ncast += 1
                        op('sp', lambda e, dc=dc, b=b: e.dma_start(out=lstage[:, b, :], in_=wst[:, dc * 2048:(dc + 1) * 2048]),
                           writes=[('stage', b)], chan='ST%d' % b)
                        if eng == 'act':
                            op('act', lambda e, b=b, slot=slot: e.activation(ring[:, slot, :], lstage[:, b, :], AF.Copy), reads=[('stage', b)], writes=[('ring', slot)])
                        else:
                            op(eng, lambda e, b=b, slot=slot: e.tensor_copy(ring[:, slot, :], lstage[:, b, :]), reads=[('stage', b)], writes=[('ring', slot)])
                        if dc < NCH_T and nt > 1:
                            op('pool', lambda e, dc=dc, slot=slot: e.dma_start(out=wscr[:, dc * 2048:(dc + 1) * 2048], in_=ring[:, slot, :]),
                               reads=[('ring', slot)], writes=[('wscr', dc)], chan='SO%d' % slot)
                    else:
                        op('sp', lambda e, slot=slot, dc=dc: e.dma_start(out=ring[:, slot, :], in_=wscr[:, dc * 2048:(dc + 1) * 2048]),
                           reads=[('wscr', dc)], writes=[('ring', slot)], chan='W%d' % slot)
                    self.next_dma += 1

            def take(self, n, span=0):
                g0 = self.cur
                self.cur += n
                self.ensure(g0 // CH)
                aps, keys = [], set()
                step = span if span else 1
                for g in range(g0, g0 + n, step):
                    q = g // CH
                    slot = q % NSLOT
                    j = g % CH
                    aps.append(ring[:, slot, j * 128:(j + step) * 128])
                    keys.add(('ring', slot))
                    if span:
                        assert (g + step - 1) // CH == q
                return aps, list(keys)

        R = Ring()

        def mm(out, pairs, reads, writes, start=True, stop=True):
            def fn(e):
                n = len(pairs)
                ins = None
                for i, (l, r) in enumerate(pairs):
                    ins = e.matmul(out, l, r, start=(start and i == 0), stop=(stop and i == n - 1))
                return ins
            op('pe', fn, reads, writes)

        op('sp', lambda e: e.dma_start(out=smp[:], in_=smp_d), writes=['smp'], chan='C0')
        op('sp', lambda e: e.dma_start(out=smt, in_=smt_d), writes=['smt'], chan='C1')
        op('dve', lambda e: e.memset(cvec[:], EPS), writes=['cvec'])
        op('dve', lambda e: e.tensor_scalar(cvec[0:8, 1:2], fb[0:8, :], -1.0, None, ALU.mult), reads=['smp', 'cvec'], writes=['cvec'])
        op('dve', lambda e: e.memset(cvec[:, 2:3], 0.25), reads=['cvec'], writes=['cvec'])
        op('dve', lambda e: e.memset(onesb[:], 1.0), writes=['onesb'])
        op('pool', lambda e: e.memset(onesf[:], 1.0), writes=['onesf'])
        op('dve', lambda e: e.tensor_copy(identb[:], identf), reads=['smp'], writes=['identb'])
        op('dve', lambda e: e.tensor_copy(negmb[:], negm), reads=['smp'], writes=['negmb'])
        op('dve', lambda e: e.tensor_copy(selb[:].rearrange("p a b -> p (a b)"), self_f), reads=['smt'], writes=['selb'])
        op('dve', lambda e: e.tensor_copy(wfb[:].rearrange("p a b -> p (a b)"), wf_f), reads=['smt'], writes=['wfb'])
        op('pool', lambda e: e.memset(fref[:], 0.0), writes=['fref'])
        op('pool', lambda e: e.memset(sinit[:], 0.0), writes=['sinit'])
        op('pool', lambda e: e.memset(ga[:], 0.0), writes=['ga'])

        ss = lambda i: pw[:, 16 * i:16 * (i + 1)]
        big = lambda i: pw[:, 1024 + 512 * i: 1024 + 512 * (i + 1)]
        bigi = pw[:, 1024 + 512 * 6: 1024 + 512 * 7].bitcast(I32)
        DT_, LRE, TH, TQ, C1, S1, P1R, P1I, DEN, QRE, QIM, TA, TB, TI_, Y5 = range(15)
        sI = pw[:, 16 * 20:16 * 21].bitcast(I32)

        def dv(fn, r=('pw',), w=('pw',)):
            op('dve', fn, reads=list(r), writes=list(w))

        def frac_reduce(y, tf, ti):
            dv(lambda e: e.tensor_copy(ti, y))
            dv(lambda e: e.tensor_copy(tf, ti))
            dv(lambda e: e.tensor_tensor(y, y, tf, ALU.subtract))
            dv(lambda e: e.tensor_single_scalar(tf, y, 0.5, ALU.is_gt))
            dv(lambda e: e.tensor_tensor(y, y, tf, ALU.subtract))
            dv(lambda e: e.tensor_single_scalar(tf, y, -0.5, ALU.is_lt))
            dv(lambda e: e.tensor_tensor(y, y, tf, ALU.add))

        def act_pw(fn, r=('pw',), w=('pw',)):
            op('act', fn, reads=list(r), writes=list(w))

        act_pw(lambda e: e.activation(ss(DT_), ldt, AF.Exp), r=('smt', 'pw'))
        dv(lambda e: e.tensor_tensor(ss(LRE), are, ss(DT_), ALU.mult), r=('smt', 'pw'))
        dv(lambda e: e.tensor_tensor(ss(TH), aim, ss(DT_), ALU.mult), r=('smt', 'pw'))
        act_pw(lambda e: e.activation(Rr[:], ss(LRE), AF.Exp), w=('Rr', 'pw'))
        dv(lambda e: e.tensor_scalar(ss(TQ), ss(TH), 1.0 / TWO_PI, None, ALU.mult))
        frac_reduce(ss(TQ), ss(TA), sI)
        act_pw(lambda e: e.activation(ss(S1), ss(TQ), AF.Sin, scale=TWO_PI))
        dv(lambda e: e.tensor_scalar(ss(Y5), ss(TQ), 0.25, None, ALU.add))
        frac_reduce(ss(Y5), ss(TA), sI)
        act_pw(lambda e: e.activation(ss(C1), ss(Y5), AF.Sin, scale=TWO_PI))
        dv(lambda e: e.tensor_scalar(ss(Y5), ss(TQ), 512.0, None, ALU.mult))
        frac_reduce(ss(Y5), ss(TA), sI)
        act_pw(lambda e: e.activation(E512[:, 1, :], ss(Y5), AF.Sin, scale=TWO_PI), w=('E512', 'pw'))
        dv(lambda e: e.tensor_scalar(ss(Y5), ss(Y5), 0.25, None, ALU.add))
        frac_reduce(ss(Y5), ss(TA), sI)
        act_pw(lambda e: e.activation(E512[:, 0, :], ss(Y5), AF.Sin, scale=TWO_PI), w=('E512', 'pw'))
        dv(lambda e: e.tensor_tensor(ss(P1R), Rr[:], ss(C1), ALU.mult), r=('Rr', 'pw'))
        dv(lambda e: e.tensor_tensor(ss(P1I), Rr[:], ss(S1), ALU.mult), r=('Rr', 'pw'))
        dv(lambda e: e.tensor_scalar(ss(P1R), ss(P1R), -1.0, None, ALU.add))
        dv(lambda e: e.tensor_tensor(ss(DEN), are, are, ALU.mult), r=('smt', 'pw'))
        dv(lambda e: e.tensor_tensor(ss(TA), aim, aim, ALU.mult), r=('smt', 'pw'))
        dv(lambda e: e.tensor_tensor(ss(DEN), ss(DEN), ss(TA), ALU.add))
        dv(lambda e: e.reciprocal(ss(DEN), ss(DEN)))
        dv(lambda e: e.tensor_tensor(ss(QRE), ss(P1R), are, ALU.mult), r=('smt', 'pw'))
        dv(lambda e: e.tensor_tensor(ss(TA), ss(P1I), aim, ALU.mult), r=('smt', 'pw'))
        dv(lambda e: e.tensor_tensor(ss(QRE), ss(QRE), ss(TA), ALU.add))
        dv(lambda e: e.tensor_tensor(ss(QRE), ss(QRE), ss(DEN), ALU.mult))
        dv(lambda e: e.tensor_tensor(ss(QIM), ss(P1I), are, ALU.mult), r=('smt', 'pw'))
        dv(lambda e: e.tensor_tensor(ss(TA), ss(P1R), aim, ALU.mult), r=('smt', 'pw'))
        dv(lambda e: e.tensor_tensor(ss(QIM), ss(QIM), ss(TA), ALU.subtract))
        dv(lambda e: e.tensor_tensor(ss(QIM), ss(QIM), ss(DEN), ALU.mult))
        bbre = pw[:, 512:768].rearrange("p (a b) -> p a b", b=16)
        bbim = pw[:, 768:1024].rearrange("p (a b) -> p a b", b=16)
        tb16 = pw[:, 400:416]
        BLK = arena[:, 0:2048].bitcast(BF16).rearrange("p (a c b) -> p a c b", c=2, b=128)
        dv(lambda e: e.memset(BLK, 0.0))
        dv(lambda e: e.memset(Ctab[:], 0.0), w=('Ctab',))
        for pt in range(16):
            q = pt % 4
            dv(lambda e, pt=pt: e.tensor_scalar(tb16, bim_[:, pt, :], ss(QIM)[:, pt:pt + 1], None, ALU.mult), r=('smt', 'pw'))
            dv(lambda e, pt=pt: e.scalar_tensor_tensor(bbre[:, pt, :], bre_[:, pt, :], ss(QRE)[:, pt:pt + 1], tb16, ALU.mult, ALU.subtract), r=('smt', 'pw'))
            dv(lambda e, pt=pt: e.tensor_scalar(tb16, bre_[:, pt, :], ss(QIM)[:, pt:pt + 1], None, ALU.mult), r=('smt', 'pw'))
            dv(lambda e, pt=pt: e.scalar_tensor_tensor(bbim[:, pt, :], bim_[:, pt, :], ss(QRE)[:, pt:pt + 1], tb16, ALU.mult, ALU.add), r=('smt', 'pw'))
            for c, src in ((0, bbre), (1, bbim)):
                dv(lambda e, pt=pt, c=c, src=src, q=q: e.tensor_copy(BLK[0:64, pt, c, 32 * q:32 * q + 16], src[0:64, pt, :]))
                dv(lambda e, pt=pt, c=c, src=src, q=q: e.tensor_copy(BLK[64:128, pt, c, 32 * q + 16:32 * q + 32], src[64:128, pt, :]))
            dv(lambda e, pt=pt, q=q: e.tensor_copy(Ctab[0:64, pt, 0, 32 * q:32 * q + 16], cre_[0:64, pt, :]), r=('smt',), w=('Ctab',))
            dv(lambda e, pt=pt, q=q: e.tensor_copy(Ctab[64:128, pt, 0, 32 * q + 16:32 * q + 32], cre_[64:128, pt, :]), r=('smt',), w=('Ctab',))
            dv(lambda e, pt=pt, q=q: e.tensor_scalar(Ctab[0:64, pt, 1, 32 * q:32 * q + 16], cim_[0:64, pt, :], -1.0, None, ALU.mult), r=('smt',), w=('Ctab',))
            dv(lambda e, pt=pt, q=q: e.tensor_scalar(Ctab[64:128, pt, 1, 32 * q + 16:32 * q + 32], cim_[64:128, pt, :], -1.0, None, ALU.mult), r=('smt',), w=('Ctab',))
        for grp in range(8):
            bank = grp % 2
            for k in range(4):
                idx = grp * 4 + k
                pt, c = idx // 2, idx % 2
                mm(ps[bank][:, k * 128:(k + 1) * 128], [(BLK[:, pt, c, :], identb[:])], reads=['pw', 'identb'], writes=[PK(bank)])
            op('act', lambda e, grp=grp, bank=bank: e.activation(
                Btab[:].rearrange("p a c b -> p (a c b)")[:, grp * 512:(grp + 1) * 512], ps[bank][:], AF.Copy),
                reads=[PK(bank)], writes=['Btab'])
        for pt in range(16):
            for which, dst in ((0, 1), (1, 0)):
                if pt % 2 == 0:
                    y = big(which * 3)
                    tf = big(which * 3 + 1)
                    ti = bigi if which == 0 else pw[:, 1024 + 512 * 2: 1024 + 512 * 3].bitcast(I32)
                    key = ('big', which)
                else:
                    pb_ = lambda i: arena[:, 2048 + 512 * i: 2048 + 512 * (i + 1)]
                    y = pb_(which * 3)
                    tf = pb_(which * 3 + 1)
                    ti = pb_(which * 3 + 2).bitcast(I32)
                    key = ('bigp', which)
                if which == 0:
                    op('act', lambda e, y=y, pt=pt: e.activation(y, iota, AF.Copy, scale=ss(TQ)[:, pt:pt + 1]), reads=['smt', 'pw'], writes=[key])
                else:
                    op('act', lambda e, y=y, pt=pt: e.activation(y, iota, AF.Identity, scale=ss(TQ)[:, pt:pt + 1], bias=cvec[:, 2:3]), reads=['smt', 'pw', 'cvec'], writes=[key])
                for fn_ in (
                    lambda e, y=y, tf=tf, ti=ti: e.tensor_copy(ti, y),
                    lambda e, y=y, tf=tf, ti=ti: e.tensor_copy(tf, ti),
                    lambda e, y=y, tf=tf, ti=ti: e.tensor_tensor(y, y, tf, ALU.subtract),
                ):
                    op('dve', fn_, reads=[key], writes=[key])
                op('act', lambda e, y=y, dst=dst, pt=pt: e.activation(Ebuf[:, pt % 2, dst, :], y, AF.Sin, scale=TWO_PI), reads=[key], writes=[('E', pt % 2)])
            op('sp', lambda e, pt=pt: e.dma_start(out=escr[:, pt, :, :], in_=Ebuf[:, pt % 2, :, :]), reads=[('E', pt % 2)], writes=[('escr', pt)], chan='EO%d' % (pt % 2))

        mT = arena[:, 2048:4096].rearrange("p (a b) -> p a b", b=256)
        mnT = arena[:, 4096:5120].bitcast(BF16).rearrange("p (a b) -> p a b", b=256)
        op('sp', lambda e: e.dma_start(out=mT, in_=memT_v), writes=['mT', 'mnT', ('bigp', 0), ('bigp', 1)], chan='C2')
        for kc in range(KC):
            b = kc % 2
            op('act', lambda e, kc=kc, b=b: e.activation(sq[:, b, 0:256], mT[:, kc, :], AF.Square), reads=['mT'], writes=[('sq', b)])
            mm(ps[7][:, 0:256], [(onesb[:], sq[:, b, 0:256])], reads=[('sq', b), 'onesb'], writes=[PK(7)], start=(kc == 0), stop=(kc == KC - 1))
        op('act', lambda e: e.activation(rstd[:, 0:256], ps[7][:, 0:256], AF.Sqrt, bias=cvec[:, 0:1], scale=1.0 / D), reads=[PK(7), 'cvec'], writes=['rstd'])
        op('dve', lambda e: e.reciprocal(rstd[:, 0:256], rstd[:, 0:256]), reads=['rstd'], writes=['rstd'])
        for kc in range(KC):
            op('dve', lambda e, kc=kc: e.scalar_tensor_tensor(mnT[:, kc, :], mT[:, kc, :], g_mem[:, kc:kc + 1], rstd[:, 0:256], ALU.mult, ALU.mult),
               reads=['mT', 'rstd', 'smp'], writes=['mnT'])
        for m in range(8):
            blks, rk = R.take(8)
            bank = m % 2
            mm(ps[bank][:, 0:256], [(blks[kc], mnT[:, kc, :]) for kc in range(KC)], reads=rk + ['mnT'], writes=[PK(bank)])
            op('act', lambda e, m=m, bank=bank: e.activation(KM[:, m, :], ps[bank][:, 0:256], AF.Copy), reads=[PK(bank)], writes=['KM'])
        spans, rk = R.take(64, span=4)
        for mb in range(2):
            for hf in range(2):
                bank = (mb * 2 + hf) % 2
                mm(ps[bank][:], [(mnT[:, kc, mb * 128:(mb + 1) * 128], spans[kc * 2 + hf]) for kc in range(KC)], reads=rk + ['mnT'], writes=[PK(bank)])
                op('dve', lambda e, mb=mb, hf=hf, bank=bank: e.tensor_copy(VM[:, mb, hf * 512:(hf + 1) * 512], ps[bank][:]), reads=[PK(bank)], writes=['VM'])
        P.barrier()

        def rms_rstd(srcs, nfeat):
            n = len(srcs)
            for i, (ap, keys) in enumerate(srcs):
                b = i % 2
                op('act', lambda e, ap=ap, b=b: e.activation(sq[:, b, :], ap, AF.Square), reads=keys, writes=[('sq', b)])
                mm(ps[7][:], [(onesb[:], sq[:, b, :])], reads=[('sq', b)], writes=[PK(7)], start=(i == 0), stop=(i == n - 1))
            op('act', lambda e: e.activation(rstd[:], ps[7][:], AF.Ln, bias=cvec[:, 0:1], scale=1.0 / nfeat), reads=[PK(7)], writes=['rstd'])
            op('act', lambda e: e.activation(rstd[:], rstd[:], AF.Exp, scale=-0.5), reads=['rstd'], writes=['rstd'])

        def pre_norm(gain, dst_base):
            rms_rstd([(xs[:, kc, :], [XK(kc)]) for kc in range(KC)], D)
            for kc in range(KC):
                eng = 'dve'
                op(eng, lambda e, kc=kc: e.scalar_tensor_tensor(ACT_[:, dst_base + kc, :], xs[:, kc, :], gain[:, kc:kc + 1], rstd[:], ALU.mult, ALU.mult),
                   reads=[XK(kc), 'rstd'], writes=[AK(dst_base + kc)])

        def post_norm_residual(osrc, okeys, gain, final=False, after_add=None):
            rms_rstd([(osrc(kc), okeys(kc)) for kc in range(KC)], D)
            for kc in range(KC):
                op('dve', lambda e, kc=kc: e.scalar_tensor_tensor(osrc(kc), osrc(kc), gain[:, kc:kc + 1], rstd[:], ALU.mult, ALU.mult),
                   reads=okeys(kc) + ['rstd'], writes=okeys(kc))
                if final:
                    op('pool', lambda e, kc=kc: e.tensor_tensor(osrc(kc), xs[:, kc, :], osrc(kc), ALU.add),
                       reads=okeys(kc) + [XK(kc)], writes=okeys(kc))
                    if after_add is not None:
                        after_add(kc)
                else:
                    op('pool', lambda e, kc=kc: e.tensor_tensor(xs[:, kc, :], xs[:, kc, :], osrc(kc), ALU.add),
                       reads=okeys(kc) + [XK(kc)], writes=[XK(kc)])

        def proj8(src_base, evac):
            for m in range(8):
                blks, rk = R.take(8)
                bank = m % 2
                mm(ps[bank][:], [(blks[kc], ACT_[:, src_base + kc, :]) for kc in range(KC)],
                   reads=rk + [AK(src_base + kc) for kc in range(KC)], writes=[PK(bank)])
                evac(m, bank)

        def evac_copy(dst, dkeys, bank, i, scale=None):
            if i % 2 == 0:
                if scale is None:
                    op('act', lambda e: e.activation(dst, ps[bank][:], AF.Copy), reads=[PK(bank)], writes=dkeys)
                else:
                    op('act', lambda e: e.activation(dst, ps[bank][:], AF.Copy, scale=scale), reads=[PK(bank)], writes=dkeys)
            else:
                if scale is None:
                    op('dve', lambda e: e.tensor_copy(dst, ps[bank][:]), reads=[PK(bank)], writes=dkeys)
                else:
                    op('dve', lambda e: e.tensor_scalar(dst, ps[bank][:], scale, None, ALU.mult), reads=[PK(bank)], writes=dkeys)

        for ti in range(nt):
            T0 = ti * TT
            if ti == 0:
                op('pool', lambda e, T0=T0: e.dma_start(out=xs, in_=xT_v[:, :, T0:T0 + TT]), writes=[XK(kc) for kc in range(KC)], chan='XL')
            pre_norm(g_mixpre, 0)
            hkeys = [AK(kc) for kc in range(KC)]
            def fg1():
                mm(ps[5][0:8, :], [(wfb[:, kc, :], ACT_[:, kc, :]) for kc in range(KC)], reads=hkeys + ['wfb'], writes=[PK(5)])
                op('act', lambda e: e.activation(ft1[:], ps[5][0:8, :], AF.Exp, bias=cvec[0:8, 1:2], scale=-1.0), reads=[PK(5)], writes=['ft1'])
                op('act', lambda e: e.activation(ft1[:], ft1[:], AF.Ln, bias=1.0, scale=1.0), reads=['ft1'], writes=['ft1'])
                op('dve', lambda e: e.tensor_tensor_scan(fabs_[:], onesf[0:8, 0:1].to_broadcast([8, 512]), ft1[:], fref[:, 0:1], ALU.mult, ALU.subtract),
                   reads=['ft1', 'fref'], writes=['fabs'])
                op('dve', lambda e: e.tensor_scalar(ft1[:], fabs_[:], fref[:, 0:1], None, ALU.subtract), reads=['fabs', 'fref'], writes=['ft1'])
                op('dve', lambda e: e.tensor_copy(ga[0:8, :], ft1[:]), reads=['ft1'], writes=['ga'])
                op('dve', lambda e: e.tensor_copy(ga[64:72, :], ft1[:]), reads=['ft1'], writes=['ga'])
                op('dve', lambda e: e.tensor_tensor(ft1[:], ft1[:], ga[0:8, :], ALU.subtract), reads=['ft1', 'ga'], writes=['ft1'])
                op('dve', lambda e: e.tensor_copy(gml[:, 0, :], ft1[:]), reads=['ft1'], writes=['gml'])
                op('dve', lambda e: e.tensor_tensor(ft1[:], ft1[:], gml[:, 0, :], ALU.subtract), reads=['ft1', 'gml'], writes=['ft1'])
                op('dve', lambda e: e.tensor_copy(gml[:, 1, :], ft1[:]), reads=['ft1'], writes=['gml'])
                op('pool', lambda e: e.dma_start(out=ga[8:16, :], in_=gml[:, 0, :]), reads=['gml'], writes=['ga'], chan='GA0')
                op('pool', lambda e: e.dma_start(out=ga[16:24, :], in_=gml[:, 1, :]), reads=['gml'], writes=['ga'], chan='GA1')
                op('pool', lambda e: e.dma_start(out=ga[72:80, :], in_=gml[:, 0, :]), reads=['gml'], writes=['ga'], chan='GA2')
                op('pool', lambda e: e.dma_start(out=ga[80:88, :], in_=gml[:, 1, :]), reads=['gml'], writes=['ga'], chan='GA3')

            def fg2():
                for jb in range(4):
                    mm(ps[4][:, jb * 8:(jb + 1) * 8], [(fabs_[:, jb * 128:(jb + 1) * 128], identf[0:8, 0:8])], reads=['fabs'], writes=[PK(4)])
                op('dve', lambda e, ti=ti: e.tensor_scalar(negF[:, 4 * ti:4 * ti + 4, :], ps[4][:, 0:32].rearrange("p (a b) -> p a b", b=8), -1.0, None, ALU.mult),
                   reads=[PK(4)], writes=['negF'])

            def fg3():
                op('dve', lambda e: e.tensor_scalar(dg8[:], identf[0:8, 0:8], fref[:, 0:1], None, ALU.mult), reads=['fref'], writes=['dg8'])
                mm(ps[4][:, 64:72], [(onesf[0:8, 0:128], dg8[:])], reads=['dg8'], writes=[PK(4)])
                op('dve', lambda e: e.tensor_copy(frbc[:], ps[4][:, 64:72]), reads=[PK(4)], writes=['frbc'])
                for kb in range(4 * ti + 4):
                    op('dve', lambda e, kb=kb: e.tensor_tensor(biasT[:, kb, :], negF[:, kb, :], frbc[:], ALU.add), reads=['negF', 'frbc'], writes=['biasT'])
                op('dve', lambda e: e.tensor_copy(fref[:], fabs_[:, 511:512]), reads=['fabs'], writes=['fref'])

            fg1()
            for m in range(12):
                blks, rk = R.take(8)
                bank = m % 2
                mm(ps[bank][:], [(blks[kc], ACT_[:, kc, :]) for kc in range(KC)], reads=rk + hkeys, writes=[PK(bank)])
                if m < 4:
                    evac_copy(ACT_[:, 8 + m, :], [AK(8 + m)], bank, m)
                elif m < 8:
                    evac_copy(ACT_[:, 8 + m, :], [AK(8 + m)], bank, m, scale=0.125)
                else:
                    evac_copy(KT[:, m - 8, T0:T0 + TT], [('KT', m - 8, ti)], bank, m)
                if m == 3:
                    fg2()
                if m == 7:
                    fg3()

            def load_E(pt):
                op('sp', lambda e, pt=pt: e.dma_start(out=Ebuf[:, pt % 3, :, :], in_=escr[:, pt, :, :]), reads=[('escr', pt)], writes=[('E', pt % 3)], chan='EL%d' % (pt % 3))

            def s5A(pt):
                ut = pt // 4
                uk = [AK(8 + ut)]
                mm(ps[0][:], [(Btab[:, pt, 0, :], ACT_[:, 8 + ut, :])], reads=uk, writes=[PK(0)])
                mm(ps[1][:], [(Btab[:, pt, 1, :], ACT_[:, 8 + ut, :])], reads=uk, writes=[PK(1)])

            def s5set(pt):
                par = pt % 2
                st = pt % 3
                if st < 2:
                    S = [SC[:, 4 * st + i, :] for i in range(4)]
                    K_ = [SK(4 * st + i) for i in range(4)]
                else:
                    S = [s5x[:, i, :] for i in range(4)]
                    K_ = [[('s5x', i)] for i in range(4)]
                return (par, S, K_, Ebuf[:, st, 0, :], Ebuf[:, st, 1, :], [('E', st)])

            def s5B1(pt):
                par, S, K, c_, s_, EK = s5set(pt)
                op('dve', lambda e: e.tensor_tensor(S[0], ps[0][:], c_, ALU.mult), reads=[PK(0)] + EK, writes=K[0])
                op('dve', lambda e: e.tensor_tensor(S[1], ps[1][:], s_, ALU.mult), reads=[PK(1)] + EK, writes=K[1])
                op('dve', lambda e: e.tensor_tensor(S[2], ps[1][:], c_, ALU.mult), reads=[PK(1)] + EK, writes=K[2])
                op('dve', lambda e: e.tensor_tensor(S[3], ps[0][:], s_, ALU.mult), reads=[PK(0)] + EK, writes=K[3])

            def s5B2(pt):
                par, S, K, c_, s_, EK = s5set(pt)
                op('pool', lambda e: e.tensor_tensor(S[0], S[0], S[1], ALU.add), reads=K[0] + K[1], writes=K[0])
                op('pool', lambda e: e.tensor_tensor(S[2], S[2], S[3], ALU.subtract), reads=K[2] + K[3], writes=K[2])

            def s5B3(pt):
                par, S, K, c_, s_, EK = s5set(pt)
                op('dve', lambda e: e.tensor_tensor_scan(S[1], Rr[:, pt:pt + 1].to_broadcast([128, 512]), S[0], sinit[:, 0, pt:pt + 1], ALU.mult, ALU.add),
                   reads=K[0] + ['sinit'], writes=K[1])
                op('dve', lambda e: e.tensor_tensor_scan(S[3], Rr[:, pt:pt + 1].to_broadcast([128, 512]), S[2], sinit[:, 1, pt:pt + 1], ALU.mult, ALU.add),
                   reads=K[2] + ['sinit'], writes=K[3])
                op('dve', lambda e: e.tensor_copy(zl[:, 0, pt:pt + 1], S[1][:, 511:512]), reads=K[1], writes=['zl'])
                op('dve', lambda e: e.tensor_copy(zl[:, 1, pt:pt + 1], S[3][:, 511:512]), reads=K[3], writes=['zl'])

            def s5C1(pt):
                par, S, K, c_, s_, EK = s5set(pt)
                op('pool', lambda e: e.tensor_tensor(S[0], S[1], c_, ALU.mult), reads=K[1] + EK, writes=K[0])
                op('pool', lambda e: e.tensor_tensor(S[2], S[3], s_, ALU.mult), reads=K[3] + EK, writes=K[2])

            def s5C1b(pt):
                par, S, K, c_, s_, EK = s5set(pt)
                op('pool', lambda e: e.tensor_tensor(S[3], S[3], c_, ALU.mult), reads=K[3] + EK, writes=K[3])
                op('pool', lambda e: e.tensor_tensor(S[1], S[1], s_, ALU.mult), reads=K[1] + EK, writes=K[1])
                if pt + 3 < 16:
                    load_E(pt + 3)

            def s5C2a(pt):
                par, S, K, c_, s_, EK = s5set(pt)
                op('dve', lambda e: e.tensor_tensor(xr[:, par, 0, :], S[0], S[2], ALU.subtract), reads=K[0] + K[2], writes=[('xr', par, 0)])

            def s5C2b(pt):
                par, S, K, c_, s_, EK = s5set(pt)
                op('dve', lambda e: e.tensor_tensor(xr[:, par, 1, :], S[3], S[1], ALU.add), reads=K[3] + K[1], writes=[('xr', par, 1)])

            def s5D(pt):
                par = pt % 2
                ut = pt // 4
                mm(ps[2][:], [(Ctab[:, pt, 0, :], xr[:, par, 0, :]), (Ctab[:, pt, 1, :], xr[:, par, 1, :])],
                   reads=[('xr', par, 0), ('xr', par, 1)], writes=[PK(2)], start=(pt % 4 == 0), stop=(pt % 4 == 3))
                if pt % 4 == 3:
                    Y = rstd[:]
                    W = sq[:].rearrange("p a b -> p (a b)").bitcast(F32)
                    YK, WK = ['rstd'], [('sq', 0), ('sq', 1)]
                    op('dve', lambda e: e.scalar_tensor_tensor(Y, ACT_[:, 8 + ut, :], d_skip[:, ut:ut + 1], ps[2][:], ALU.mult, ALU.add),
                       reads=[PK(2), AK(8 + ut)], writes=YK)
                    op('dve', lambda e: e.tensor_tensor(W, Y, Y, ALU.mult), reads=YK, writes=WK)
                    op('dve', lambda e: e.tensor_scalar(W, W, 0.044715, 1.0, ALU.mult, ALU.add), reads=WK, writes=WK)
                    op('dve', lambda e: e.tensor_tensor(W, W, Y, ALU.mult), reads=WK + YK, writes=WK)
                    gelq.append([1,
                                 lambda: op('act', lambda e: e.activation(W, W, AF.Tanh, scale=float(np.sqrt(2.0 / np.pi))), reads=WK, writes=WK),
                                 lambda: op('dve', lambda e: e.scalar_tensor_tensor(W, W, 1.0, Y, ALU.add, ALU.mult), reads=WK + YK, writes=WK),
                                 lambda ut=ut: op('act', lambda e: e.activation(ACT_[:, ut, :], W, AF.Copy, scale=0.5), reads=WK, writes=[AK(ut)])])

            gelq = []

            def gelu_act():
                for g in gelq:
                    if g[0] == 1:
                        g[1]()
                        g[0] = 2
                    elif g[0] == 3:
                        g[3]()
                        g[0] = 4

            def gelu_dve():
                for g in gelq:
                    if g[0] == 2:
                        g[2]()
                        g[0] = 3

            nkb = 4 * ti + 4
            YFt = [SC[:, 8, :], SC[:, 9, :], SC[:, 10, :], arena[:, 9728 + 1024:9728 + 1536]]
            YFk = [SK(8), SK(9), SK(10), [AK(4), AK(5)]]
            fox_items = []
            rot = [0, 0]
            for m_ in range(4):
                hA, hB = 2 * m_, 2 * m_ + 1
                qk = AK(12 + m_)

                def stageA(kb, m_=m_, hA=hA, hB=hB, qk=qk):
                    sa = 3 + (rot[0] % 3)
                    sb = 3 + ((rot[0] + 1) % 3)
                    rot[0] += 2
                    ra = rot[1] % 4
                    rb = (rot[1] + 1) % 4
                    rot[1] += 2
                    intile = kb >= 4 * ti
                    c0 = 128 * (kb - 4 * ti) if intile else 0

                    def fn(e):
                        e.matmul(ps[sa][:, c0:512], KT[0:64, m_, kb * 128:(kb + 1) * 128], ACT_[0:64, 12 + m_, c0:512], start=True, stop=False)
                        e.matmul(ps[sb][:, c0:512], KT[64:128, m_, kb * 128:(kb + 1) * 128], ACT_[64:128, 12 + m_, c0:512], start=True, stop=False)
                        e.matmul(ps[sa][:, c0:512], selb[0:24, hA, :], ga[0:24, c0:512], start=False, stop=not intile)
                        i2 = e.matmul(ps[sb][:, c0:512], selb[64:88, hB, :], ga[64:88, c0:512], start=False, stop=not intile)
                        if intile:
                            e.matmul(ps[sa][:, c0:c0 + 128], identb[:], negmb[:], start=False, stop=True)
                            i2 = e.matmul(ps[sb][:, c0:c0 + 128], identb[:], negmb[:], start=False, stop=True)
                        return i2
                    op('pe', fn, reads=[('KT', m_, kb // 4), qk, 'ga'], writes=[PK(sa), PK(sb)])
                    for (sx, rx, hx) in ((sa, ra, hA), (sb, rb, hB)):
                        op('act', lambda e, sx=sx, rx=rx, hx=hx: e.activation(pT[:, rx, c0:512], ps[sx][:, c0:512], AF.Exp, bias=biasT[:, kb, hx:hx + 1], scale=1.0),
                           reads=[PK(sx), 'biasT'], writes=[('pT', rx)])
                    return ra, rb, c0

                def stageB(kb, st, hA=hA, hB=hB):
                    ra, rb, c0 = st
                    first, last = (kb == 0), (kb == nkb - 1)

                    def fn(e):
                        e.matmul(ps[6][0:64, c0:512], VC[:, kb, hA, :], pT[:, ra, c0:512], start=first, stop=last, tile_position=(0, 0))
                        e.matmul(ps[6][64:128, c0:512], VC[:, kb, hB, :], pT[:, rb, c0:512], start=first, stop=last, tile_position=(0, 64))
                        e.matmul(ps[7][0:64, c0:512], onesb[:, 0:64], pT[:, ra, c0:512], start=first, stop=last, tile_position=(0, 0))
                        return e.matmul(ps[7][64:128, c0:512], onesb[:, 64:128], pT[:, rb, c0:512], start=first, stop=last, tile_position=(0, 64))
                    op('pe', fn, reads=[('pT', ra), ('pT', rb), ('VC', kb)], writes=[PK(6), PK(7)])

                def fin(m_=m_):
                    op('act', lambda e: e.activation(rl[:], ps[7][:], AF.Copy), reads=[PK(7)], writes=['rl'])
                    op('act', lambda e: e.activation(rlb[:], ps[6][:], AF.Copy), reads=[PK(6)], writes=['rlb'])
                    op('dve', lambda e: e.reciprocal(rl[:], rl[:]), reads=['rl'], writes=['rl'])
                    op('dve', lambda e: e.tensor_tensor(YFt[m_], rlb[:], rl[:], ALU.mult), reads=['rlb', 'rl'], writes=YFk[m_])

                state = {}

                def item_first(stageA=stageA, state=state):
                    state[0] = stageA(0)

                def item_mid(kb, stageA=stageA, stageB=stageB, state=state):
                    if kb + 1 < nkb:
                        state[kb + 1] = stageA(kb + 1)
                    stageB(kb, state[kb])

                fox_items.append(item_first)
                if m_ > 0:
                    fox_items.append(prev_fin[0])
                for kb in range(nkb):
                    fox_items.append(lambda kb=kb, item_mid=item_mid: item_mid(kb))
                prev_fin = [fin]
            fox_items.append(prev_fin[0])

            def s5_slot(k):
                if ok(k):
                    s5A(k)
                    s5B1(k)
                    s5B2(k)
                if ok(k - 2):
                    s5C2a(k - 2)
                    s5C2b(k - 2)
                if ok(k - 1):
                    s5B3(k - 1)
                    s5C1(k - 1)
                    s5C1b(k - 1)
                gelu_dve()
                if ok(k - 3):
                    s5D(k - 3)

            ok = lambda p: 0 <= p < 16
            load_E(0)
            load_E(1)
            load_E(2)
            spans, rk = R.take(32, span=4)

            def vproj(tb):
                bank = 3 + tb % 2
                mm(ps[bank][:], [(ACT_[:, kc, tb * 128:(tb + 1) * 128], spans[kc]) for kc in range(KC)], reads=rk + hkeys, writes=[PK(bank)])
                blk = 4 * ti + tb
                op('dve' if tb % 2 else 'act',
                   (lambda e, blk=blk, bank=bank: e.tensor_copy(VC[:, blk, :, :], ps[bank][:].rearrange("p (h d) -> p h d", d=64))) if tb % 2 else
                   (lambda e, blk=blk, bank=bank: e.activation(VC[:, blk, :, :], ps[bank][:].rearrange("p (h d) -> p h d", d=64), AF.Copy)),
                   reads=[PK(bank)], writes=[('VC', blk)] + ([('stage', 0), ('stage', 1)] if 4 <= blk < 20 else []))
            NSL = 20
            per = -(-len(fox_items) // NSL)
            fi = 0
            for k in range(NSL):
                s5_slot(k)
                if k == 0:
                    vproj(0)
                    vproj(1)
                if k == 1:
                    vproj(2)
                    vproj(3)
                for _ in range(per):
                    if fi < len(fox_items):
                        fox_items[fi]()
                        fi += 1
                gelu_act()
            while fi < len(fox_items):
                fox_items[fi]()
                fi += 1
            for _ in range(3):
                gelu_dve()
                gelu_act()
            assert all(g[0] == 4 for g in gelq)
            c5, s5 = E512[:, 0, :], E512[:, 1, :]
            op('dve', lambda e: e.tensor_tensor(ztmp[:, 0, :], zl[:, 0, :], c5, ALU.mult), reads=['zl'], writes=['ztmp'])
            op('dve', lambda e: e.tensor_tensor(ztmp[:, 1, :], zl[:, 1, :], s5, ALU.mult), reads=['zl'], writes=['ztmp'])
            op('dve', lambda e: e.tensor_tensor(ztmp[:, 2, :], zl[:, 1, :], c5, ALU.mult), reads=['zl'], writes=['ztmp'])
            op('dve', lambda e: e.tensor_tensor(ztmp[:, 3, :], zl[:, 0, :], s5, ALU.mult), reads=['zl'], writes=['ztmp'])
            op('dve', lambda e: e.tensor_tensor(sinit[:, 0, :], ztmp[:, 0, :], ztmp[:, 1, :], ALU.subtract), reads=['ztmp'], writes=['sinit'])
            op('dve', lambda e: e.tensor_tensor(sinit[:, 1, :], ztmp[:, 2, :], ztmp[:, 3, :], ALU.add), reads=['ztmp'], writes=['sinit'])
            for m in range(4):
                blks, rk = R.take(4)
                bank = m % 2
                mm(ps[bank][:], [(blks[kc], ACT_[:, kc, :]) for kc in range(4)], reads=rk + [AK(kc) for kc in range(4)], writes=[PK(bank)])
                op('act', lambda e, m=m, bank=bank: e.activation(SC[:, 4, :], ps[bank][:], AF.Sigmoid, bias=glu_b[:, m:m + 1], scale=1.0), reads=[PK(bank)], writes=SK(4))
                op('dve', lambda e, m=m: e.tensor_tensor(SC[:, m, :], ACT_[:, m, :], SC[:, 4, :], ALU.mult), reads=SK(4) + [AK(m)], writes=SK(m))
            rms_rstd([(SC[:, m, :], SK(m)) for m in range(4)], 512)
            for m in range(4):
                op('dve', lambda e, m=m: e.scalar_tensor_tensor(ACT_[:, m, :], SC[:, m, :], g_ssm[:, m:m + 1], rstd[:], ALU.mult, ALU.mult),
                   reads=SK(m) + ['rstd'], writes=[AK(m)])
            rms_rstd([(YFt[m], YFk[m]) for m in range(4)], 512)
            for m in (3, 0, 1, 2):
                op('dve', lambda e, m=m: e.scalar_tensor_tensor(ACT_[:, 4 + m, :], YFt[m], g_fox[:, m:m + 1], rstd[:], ALU.mult, ALU.mult),
                   reads=YFk[m] + ['rstd'], writes=[AK(4 + m)])

            proj8(0, lambda m, bank: evac_copy(SC[:, m, :], SK(m), bank, m))
            post_norm_residual(lambda kc: SC[:, kc, :], SK, g_mixpost)

            pre_norm(g_xapre, 8)
            proj8(8, lambda m, bank: evac_copy(ACT_[:, m, :], [AK(m)], bank, m, scale=1.0 / 16.0))
            for hx in range(4):
                c0_, c1_ = 2 * hx, 2 * hx + 1
                par = hx % 2
                ob = (4, 5, 6) if par == 0 else (0, 1, 7)
                rlt, rlk = (rl, 'rl') if par == 0 else (rlb, 'rlb')
                for mb in range(2):
                    sb = 2 + mb
                    pi = 2 * par + mb
                    mm(ps[sb][:], [(KM[:, c0_, mb * 128:(mb + 1) * 128], ACT_[:, c0_, :]), (KM[:, c1_, mb * 128:(mb + 1) * 128], ACT_[:, c1_, :])],
                       reads=[AK(c0_), AK(c1_)], writes=[PK(sb)])
                    op('act', lambda e, sb=sb, pi=pi: e.activation(pT[:, pi, :], ps[sb][:], AF.Exp), reads=[PK(sb)], writes=[('pT', pi)])
                for mb in range(2):
                    pi = 2 * par + mb
                    for dc in range(2):
                        mm(ps[ob[dc]][:], [(VM[:, mb, (2 * hx + dc) * 128:(2 * hx + dc + 1) * 128], pT[:, pi, :])], reads=[('pT', pi)], writes=[PK(ob[dc])],
                           start=(mb == 0), stop=(mb == 1))
                    mm(ps[ob[2]][:], [(onesb[:], pT[:, pi, :])], reads=[('pT', pi)], writes=[PK(ob[2])], start=(mb == 0), stop=(mb == 1))
                op('dve', lambda e, rlt=rlt, ob=ob: e.reciprocal(rlt[:], ps[ob[2]][:]), reads=[PK(ob[2])], writes=[rlk])
                for dc in range(2):
                    op('dve', lambda e, dc=dc, hx=hx, rlt=rlt, ob=ob: e.tensor_tensor(ACT_[:, 8 + 2 * hx + dc, :], ps[ob[dc]][:], rlt[:], ALU.mult),
                       reads=[PK(ob[dc]), rlk], writes=[AK(8 + 2 * hx + dc)])
            proj8(8, lambda m, bank: evac_copy(SC[:, m, :], SK(m), bank, m))
            post_norm_residual(lambda kc: SC[:, kc, :], SK, g_xapost)

            pre_norm(g_ffnpre, 0)
            for m in range(HC):
                blks, rk = R.take(16)
                par = m % 2
                bg, bu = 2 + 2 * par, 3 + 2 * par
                mm(ps[bg][:], [(blks[kc], ACT_[:, kc, :]) for kc in range(KC)], reads=rk + hkeys, writes=[PK(bg)])
                mm(ps[bu][:], [(blks[8 + kc], ACT_[:, kc, :]) for kc in range(KC)], reads=rk + hkeys, writes=[PK(bu)])
                tq_, tk_ = (rl, 'rl') if par == 0 else (rlb, 'rlb')
                op('act', lambda e, tq_=tq_, bg=bg: e.activation(tq_[:], ps[bg][:], AF.Silu), reads=[PK(bg)], writes=[tk_])
                op('dve', lambda e, tq_=tq_, bu=bu, m=m: e.tensor_tensor(hid[:, m, :], tq_[:], ps[bu][:], ALU.mult),
                   reads=[PK(bu), tk_], writes=[HK(m)])
            for m in range(8):
                blks, rk = R.take(HC)
                bank = m % 2
                mm(ps[bank][:], [(blks[kc], hid[:, kc, :]) for kc in range(HC)], reads=rk + [HK(kc) for kc in range(HC)], writes=[PK(bank)])
                evac_copy(O3[:, m, :], O3K(m), bank, m)
            def next_x(kc, T0=T0, ti=ti):
                if ti + 1 < nt:
                    op('sp', lambda e: e.dma_start(out=xs[:, kc, :], in_=xT_v[:, kc, T0 + TT:T0 + 2 * TT]), writes=[XK(kc)], chan='XL%d' % kc)
            post_norm_residual(lambda kc: O3[:, kc, :], O3K, g_ffnpost, final=True, after_add=next_x)
            op('sp', lambda e, T0=T0: e.dma_start(out=yT_v[:, :, T0:T0 + TT], in_=O3), reads=[AK(c) for c in range(16)], writes=[('yT', ti)], chan='XO')

        op('sp', None, reads=[('yT', ti) for ti in range(nt)])
        with nc.Block() as block:
            P.emit(block)
    return nc


def _blk(W, kc, m):
    return W[kc * 128:(kc + 1) * 128, m * 128:(m + 1) * 128]


def _weight_stream(w_in, glu_w, w_out, xa_wq, xa_wo, w_gate, w_up, w_down, xa_wkv):
    blocks = []
    wi = w_in[:, :1536]
    for m in range(12):
        for kc in range(8):
            blocks.append(_blk(wi, kc, m))
    wv = w_in[:, 1536:2048]
    for kc in range(8):
        for j in range(4):
            blocks.append(_blk(wv, kc, j))
    for m in range(4):
        for kc in range(4):
            blocks.append(_blk(glu_w, kc, m))
    for W in (w_out, xa_wq, xa_wo):
        for m in range(8):
            for kc in range(8):
                blocks.append(_blk(W, kc, m))
    for m in range(HC):
        for kc in range(8):
            blocks.append(_blk(w_gate, kc, m))
        for kc in range(8):
            blocks.append(_blk(w_up, kc, m))
    for m in range(8):
        for kc in range(HC):
            blocks.append(_blk(w_down, kc, m))
    assert len(blocks) == NBT
    wk, wvv = xa_wkv[:, :1024], xa_wkv[:, 1024:]
    for m in range(8):
        for kc in range(8):
            blocks.append(_blk(wk, kc, m))
    for kc in range(8):
        for j in range(8):
            blocks.append(_blk(wvv, kc, j))
    assert len(blocks) == NBT + NBP
    return np.ascontiguousarray(np.concatenate(blocks, axis=1), dtype=np.float32)


def _fm(v, n):
    return np.asarray(v, np.float32).reshape(n, 128).T


def _prep_shared(inp):
    f = lambda k: np.asarray(inp[k], np.float32)
    wst = _weight_stream(f("w_in"), f("ssm_glu_w"), f("w_out"), f("xa_wq"), f("xa_wo"), f("w_gate"), f("w_up"), f("w_down"), f("xa_wkv"))
    smp = np.zeros((128, SMP_N), np.float32)
    s_idx = np.arange(128)[:, None]
    t_idx = np.arange(128)[None, :]
    smp[:, 0:128] = np.where(s_idx <= t_idx, 0.0, -30000.0)
    smp[:, 128:256] = np.eye(128, dtype=np.float32)
    pvs = [(_fm(f("mix_pre_g"), 8)), _fm(f("ssm_out_g"), 4), _fm(f("fox_out_g"), 4), _fm(f("mix_post_g"), 8), _fm(f("xa_pre_g"), 8),
           _fm(f("mem_g"), 8), _fm(f("xa_post_g"), 8), _fm(f("ffn_pre_g"), 8), _fm(f("ffn_post_g"), 8), _fm(f("ssm_d"), 4), _fm(f("ssm_glu_b"), 4)]
    smp[:, 256:328] = np.concatenate(pvs, axis=1)
    smp[0:8, 328] = f("fox_f_bias")
    smt = np.zeros((128, SMT_N), np.float32)
    smt[:, 0:512] = np.arange(512, dtype=np.float32)[None, :]
    sel = np.zeros((24, 8, 128), np.float32)
    for r in range(24):
        sel[r, r % 8, :] = 1.0
    smt[0:24, 512:1536] = sel.reshape(24, 1024)
    smt[64:88, 512:1536] = sel.reshape(24, 1024)
    wf = f("w_in")[:, 2048:2056]
    smt[:, 1536:1600] = wf.reshape(8, 128, 8).transpose(1, 0, 2).reshape(128, 64)
    o = 1600

    def gl(a):
        a = np.asarray(a, np.float32)
        tail = a.shape[2:]
        a = a.reshape(16, 2, 64, *tail)
        a = np.moveaxis(a, 0, 2)
        return a.reshape(128, 16, *tail)
    smt[:, o:o + 16] = gl(f("ssm_a_re"))
    smt[:, o + 16:o + 32] = gl(f("ssm_a_im"))
    smt[:, o + 32:o + 48] = gl(np.repeat(f("ssm_log_dt")[:, None], 64, axis=1))
    smt[:, o + 48:o + 304] = gl(f("ssm_b_re")).reshape(128, 256)
    smt[:, o + 304:o + 560] = gl(f("ssm_b_im")).reshape(128, 256)
    smt[:, o + 560:o + 816] = gl(np.transpose(f("ssm_c_re"), (0, 2, 1))).reshape(128, 256)
    smt[:, o + 816:o + 1072] = gl(np.transpose(f("ssm_c_im"), (0, 2, 1))).reshape(128, 256)
    return wst, smp, smt


_NC_CACHE = {}


def kernel(**inputs):
    x = np.asarray(inputs["x"], np.float32)
    mem = np.asarray(inputs["mem"], np.float32)
    B = x.shape[0]
    nt = x.shape[1] // TT
    wst, smp, smt = _prep_shared(inputs)
    if nt not in _NC_CACHE:
        _NC_CACHE[nt] = build(nt)
    nc = _NC_CACHE[nt]
    in_maps = []
    for b in range(B):
        in_maps.append({"xT": np.ascontiguousarray(x[b].T), "memT": np.ascontiguousarray(mem[b].T),
                        "wst": wst, "smp": smp, "smt": smt})
    res = run_bass_kernel_spmd(nc, in_maps, core_ids=list(range(B)))
    out = np.stack([np.ascontiguousarray(r["yT"].T) for r in res.results], axis=0)
    return out.astype(np.float32)
```
